# Optimizing a Trainium2 kernel written in Bass

```python
import math
import jax, jax.numpy as jnp
from jax import lax
import numpy as np

D_MODEL = 1024
BATCH = 2
SEQ = 8192
DEPTH = 1
DEC_BATCH = 128
DEC_SEQ = 4
PAST_LEN = 16384
PAGE_SIZE = 128

HEAD_DIM = 64
N_HEADS = D_MODEL // 128
N_KV_HEADS = N_HEADS // 4
GQA_GROUP = N_HEADS // N_KV_HEADS
WINDOW = 128
ATTN_WIDTH = N_HEADS * HEAD_DIM
KV_WIDTH = N_KV_HEADS * HEAD_DIM
ROPE_THETA = 10000.0
SSM_WIDTH = D_MODEL // 2
SSM_GROUP_CH = 16
SSM_GROUPS = SSM_WIDTH // SSM_GROUP_CH
SSM_STATE = 64
D_FF = ((8 * D_MODEL // 3 + 127) // 128) * 128
IN_COLS = ATTN_WIDTH + 2 * KV_WIDTH + SSM_WIDTH + 2 * D_MODEL
SPLIT_POINTS = [ATTN_WIDTH, ATTN_WIDTH + KV_WIDTH, ATTN_WIDTH + 2 * KV_WIDTH,
                ATTN_WIDTH + 2 * KV_WIDTH + SSM_WIDTH,
                ATTN_WIDTH + 2 * KV_WIDTH + SSM_WIDTH + D_MODEL]
RMS_EPS = 1e-6
MASK_VALUE = -1e30

kernel_name = "macaron_griffin_swa_s5_decode_step"


def rms_norm(x, g):
    xf = x.astype(jnp.float32)
    y = xf * lax.rsqrt(jnp.mean(xf * xf, axis=-1, keepdims=True) + RMS_EPS) * g.astype(jnp.float32)
    return y.astype(x.dtype)


def swiglu(x, w1, w3, w2):
    return (jax.nn.silu(x @ w1) * (x @ w3)) @ w2


def rope(x, pos):
    half = HEAD_DIM // 2
    inv = ROPE_THETA ** (-2.0 * jnp.arange(half, dtype=jnp.float32) / HEAD_DIM)
    ang = pos[:, None] * inv[None, :]
    cos = jnp.cos(ang)[:, None, :]
    sin = jnp.sin(ang)[:, None, :]
    xf = x.astype(jnp.float32)
    x1, x2 = xf[..., :half], xf[..., half:]
    return jnp.concatenate([x1 * cos - x2 * sin, x2 * cos + x1 * sin], axis=-1).astype(x.dtype)


def sink_attend(scores, mask, sinks, v, av_eq):
    sink = sinks.astype(jnp.float32).reshape(N_KV_HEADS, GQA_GROUP, 1, 1)
    s = jnp.where(mask, scores, MASK_VALUE)
    m = jnp.maximum(jnp.max(s, axis=-1, keepdims=True), sink)
    p = jnp.exp(s - m)
    denom = jnp.sum(p, axis=-1, keepdims=True) + jnp.exp(sink - m)
    probs = (p / denom).astype(v.dtype)
    return jnp.einsum(av_eq, probs, v)


def swa_prompt(q, k, v, sinks):
    B, L = q.shape[:2]
    nb = L // WINDOW
    qb = q.reshape(B, nb, WINDOW, N_KV_HEADS, GQA_GROUP, HEAD_DIM)
    pad = ((0, 0), (WINDOW, 0), (0, 0), (0, 0))
    kp = jnp.pad(k, pad).reshape(B, nb + 1, WINDOW, N_KV_HEADS, HEAD_DIM)
    vp = jnp.pad(v, pad).reshape(B, nb + 1, WINDOW, N_KV_HEADS, HEAD_DIM)
    kb = jnp.concatenate([kp[:, :-1], kp[:, 1:]], axis=2)
    vb = jnp.concatenate([vp[:, :-1], vp[:, 1:]], axis=2)
    scores = jnp.einsum('bnqkgd,bnskd->bnkgqs', qb, kb).astype(jnp.float32)
    qi = jnp.arange(WINDOW)[:, None]
    sj = jnp.arange(2 * WINDOW)[None, :]
    diff = WINDOW + qi - sj
    kpos = (jnp.arange(nb)[:, None, None] - 1) * WINDOW + sj[None]
    mask = (diff >= 0)[None] & (diff < WINDOW)[None] & (kpos >= 0)
    mask = mask[None, :, None, None, :, :]
    out = sink_attend(scores, mask, sinks, vb, 'bnkgqs,bnskd->bnqkgd')
    return out.reshape(B, L, ATTN_WIDTH)


def swa_sample(q, k_new, v_new, k_buf, v_buf, sinks):
    B, T = q.shape[:2]
    kk = jnp.concatenate([k_buf.astype(k_new.dtype), k_new], axis=1)
    vv = jnp.concatenate([v_buf.astype(v_new.dtype), v_new], axis=1)
    qg = q.reshape(B, T, N_KV_HEADS, GQA_GROUP, HEAD_DIM)
    scores = jnp.einsum('btkgd,bskd->bkgts', qg, kk).astype(jnp.float32)
    diff = WINDOW + jnp.arange(T)[:, None] - jnp.arange(WINDOW + T)[None, :]
    mask = (diff >= 0) & (diff < WINDOW)
    out = sink_attend(scores, mask, sinks, vv, 'bkgts,bskd->btkgd')
    return out.reshape(B, T, ATTN_WIDTH), kk[:, T:], vv[:, T:]


def _ssm_combine(e1, e2):
    a1r, a1i, b1r, b1i = e1
    a2r, a2i, b2r, b2i = e2
    return (a2r * a1r - a2i * a1i,
            a2r * a1i + a2i * a1r,
            a2r * b1r - a2i * b1i + b2r,
            a2r * b1i + a2i * b1r + b2i)


def s5_scan(u, x0_re, x0_im, a_re, a_im, log_dt, b_re, b_im, c_re, c_im, d_skip):
    f32 = jnp.float32
    u = u.astype(f32)
    a_re, a_im = a_re.astype(f32), a_im.astype(f32)
    dt = jnp.exp(log_dt.astype(f32))[:, None]
    mag = jnp.exp(dt * a_re)
    ab_re, ab_im = mag * jnp.cos(dt * a_im), mag * jnp.sin(dt * a_im)
    den = a_re * a_re + a_im * a_im
    nr, ni = ab_re - 1.0, ab_im
    f_re = (nr * a_re + ni * a_im) / den
    f_im = (ni * a_re - nr * a_im) / den
    b_re, b_im = b_re.astype(f32), b_im.astype(f32)
    bb_re = f_re[..., None] * b_re - f_im[..., None] * b_im
    bb_im = f_re[..., None] * b_im + f_im[..., None] * b_re
    bu_re = jnp.einsum('btgc,gnc->btgn', u, bb_re)
    bu_im = jnp.einsum('btgc,gnc->btgn', u, bb_im)
    ar = jnp.broadcast_to(ab_re, bu_re.shape)
    ai = jnp.broadcast_to(ab_im, bu_re.shape)
    ac_re, ac_im, xs_re, xs_im = lax.associative_scan(_ssm_combine, (ar, ai, bu_re, bu_im), axis=1)
    if x0_re is not None:
        x0r = x0_re.astype(f32)[:, None]
        x0i = x0_im.astype(f32)[:, None]
        xs_re = xs_re + ac_re * x0r - ac_im * x0i
        xs_im = xs_im + ac_re * x0i + ac_im * x0r
    y = (jnp.einsum('btgn,gcn->btgc', xs_re, c_re.astype(f32))
         - jnp.einsum('btgn,gcn->btgc', xs_im, c_im.astype(f32))
         + d_skip.astype(f32) * u)
    return y, xs_re[:, -1], xs_im[:, -1]


def trunk_layer(x, pos0, k_buf, v_buf, s_re, s_im, p):
    B, T, _ = x.shape
    x = x + 0.5 * swiglu(rms_norm(x, p['ffn1_norm']), p['ffn1_w1'], p['ffn1_w3'], p['ffn1_w2'])
    h = rms_norm(x, p['mix_norm'])
    proj = h @ p['w_in']
    q, k, v, u, ga, gs = jnp.split(proj, SPLIT_POINTS, axis=-1)
    q = q.reshape(B, T, N_HEADS, HEAD_DIM)
    k = k.reshape(B, T, N_KV_HEADS, HEAD_DIM)
    v = v.reshape(B, T, N_KV_HEADS, HEAD_DIM)
    pos = (pos0 + jnp.arange(T)).astype(jnp.float32)
    q = rope(rms_norm(q, p['q_norm']), pos) * (HEAD_DIM ** -0.5)
    k = rope(rms_norm(k, p['k_norm']), pos)
    if k_buf is None:
        attn = swa_prompt(q, k, v, p['attn_sinks'])
        new_k, new_v = k[:, T - WINDOW:], v[:, T - WINDOW:]
    else:
        attn, new_k, new_v = swa_sample(q, k, v, k_buf, v_buf, p['attn_sinks'])
    y_ssm, new_re, new_im = s5_scan(u.reshape(B, T, SSM_GROUPS, SSM_GROUP_CH), s_re, s_im,
                                    p['ssm_a_re'], p['ssm_a_im'], p['ssm_log_dt'],
                                    p['ssm_b_re'], p['ssm_b_im'], p['ssm_c_re'], p['ssm_c_im'],
                                    p['ssm_d'])
    z = jax.nn.gelu(y_ssm.reshape(B, T, SSM_WIDTH).astype(x.dtype))
    ssm = z * jax.nn.sigmoid(z @ p['w_glu'] + p['b_glu'])
    merged = (jax.nn.sigmoid(ga) * (attn @ p['w_attn_out'])
              + jax.nn.sigmoid(gs) * (ssm @ p['w_ssm_out']))
    x = x + merged @ p['w_out']
    x = x + 0.5 * swiglu(rms_norm(x, p['ffn2_norm']), p['ffn2_w1'], p['ffn2_w3'], p['ffn2_w2'])
    return x, new_k, new_v, new_re, new_im


def setup_inputs(seed: int = 0) -> dict:
    key = jax.random.key(seed)
    ks = iter(jax.random.split(key, 48))
    f32 = jnp.float32

    def nrm(shape, scale):
        return jax.random.normal(next(ks), shape, f32) * scale

    L, D, G, N, CH = DEPTH, D_MODEL, SSM_GROUPS, SSM_STATE, SSM_GROUP_CH
    return {
        'x_prompt': nrm((BATCH, SEQ, D), 1.0),
        'x_sample': nrm((DEC_BATCH, DEC_SEQ, D), 1.0),
        'cache_k': nrm((L, DEC_BATCH, WINDOW, N_KV_HEADS, HEAD_DIM), 1.0),
        'cache_v': nrm((L, DEC_BATCH, WINDOW, N_KV_HEADS, HEAD_DIM), 1.0),
        'state_ssm_re': nrm((L, DEC_BATCH, G, N), 0.5),
        'state_ssm_im': nrm((L, DEC_BATCH, G, N), 0.5),
        'ffn1_norm': 1.0 + nrm((L, D), 0.02),
        'ffn1_w1': nrm((L, D, D_FF), D ** -0.5),
        'ffn1_w3': nrm((L, D, D_FF), D ** -0.5),
        'ffn1_w2': nrm((L, D_FF, D), D_FF ** -0.5),
        'mix_norm': 1.0 + nrm((L, D), 0.02),
        'w_in': nrm((L, D, IN_COLS), D ** -0.5),
        'q_norm': 1.0 + nrm((L, HEAD_DIM), 0.02),
        'k_norm': 1.0 + nrm((L, HEAD_DIM), 0.02),
        'attn_sinks': nrm((L, N_HEADS), 0.5),
        'w_attn_out': nrm((L, ATTN_WIDTH, D), ATTN_WIDTH ** -0.5),
        'ssm_a_re': -0.5 + nrm((L, G, N), 0.01),
        'ssm_a_im': math.pi * jnp.arange(N, dtype=f32) + nrm((L, G, N), 0.01),
        'ssm_log_dt': jax.random.uniform(next(ks), (L, G), f32, math.log(1e-3), math.log(1e-1)),
        'ssm_b_re': nrm((L, G, N, CH), (2 * CH) ** -0.5),
        'ssm_b_im': nrm((L, G, N, CH), (2 * CH) ** -0.5),
        'ssm_c_re': nrm((L, G, CH, N), (2 * N) ** -0.5),
        'ssm_c_im': nrm((L, G, CH, N), (2 * N) ** -0.5),
        'ssm_d': 1.0 + nrm((L, G, CH), 0.1),
        'w_glu': nrm((L, SSM_WIDTH, SSM_WIDTH), SSM_WIDTH ** -0.5),
        'b_glu': nrm((L, SSM_WIDTH), 0.02),
        'w_ssm_out': nrm((L, SSM_WIDTH, D), SSM_WIDTH ** -0.5),
        'w_out': nrm((L, D, D), D ** -0.5),
        'ffn2_norm': 1.0 + nrm((L, D), 0.02),
        'ffn2_w1': nrm((L, D, D_FF), D ** -0.5),
        'ffn2_w3': nrm((L, D, D_FF), D ** -0.5),
        'ffn2_w2': nrm((L, D_FF, D), D_FF ** -0.5),
    }


def reference(x_prompt, x_sample, cache_k, cache_v, state_ssm_re, state_ssm_im,
              ffn1_norm, ffn1_w1, ffn1_w3, ffn1_w2, mix_norm, w_in, q_norm, k_norm,
              attn_sinks, w_attn_out, ssm_a_re, ssm_a_im, ssm_log_dt, ssm_b_re, ssm_b_im,
              ssm_c_re, ssm_c_im, ssm_d, w_glu, b_glu, w_ssm_out, w_out,
              ffn2_norm, ffn2_w1, ffn2_w3, ffn2_w2):
    xp, xs = x_prompt, x_sample
    kp_l, vp_l, rp_l, ip_l = [], [], [], []
    ks_l, vs_l, rs_l, is_l = [], [], [], []
    for l in range(DEPTH):
        p = dict(ffn1_norm=ffn1_norm[l], ffn1_w1=ffn1_w1[l], ffn1_w3=ffn1_w3[l], ffn1_w2=ffn1_w2[l],
                 mix_norm=mix_norm[l], w_in=w_in[l], q_norm=q_norm[l], k_norm=k_norm[l],
                 attn_sinks=attn_sinks[l], w_attn_out=w_attn_out[l],
                 ssm_a_re=ssm_a_re[l], ssm_a_im=ssm_a_im[l], ssm_log_dt=ssm_log_dt[l],
                 ssm_b_re=ssm_b_re[l], ssm_b_im=ssm_b_im[l], ssm_c_re=ssm_c_re[l], ssm_c_im=ssm_c_im[l],
                 ssm_d=ssm_d[l], w_glu=w_glu[l], b_glu=b_glu[l], w_ssm_out=w_ssm_out[l], w_out=w_out[l],
                 ffn2_norm=ffn2_norm[l], ffn2_w1=ffn2_w1[l], ffn2_w3=ffn2_w3[l], ffn2_w2=ffn2_w2[l])
        xp, kp, vp, rp, ip = trunk_layer(xp, 0, None, None, None, None, p)
        xs, ks_, vs_, rs_, is_ = trunk_layer(xs, PAST_LEN, cache_k[l], cache_v[l],
                                             state_ssm_re[l], state_ssm_im[l], p)
        kp_l.append(kp); vp_l.append(vp); rp_l.append(rp); ip_l.append(ip)
        ks_l.append(ks_); vs_l.append(vs_); rs_l.append(rs_); is_l.append(is_)
    new_k_prompt, new_v_prompt = jnp.stack(kp_l), jnp.stack(vp_l)
    new_re_prompt, new_im_prompt = jnp.stack(rp_l), jnp.stack(ip_l)
    new_k_sample, new_v_sample = jnp.stack(ks_l), jnp.stack(vs_l)
    new_re_sample, new_im_sample = jnp.stack(rs_l), jnp.stack(is_l)
    return (xp, xs, new_k_prompt, new_v_prompt, new_re_prompt, new_im_prompt,
            new_k_sample, new_v_sample, new_re_sample, new_im_sample)
```

```python
import numpy as np
from contextlib import ExitStack
import concourse.bass as bass
import concourse.mybir as mybir
from concourse.bass_utils import run_bass_kernel_spmd

F32 = mybir.dt.float32
BF16 = mybir.dt.bfloat16
I32 = mybir.dt.int32
AF = mybir.ActivationFunctionType
ALU = mybir.AluOpType
AX = mybir.AxisListType

D = 1024
DFF = 2816
NF = DFF // 128
NT = 18
TOK = NT * 128
NPT = 16
SEG = 2048
HD = 64
PAST_LEN = 16384
EPS = 1e-6
NG = 32
NS = 64
CH = 16
LC = 16
NJ = SEG // LC
NSEQ = 16
USE_CC = False


class Buf:
    __slots__ = ("name", "w", "r", "ep")

    def __init__(self, name=""):
        self.name = name
        self.w = None
        self.r = []
        self.ep = 0


class Op:
    __slots__ = ("eng", "fn", "waits", "sig", "tok", "is_dma", "dsem", "dval", "idx", "stage", "inc")


class Prog:
    ENGS = ["pe", "act", "dve", "pool", "sp"]
    NDMASEM = 8
    ROLL = 20000

    def __init__(self, nc):
        self.nc = nc
        self.ops = {e: [] for e in self.ENGS}
        self.epoch = 0
        self.dma_expect = {}
        self.dma_rr = {e: 0 for e in self.ENGS}
        self.stage = ""
        self.name2stage = {}
        self.capture = None
        self.captured = []

    def _need(self, op, tok):
        if tok is None:
            return
        if tok[0] == "e":
            if op.eng == "pe" and tok[1] == "pe":
                return
            self.ops[tok[1]][tok[2]].sig = True
        op.waits.append(tok)

    def _touch(self, b):
        if b.ep != self.epoch:
            b.w, b.r, b.ep = None, [], self.epoch

    def _deps(self, op, reads, writes):
        for b in reads:
            self._touch(b)
            self._need(op, b.w)
        for b in writes:
            self._touch(b)
            self._need(op, b.w)
            for t in b.r:
                self._need(op, t)

    def _commit(self, tok, reads, writes):
        for b in reads:
            b.r.append(tok)
        for b in writes:
            b.w = tok
            b.r = []

    def replay(self, n):
        cap, self.capture = self.capture, None
        st = self.stage
        while n > 0 and self.captured:
            kind, stage, args, kw = self.captured.pop(0)
            self.stage = stage
            if kind == "op":
                self.op(*args, **kw)
            elif kind == "dma":
                self.dma(*args, **kw)
            else:
                self.barrier()
            n -= 1
        self.stage = st
        self.capture = cap

    def op(self, eng, fn, reads=(), writes=()):
        if self.capture:
            self.captured.append(("op", self.stage, (eng, fn), dict(reads=list(reads), writes=list(writes))))
            return None
        o = Op()
        o.eng, o.fn, o.waits, o.sig, o.is_dma = eng, fn, [], False, False
        o.idx = len(self.ops[eng])
        o.stage = self.stage
        o.tok = ("e", eng, o.idx)
        self._deps(o, reads, writes)
        self.ops[eng].append(o)
        self._commit(o.tok, reads, writes)
        return o

    def dma(self, q, fn, reads=(), writes=(), inc=16, semkey=None):
        if self.capture:
            self.captured.append(("dma", self.stage, (q, fn), dict(reads=list(reads), writes=list(writes), inc=inc,
                                                                    semkey=semkey)))
            return None
        o = Op()
        o.eng, o.fn, o.waits, o.sig, o.is_dma = q, fn, [], False, True
        o.idx = len(self.ops[q])
        o.stage = self.stage
        o.inc = inc
        if semkey is None:
            k = self.dma_rr[q] % self.NDMASEM
            self.dma_rr[q] += 1
        else:
            q, k = semkey
        prev = self.dma_expect.get((q, k), 0)
        if prev:
            o.waits.append(("d", q, k, prev))
        val = prev + inc
        self.dma_expect[(q, k)] = val
        o.dsem, o.dval = (q, k), val
        o.tok = ("d", q, k, val)
        self._deps(o, reads, writes)
        self.ops[o.eng].append(o)
        self._commit(o.tok, reads, writes)
        return o

    def barrier(self):
        if self.capture:
            self.captured.append(("bar", self.stage, (), {}))
            return
        toks = []
        for e in self.ENGS:
            for o in reversed(self.ops[e]):
                if not o.is_dma:
                    o.sig = True
                    toks.append(o.tok)
                    break
        for (q, k), v in self.dma_expect.items():
            toks.append(("d", q, k, v))
        for e in self.ENGS:
            o = Op()
            o.eng, o.fn, o.waits, o.sig, o.is_dma = e, None, list(toks), False, False
            o.idx = len(self.ops[e])
            o.tok = ("e", e, o.idx)
            self.ops[e].append(o)
        self.epoch += 1

    def emit(self, st):
        nc = self.nc
        sigval = {}
        nsig = {}
        for e in self.ENGS:
            n = 0
            for o in self.ops[e]:
                if o.sig and not o.is_dma and o.fn is not None:
                    n += 1
                sigval[(e, o.idx)] = n
            nsig[e] = n
        esem = {}
        for e in self.ENGS:
            ngen = nsig[e] // self.ROLL + 1
            esem[e] = [st.enter_context(nc.semaphore(f"s_{e}_{g}")) for g in range(ngen)]
        dsem = {}
        for (q, k) in self.dma_expect:
            dsem[(q, k)] = st.enter_context(nc.semaphore(f"d_{q}_{k}"))
        emap = {"pe": "tensor", "act": "scalar", "dve": "vector", "pool": "gpsimd", "sp": "sync"}
        block = st.enter_context(nc.Block())
        ROLL = self.ROLL

        def resolve(tok):
            if tok[0] == "e":
                v = sigval[(tok[1], tok[2])]
                g = (v - 1) // ROLL if v > 0 else 0
                return (("e", tok[1], g), esem[tok[1]][g], v - g * ROLL)
            return (("d", tok[1], tok[2]), dsem[(tok[1], tok[2])], tok[3])

        def run(e, eng):
            known = {}
            for o in self.ops[e]:
                need = {}
                for t in o.waits:
                    key, sem, v = resolve(t)
                    if v <= 0 or known.get(key, 0) >= v:
                        continue
                    if need.get(key, (None, 0))[1] < v:
                        need[key] = (sem, v)
                for key, (sem, v) in need.items():
                    eng.wait_ge(sem, v)
                    known[key] = v
                if o.fn is None:
                    continue
                ins = o.fn(eng)
                try:
                    self.name2stage[ins.ins.name] = o.stage
                except Exception:
                    pass
                if o.is_dma:
                    ins.then_inc(dsem[o.dsem], o.inc)
                elif o.sig:
                    v = sigval[(e, o.idx)]
                    g = (v - 1) // ROLL
                    ins.then_inc(esem[e][g], 1)

        for e in self.ENGS:
            getattr(block, emap[e])(lambda eng, e=e: run(e, eng))


class K:
    def mm(self, out, lhsT, rhs, start, stop, reads, writes):
        return self.P.op("pe", lambda e: e.matmul(out, lhsT=lhsT, rhs=rhs, start=start, stop=stop),
                         reads=reads, writes=writes)

    def tr(self, out, in_, ident, reads, writes):
        return self.P.op("pe", lambda e: e.transpose(out, in_, ident), reads=reads, writes=writes)

    def act(self, out, in_, func, reads, writes, **kw):
        return self.P.op("act", lambda e: e.activation(out=out, in_=in_, func=func, **kw),
                         reads=reads, writes=writes)

    def tt(self, eng, out, in0, in1, op, reads, writes):
        return self.P.op(eng, lambda e: e.tensor_tensor(out=out, in0=in0, in1=in1, op=op),
                         reads=reads, writes=writes)

    def stt(self, out, in0, scalar, in1, op0, op1, reads, writes):
        return self.P.op("dve", lambda e: e.scalar_tensor_tensor(out=out, in0=in0, scalar=scalar, in1=in1,
                                                                   op0=op0, op1=op1),
                         reads=reads, writes=writes)

    def ts(self, eng, out, in0, s1, s2, op0, op1, reads, writes):
        if s2 is None:
            return self.P.op(eng, lambda e: e.tensor_scalar(out=out, in0=in0, scalar1=s1, scalar2=None, op0=op0),
                             reads=reads, writes=writes)
        return self.P.op(eng, lambda e: e.tensor_scalar(out=out, in0=in0, scalar1=s1, scalar2=s2, op0=op0,
                                                        op1=op1), reads=reads, writes=writes)

    def cp(self, eng, out, in_, reads, writes):
        if eng == "act":
            return self.P.op("act", lambda e: e.copy(out=out, in_=in_), reads=reads, writes=writes)
        return self.P.op(eng, lambda e: e.tensor_copy(out=out, in_=in_), reads=reads, writes=writes)

    def recip(self, out, in_, reads, writes):
        return self.P.op("dve", lambda e: e.reciprocal(out=out, in_=in_), reads=reads, writes=writes)

    def memset(self, eng, ap, val, writes):
        return self.P.op(eng, lambda e: e.memset(ap, val), writes=writes)

    def dma(self, q, out, in_, reads=(), writes=()):
        return self.P.dma(q, lambda e: e.dma_start(out=out, in_=in_), reads=reads, writes=writes)


def build_program(debug=(), stop=None, skip_ffn=False):
    nc = bass.Bass("TRN2", target_bir_lowering=False)
    P = Prog(nc)
    _CACHE["P"] = P
    k = K()
    k.nc, k.P = nc, P
    k.debug = set(debug)
    k.stop = stop
    din = {}
    dout = {}

    def inp(name, shape, dt=F32):
        din[name] = nc.dram_tensor(name, list(shape), dt, kind="ExternalInput").ap()
        return din[name]

    def outp(name, shape, dt=F32):
        dout[name] = nc.dram_tensor(name, list(shape), dt, kind="ExternalOutput").ap()
        return dout[name]

    k.inp, k.outp, k.din, k.dout = inp, outp, din, dout

    inp("x", [TOK, D])
    inp("ffn1_g", [128, 8]); inp("mix_g", [128, 8]); inp("ffn2_g", [128, 8])
    inp("ffn1_w1", [D, DFF]); inp("ffn1_w3", [D, DFF]); inp("ffn1_w2", [DFF, D])
    inp("ffn2_w1", [D, DFF]); inp("ffn2_w3", [D, DFF]); inp("ffn2_w2", [DFF, D])
    inp("identb", [128, 128])
    inp("w_qk", [D, 640]); inp("w_in", [D, 3328])
    inp("gq2", [128, 1]); inp("gk2", [128, 1]); inp("invf", [128, 1]); inp("pos", [1, TOK])
    inp("rotm", [128, 128]); inp("onesbd", [128, 128])
    inp("mdiag", [128, 128]); inp("mprev", [128, 128]); inp("mprev1", [128, 128]); inp("smask", [64, 64])
    inp("sinks", [1, 8])
    inp("cache_k", [NSEQ, 128, 128]); inp("cache_v", [NSEQ, 128, 128])
    if USE_CC:
        inp("selb", [64, 8]); inp("wsel", [64, 24])
    else:
        inp("xpre", [3, SEG, D])
    inp("identf", [128, 128])
    inp("a_reT", [64, NG]); inp("a_imT", [64, NG]); inp("logdt_b", [64, NG])
    inp("bT_re", [64, NG, CH]); inp("bT_im", [64, NG, CH]); inp("cT_re", [64, NG, CH]); inp("cT_im", [64, NG, CH])
    inp("dcol", [128, NG]); inp("tmask", [128, 2, 256]); inp("dmask", [128, 2, 256])
    inp("x0", [64, 2, NG, NSEQ])
    inp("w_glu", [512, 512]); inp("b_glu", [128, 4]); inp("w_ao", [512, D]); inp("w_ssm_out", [512, D])
    inp("w_out", [D, D]); inp("wm5", [8, 128, 3072])
    outp("y", [17 * 128, D])
    outp("kp", [128, 128]); outp("vp", [128, 128]); outp("rp", [NG, NS]); outp("ip", [NG, NS])
    outp("ks", [NSEQ, 128, 128]); outp("vs", [NSEQ, 128, 128])
    outp("rs", [NSEQ, NG, NS]); outp("is", [NSEQ, NG, NS])

    with ExitStack() as st:
        k.st = st

        uid = [0]

        def sb(name, shape, dt, scope=None):
            uid[0] += 1
            return (scope or st).enter_context(nc.sbuf_tensor(f"{name}_{uid[0]}", list(shape), dt))

        def ps(name, shape, dt, scope=None):
            uid[0] += 1
            return (scope or st).enter_context(nc.psum_tensor(f"{name}_{uid[0]}", list(shape), dt))

        k.sb, k.ps = sb, ps

        k.X = sb("X", [128, NT, D], F32)
        k.BX = [[Buf(f"X{t}_{h}") for h in range(2)] for t in range(NT)]
        k.identb = sb("identb_s", [128, 128], BF16)
        k.Bident = Buf("ident")
        k.dma("pool", k.identb[:], din["identb"], writes=[k.Bident])
        k.gvec = {}
        for nm in ("ffn1_g", "mix_g", "ffn2_g"):
            t = sb(nm + "_s", [128, 8], F32)
            b = Buf(nm)
            k.dma("sp", t[:], din[nm], writes=[b])
            k.gvec[nm] = (t, b)

        k.identf = sb("identf_s", [128, 128], F32)
        k.Bidentf = Buf("identf")
        k.dma("sp", k.identf[:], din["identf"], writes=[k.Bidentf])
        k.YC = sb("YC", [64, 2, NG], F32)
        k.BYC = Buf("YC")
        k.memset("dve", k.YC[:], 0.0, [k.BYC])
        P.stage = "setup"
        ssm_setup_alloc(k)
        setup_scope = ExitStack()
        P.capture = True
        ssm_setup(k, setup_scope)
        P.capture = None
        n_cap = len(P.captured)
        import os as _os
        nsb = 1 if (USE_CC or stop is not None) else 4
        for sbk in range(4 - nsb, 4):
            own = sbk == 3
            if own:
                xin = din["x"].rearrange("(t p) d -> p t d", p=128)
                tiles = list(range(NT))
                for t in tiles:
                    k.dma("sp" if t % 2 == 0 else "act", k.X[:, t, :], xin[:, t, :], writes=k.BX[t])
            else:
                xin = din["xpre"][sbk].rearrange("(t p) d -> p t d", p=128)
                tiles = list(range(1, 17))
                for t in tiles:
                    k.dma("sp" if t % 2 == 0 else "act", k.X[:, t, :], xin[:, t - 1, :], writes=k.BX[t])
            P.stage = f"sb{sbk}.ffn1"
            if not skip_ffn:
                if P.captured:
                    nblk = ((NF + 1) // 2) * ((len(tiles) + 1) // 2)
                    ffn(k, "ffn1", tiles, G=2, replay=n_cap // (nblk - 12) + 1)
                else:
                    ffn(k, "ffn1", tiles)
            else:
                P.replay(1 << 30)
            if setup_scope is not None:
                setup_scope.close()
                setup_scope = None
            if not own:
                P.stage = f"sb{sbk}.pmix"
                mixer_prefix(k)
                continue
            if "x1" in k.debug:
                o = outp("dbg_x1", [TOK, D])
                ov = o.rearrange("(t p) d -> p t d", p=128)
                for t in range(NT):
                    k.dma("sp", ov[:, t, :], k.X[:, t, :], reads=k.BX[t])
            mixer(k)
            P.stage = "ffn2"
            if stop is None and not skip_ffn:
                ffn(k, "ffn2", list(range(1, NT)))
            P.stage = "store"
        yv = dout["y"].rearrange("(t p) d -> p t d", p=128)
        for t in range(1, NT):
            q = "sp" if t % 2 == 0 else "act"
            k.dma(q, yv[:, t - 1, :], k.X[:, t, :], reads=k.BX[t])
        P.barrier()
        P.emit(st)
    return nc


def norm_transpose(k, gname, tiles, xnT, BxnT, scope, tp, Btp, t0=0):
    P = k.P
    g, Bg = k.gvec[gname]
    ss = k.sb("nt_ss_" + gname, [128, NT], F32, scope)
    rs = k.sb("nt_rs_" + gname, [128, NT], F32, scope)
    junk = [k.sb(f"nt_junk{i}_" + gname, [128, D], BF16, scope) for i in range(2)]
    xs = [k.sb(f"nt_xs{i}_" + gname, [128, D], BF16, scope) for i in range(2)]
    Bjunk = [Buf() for _ in range(2)]
    Bxs = [Buf() for _ in range(2)]
    Bss = [Buf() for _ in range(NT)]
    for n, t in enumerate(tiles):
        s = n % 2
        k.act(junk[s][:], k.X[:, t, :], AF.Square, k.BX[t], [Bjunk[s], Bss[t]], accum_out=ss[:, t:t + 1])
        k.act(rs[:, t:t + 1], ss[:, t:t + 1], AF.Sqrt, [Bss[t]], [Bss[t]], scale=1.0 / D, bias=EPS)
        k.recip(rs[:, t:t + 1], rs[:, t:t + 1], [Bss[t]], [Bss[t]])
        k.act(xs[s][:], k.X[:, t, :], AF.Copy, k.BX[t] + [Bss[t]], [Bxs[s]], scale=rs[:, t:t + 1])
        for kk in range(8):
            k.tr(tp[:, kk, :], xs[s][:, kk * 128:(kk + 1) * 128], k.identb[:], [Bxs[s], k.Bident], [Btp])
        k.tt("dve", xnT[:, :, (t - t0) * 128:(t - t0 + 1) * 128], tp[:],
             g[:, :].unsqueeze(2).to_broadcast([128, 8, 128]), ALU.mult, [Btp, Bg], [BxnT[t]])


def ffn(k, name, tiles, G=4, replay=0):
    nc, P = k.nc, k.P
    w1 = k.din[name + "_w1"].rearrange("(kk p) f -> p kk f", p=128)
    w3 = k.din[name + "_w3"].rearrange("(kk p) f -> p kk f", p=128)
    w2 = k.din[name + "_w2"].rearrange("(f p) d -> p f d", p=128)
    with ExitStack() as sc:
        xnT = k.sb(name + "_xnT", [128, 8, TOK], BF16, sc)
        BxnT = [Buf() for _ in range(NT)]
        with ExitStack() as stp:
            tp = k.ps(name + "_tp", [128, 8, 128], BF16, stp)
            Btp = Buf()
            norm_transpose(k, name + "_g", tiles, xnT, BxnT, sc, tp, Btp)
            P.barrier()
        W1 = k.sb(name + "_W1", [128, 2, 8, G * 128], BF16, sc)
        W3 = k.sb(name + "_W3", [128, 2, 8, G * 128], BF16, sc)
        W2 = k.sb(name + "_W2", [128, 2, G, D], BF16, sc)
        BW = [Buf() for _ in range(2)]
        AB = [k.ps(name + f"_AB{i}", [128, 2, 256], F32, sc) for i in range(2)]
        BAB = [Buf() for _ in range(2)]
        OUT = [[k.ps(name + f"_O{t}{h}", [128, 512], F32, sc) for h in range(2)] for t in range(2)]
        BOUT = [[Buf() for h in range(2)] for t in range(2)]
        SA = [k.sb(name + f"_sa{i}", [128, 256], F32, sc) for i in range(2)]
        BSA = [Buf() for _ in range(2)]
        H = k.sb(name + "_H", [128, 2, G, 256], BF16, sc)
        BH = [[Buf() for _ in range(G)] for _ in range(2)]
        groups = []
        f0 = 0
        while f0 < NF:
            groups.append((f0, min(G, NF - f0)))
            f0 += G
        blocks = [tiles[i:i + 2] for i in range(0, len(tiles), 2)]
        abn = 0
        hsn = 0
        for gi, (f0, gn) in enumerate(groups):
            slot = gi % 2
            c0, c1 = f0 * 128, (f0 + gn) * 128
            k.dma("pool", W1[:, slot, :, 0:c1 - c0], w1[:, :, c0:c1], writes=[BW[slot]])
            k.dma("pool", W3[:, slot, :, 0:c1 - c0], w3[:, :, c0:c1], writes=[BW[slot]])
            k.dma("pool", W2[:, slot, 0:gn, :], w2[:, f0:f0 + gn, :], writes=[BW[slot]])
            for blk in blocks:
                nb = 128 * len(blk)
                col0 = blk[0] * 128
                assert all(blk[i] == blk[0] + i for i in range(len(blk)))
                hs = hsn % 2
                hsn += 1
                rd_x = [BxnT[t] for t in blk]

                def emit_ab(fi):
                    nonlocal abn
                    a = abn % 2
                    abn += 1
                    for wi, W in enumerate((W1, W3)):
                        for kk in range(8):
                            k.mm(AB[a][:, wi, 0:nb], W[:, slot, kk, fi * 128:(fi + 1) * 128],
                                 xnT[:, kk, col0:col0 + nb], kk == 0, kk == 7,
                                 [BW[slot]] + rd_x, [BAB[a]])
                    k.act(SA[a][:, 0:nb], AB[a][:, 0, 0:nb], AF.Silu, [BAB[a]], [BSA[a]])
                    k.tt("dve", H[:, hs, fi, 0:nb], AB[a][:, 1, 0:nb], SA[a][:, 0:nb], ALU.mult,
                         [BAB[a], BSA[a]], [BH[hs][fi]])

                def emit_w2(fi):
                    for tl in range(len(blk)):
                        for hh in range(2):
                            k.mm(OUT[tl][hh][:], H[:, hs, fi, tl * 128:(tl + 1) * 128],
                                 W2[:, slot, fi, hh * 512:(hh + 1) * 512], fi == 0, fi == gn - 1,
                                 [BW[slot], BH[hs][fi]], [BOUT[tl][hh]])

                emit_ab(0)
                for fi in range(1, gn):
                    emit_ab(fi)
                    emit_w2(fi - 1)
                emit_w2(gn - 1)
                for tl, t in enumerate(blk):
                    for hh in range(2):
                        xs_ = k.X[:, t, hh * 512:(hh + 1) * 512]
                        k.stt(xs_, OUT[tl][hh][:], 0.5, xs_, ALU.mult, ALU.add,
                              [BOUT[tl][hh], k.BX[t][hh]], [k.BX[t][hh]])
                if replay:
                    P.replay(replay)
        if replay:
            P.replay(1 << 30)
        P.barrier()


TWO_PI = float(2 * np.pi)
CW1 = 6.28125
CW2 = float(2 * np.pi - 6.28125)


def sin_table(k, out, ang, shift, tmp, ki, rr, Bout, Bang, Btmp, Bki, Brr):
    k.ts("dve", tmp, ang, shift, 1.0 / TWO_PI, ALU.add, ALU.mult, [Bang], [Btmp])
    k.cp("dve", ki, tmp, [Btmp], [Bki])
    k.cp("dve", tmp, ki, [Bki], [Btmp])
    k.stt(rr, tmp, -CW1, ang, ALU.mult, ALU.add, [Btmp, Bang], [Brr])
    k.stt(rr, tmp, -CW2, rr, ALU.mult, ALU.add, [Btmp, Brr], [Brr])
    k.ts("dve", rr, rr, shift, 3.1415925, ALU.add, ALU.min, [Brr], [Brr])
    k.ts("dve", rr, rr, -3.1415925, None, ALU.max, None, [Brr], [Brr])
    k.act(out, rr, AF.Sin, [Brr], [Bout])


def setup_rope(k, scope):
    P = k.P
    k.cosT = k.sb("cosT", [128, TOK], BF16, scope)
    k.sinT = k.sb("sinT", [128, TOK], BF16, scope)
    k.Brope = Buf()
    with ExitStack() as sc:
        invf = k.sb("invf_s", [128, 1], F32, sc)
        Binv = Buf()
        k.dma("sp", invf[:], k.din["invf"], writes=[Binv])
        CHK = 768
        posb = k.sb("posb", [128, CHK], F32, sc)
        ang = k.sb("ang", [128, CHK], F32, sc)
        tmp = k.sb("rtmp", [128, CHK], F32, sc)
        ki = k.sb("rki", [128, CHK], I32, sc)
        rr = k.sb("rrr", [128, CHK], F32, sc)
        Bpos, Bang, Btmp, Bki, Brr = Buf(), Buf(), Buf(), Buf(), Buf()
        import os
        dbg = int(os.environ.get("ROPE_DBG", "0"))
        for c0 in range(0, TOK, CHK):
            if dbg == 1 and c0 > 0:
                break
            k.dma("sp", posb[:], k.din["pos"][0:1, c0:c0 + CHK].to_broadcast([128, CHK]), writes=[Bpos])
            k.ts("dve", ang[:], posb[:], invf[:, 0:1], None, ALU.mult, None, [Bpos, Binv], [Bang])
            sin_table(k, k.sinT[:, c0:c0 + CHK], ang[:], 0.0, tmp[:], ki[:], rr[:], k.Brope, Bang, Btmp, Bki, Brr)
            if dbg == 2:
                continue
            sin_table(k, k.cosT[:, c0:c0 + CHK], ang[:], float(np.pi / 2), tmp[:], ki[:], rr[:], k.Brope, Bang, Btmp,
                      Bki, Brr)
        P.barrier()


def mixer(k):
    nc, P, din = k.nc, k.P, k.din
    with ExitStack() as mx:
        k.XS = k.sb("XS", [64, 2, NG, NSEQ], F32, mx)
        k.BXS = Buf("XS")
        k.selb = k.sb("selb", [64, 8], F32, mx)
        k.wsel = k.sb("wsel", [64, 24], F32, mx)
        k.gsel = k.sb("gsel", [64, 8, 64], F32, mx)
        k.Gx = k.sb("Gx", [64, 8, 64], F32, mx)
        NQ = 4 * TOK
        AR = k.sb("arena", [128, NQ + 2 * NQ + 9728], BF16, mx)
        k.AR = AR
        QA = AR[:, 0:NQ].rearrange("p (a t) -> p a t", a=4)
        BQA = [[Buf() for _ in range(2)] for _ in range(NT)]
        k.QA, k.BQA = QA, BQA
        k.hnT_ar = AR[:, NQ:3 * NQ].rearrange("p (a t) -> p a t", a=8)
        k.zT = AR[:, NQ:2 * NQ].rearrange("p (a t) -> p a t", a=4)
        k.ssmT = AR[:, 2 * NQ:3 * NQ].rearrange("p (a t) -> p a t", a=4)
        k.UT = AR[:, 3 * NQ:3 * NQ + 8192].rearrange("p (g s j) -> p g s j", g=NG, s=2)
        k.UTs = AR[0:64, 3 * NQ + 8192:3 * NQ + 8704].rearrange("p (g q) -> p g q", q=NSEQ)
        k.mTh = AR[:, 3 * NQ:3 * NQ + 9216].rearrange("p (a t) -> p a t", a=8)
        k.BUT, k.BUTs, k.BzT = Buf(), Buf(), Buf()
        with ExitStack() as sa:
            KT = k.sb("KT", [128, TOK], BF16, sa)
            BKT = [Buf() for _ in range(NT)]
            KTf = k.sb("KTf", [128, 256], F32, sa)
            BKTf = Buf()
            V = k.sb("V", [128, NT, 128], BF16, sa)
            BV = [Buf() for _ in range(NT)]
            Vf = k.sb("Vf", [128, 2, 128], F32, sa)
            BVf = Buf()
            consts = {}
            for nm, shp, dt in (("rotm", [128, 128], BF16), ("onesbd", [128, 128], BF16),
                                ("mdiag", [128, 128], BF16), ("mprev", [128, 128], BF16),
                                ("mprev1", [128, 128], BF16), ("gq2", [128, 1], F32), ("gk2", [128, 1], F32)):
                t = k.sb(nm + "_s", shp, dt, sa)
                bb = Buf()
                k.dma("pool", t[:], din[nm], writes=[bb])
                consts[nm] = (t, bb)
            smask = k.sb("smask_s", [64, 64], BF16, sa)
            Bsm = Buf()
            k.dma("pool", smask[:], din["smask"], writes=[Bsm])
            sinkb = k.sb("sinkb", [128, 8], F32, sa)
            Bsink = Buf()
            k.dma("sp", sinkb[:], din["sinks"][0:1, :].to_broadcast([128, 8]), writes=[Bsink])
            k.act(sinkb[:], sinkb[:], AF.Exp, [Bsink], [Bsink])
            ones = k.sb("ones_s", [128, 128], BF16, sa)
            Bones = Buf()
            k.memset("pool", ones[:], 1.0, [Bones])

            with ExitStack() as shn:
                hnT = k.hnT_ar
                BhnT = [Buf() for _ in range(NT)]
                setup_rope(k, shn)
                W = k.sb("Wmix", [128, 8, 640], BF16, shn)
                BW = Buf()
                P.stage = "m2a"
                with ExitStack() as s1:
                    tp = k.ps("mx_tp", [128, 8, 128], BF16, s1)
                    norm_transpose(k, "mix_g", list(range(NT)), hnT, BhnT, s1, tp, Buf())
                    P.barrier()
                if k.stop == "m2a":
                    return
                P.stage = "m2b"
                k.dma("pool", W[:], din["w_qk"].rearrange("(kk p) c -> p kk c", p=128), writes=[BW])
                with ExitStack() as s2:
                    NB = 512
                    pq = [k.ps(f"pq{i}", [128, NB], F32, s2) for i in range(2)]
                    pss = [k.ps(f"pss{i}", [128, NB], F32, s2) for i in range(2)]
                    prot = [k.ps(f"prot{i}", [128, NB], F32, s2) for i in range(2)]
                    Bpq, Bpss, Bprot = [Buf(), Buf()], [Buf(), Buf()], [Buf(), Buf()]
                    sq = [k.sb(f"sq{i}", [128, NB], BF16, s2) for i in range(2)]
                    rstd = [k.sb(f"rstd{i}", [128, NB], F32, s2) for i in range(2)]
                    xnb = [k.sb(f"xnb{i}", [128, NB], BF16, s2) for i in range(2)]
                    t1 = [k.sb(f"t1_{i}", [128, NB], F32, s2) for i in range(2)]
                    t2_0 = k.sb("t2_0", [128, NB], F32, s2)
                    t2 = [t2_0, t2_0]
                    Bsq, Brstd, Bxnb, Bt1 = ([Buf(), Buf()] for _ in range(4))
                    Bt2_0 = Buf()
                    Bt2 = [Bt2_0, Bt2_0]
                    rotm, Brot = consts["rotm"]
                    onesbd, Bobd = consts["onesbd"]
                    it = 0
                    for ct in range(5):
                        isk = ct == 4
                        g2, Bg2 = consts["gk2" if isk else "gq2"]
                        for c0 in range(0, TOK, NB):
                            nb = min(NB, TOK - c0)
                            tl = list(range(c0 // 128, (c0 + nb) // 128))
                            i = it % 2
                            it += 1
                            for kk in range(8):
                                k.mm(pq[i][:, 0:nb], W[:, kk, ct * 128:(ct + 1) * 128], hnT[:, kk, c0:c0 + nb],
                                     kk == 0, kk == 7, [BW] + [BhnT[t] for t in tl], [Bpq[i]])
                            k.act(sq[i][:, 0:nb], pq[i][:, 0:nb], AF.Square, [Bpq[i]], [Bsq[i]])
                            k.mm(pss[i][:, 0:nb], onesbd[:], sq[i][:, 0:nb], True, True, [Bobd, Bsq[i]], [Bpss[i]])
                            if isk:
                                k.act(rstd[i][:, 0:nb], pss[i][:, 0:nb], AF.Sqrt, [Bpss[i]], [Brstd[i]],
                                      scale=1.0 / HD, bias=EPS)
                            else:
                                k.act(rstd[i][:, 0:nb], pss[i][:, 0:nb], AF.Sqrt, [Bpss[i]], [Brstd[i]],
                                      scale=1.0, bias=HD * EPS)
                            k.recip(rstd[i][:, 0:nb], rstd[i][:, 0:nb], [Brstd[i]], [Brstd[i]])
                            k.stt(xnb[i][:, 0:nb], pq[i][:, 0:nb], g2[:, 0:1], rstd[i][:, 0:nb], ALU.mult, ALU.mult,
                                  [Bpq[i], Bg2, Brstd[i]], [Bxnb[i]])
                            k.mm(prot[i][:, 0:nb], rotm[:], xnb[i][:, 0:nb], True, True, [Brot, Bxnb[i]], [Bprot[i]])
                            k.tt("pool", t1[i][:, 0:nb], xnb[i][:, 0:nb], k.cosT[:, c0:c0 + nb], ALU.mult,
                                 [Bxnb[i], k.Brope], [Bt1[i]])
                            k.tt("dve", t2[i][:, 0:nb], prot[i][:, 0:nb], k.sinT[:, c0:c0 + nb], ALU.mult,
                                 [Bprot[i], k.Brope], [Bt2[i]])
                            if isk:
                                k.tt("pool", KT[:, c0:c0 + nb], t1[i][:, 0:nb], t2[i][:, 0:nb], ALU.add,
                                     [Bt1[i], Bt2[i]], [BKT[t] for t in tl])
                                if c0 + nb == TOK:
                                    k.tt("pool", KTf[:, :], t1[i][:, nb - 256:nb], t2[i][:, nb - 256:nb], ALU.add,
                                         [Bt1[i], Bt2[i]], [BKTf])
                            else:
                                k.tt("pool", QA[:, ct, c0:c0 + nb], t1[i][:, 0:nb], t2[i][:, 0:nb], ALU.add,
                                     [Bt1[i], Bt2[i]], [BQA[t][h] for t in tl for h in range(2)])
                    P.barrier()
                if k.stop == "m2b":
                    return
                P.stage = "m2c"
                k.dma("pool", W[:], din["w_in"].rearrange("(kk p) c -> p kk c", p=128)[:, :, 640:1280], writes=[BW])
                with ExitStack() as s3:
                    pv = [k.ps(f"pv{i}", [128, 512], F32, s3) for i in range(2)]
                    Bpv = [Buf(), Buf()]
                    for t in range(NT):
                        i = t % 2
                        for kk in range(8):
                            k.mm(pv[i][:, 0:128], hnT[:, kk, t * 128:(t + 1) * 128], W[:, kk, 0:128], kk == 0, kk == 7,
                                 [BW, BhnT[t]], [Bpv[i]])
                        k.cp("act", V[:, t, :], pv[i][:, 0:128], [Bpv[i]], [BV[t]])
                        if t >= 16:
                            k.cp("dve", Vf[:, t - 16, :], pv[i][:, 0:128], [Bpv[i], BV[t]], [BVf])
                    ssm_project(k, hnT, BhnT, W, BW, s3, True)
                    P.barrier()
                if "qk" in k.debug:
                    o = k.outp("dbg_qa", [128, 4 * TOK], BF16)
                    k.dma("sp", o, k.AR[:, 0:4 * TOK], reads=[b for r in BQA for b in r])
                    o = k.outp("dbg_kt", [128, TOK], BF16)
                    k.dma("sp", o, KT[:], reads=BKT)
                    o = k.outp("dbg_v", [128, NT * 128], BF16)
                    k.dma("sp", o, V[:].rearrange("p a t -> p (a t)"), reads=BV)
                    P.barrier()
                if k.stop == "m2c":
                    return
            if USE_CC:
                P.stage = "m4a"
                ssm_scan(k, False)
                exchange_start(k)
            P.stage = "m3"
            attention(k, KT, BKT, V, BV, consts, smask, Bsm, sinkb, Bsink, ones, Bones)
            outputs_kv(k, KTf, BKTf, Vf, BVf)
            P.barrier()
        if "attn" in k.debug:
            o = k.outp("dbg_attn", [128, 4 * TOK], BF16)
            k.dma("sp", o, k.AR[:, 0:4 * TOK], reads=[b for r in BQA for b in r])
            P.barrier()
        if k.stop == "att":
            return
        P.stage = "m4"
        if USE_CC:
            exchange_finish(k)
        ssm_scan(k, True)
        outputs_state(k)
        if "zt" in k.debug:
            o = k.outp("dbg_zt", [128, 4 * TOK], BF16)
            k.dma("sp", o, k.AR[:, 4 * TOK:8 * TOK], reads=[k.BzT])
            P.barrier()
        if k.stop == "ssm":
            return
        mixer_tail(k)


def exchange_start(k):
    nc, P, din = k.nc, k.P, k.din
    k.Bxc = Buf("xchg")
    k.dma("sp", k.selb[:], din["selb"], writes=[k.Bxc])
    k.dma("sp", k.wsel[:], din["wsel"], writes=[k.Bxc])
    ycf = k.YC[:].rearrange("p c g -> p (c g)")
    for s_ in range(8):
        k.ts("dve", k.gsel[:, s_, :], ycf, k.selb[:, s_:s_ + 1], None, ALU.mult, None, [k.BYC, k.Bxc], [k.Bxc])
    bin_d = nc.dram_tensor("xc_in", [512, 64], F32).ap()
    gat_d = nc.dram_tensor("xc_out", [512, 64], F32).ap()
    Bbin, Bgat = Buf(), Buf()
    k.dma("sp", bin_d.rearrange("(s n) f -> n s f", n=64), k.gsel[:], reads=[k.Bxc], writes=[Bbin])
    P.barrier()
    P.dma("pool", lambda e: e.collective_compute("AllReduce", ALU.add, replica_groups=[list(range(8))],
                                                   ins=[bin_d.opt()], outs=[gat_d.opt()]),
          reads=[Bbin], writes=[Bgat], inc=1, semkey=("cc", 0))
    P.barrier()
    k.dma("sp", k.Gx[:], gat_d.rearrange("(s n) f -> n s f", n=64), reads=[Bgat], writes=[k.Bxc])


def exchange_finish(k):
    P = k.P
    with ExitStack() as sc:
        Hs = k.sb("xc_H", [64, 3, 64], F32, sc)
        t1, t2 = k.sb("xc_t1", [64, NG], F32, sc), k.sb("xc_t2", [64, NG], F32, sc)
        pr = k.sb("xc_pr", [64, 2, NG], F32, sc)
        BH, Bt = Buf(), Buf()
        k.memset("dve", Hs[:], 0.0, [BH])
        for d in range(3):
            for s_ in range(8):
                k.stt(Hs[:, d, :], k.Gx[:, s_, :], k.wsel[:, d * 8 + s_:d * 8 + s_ + 1], Hs[:, d, :], ALU.mult, ALU.add,
                      [k.Bxc, BH], [BH])
        hv = lambda d, c: Hs[:, d, c * NG:(c + 1) * NG]
        k.cp("dve", k.YC[:].rearrange("p c g -> p (c g)"), Hs[:, 0, :], [BH, k.BYC], [k.BYC])
        for d in (1, 2):
            cmul(k, "dve", pr[:, 0], pr[:, 1], k.APX[:, d - 1, 0, :], k.APX[:, d - 1, 1, :], hv(d, 0), hv(d, 1),
                 t1[:], t2[:], [k.Bsm, BH], [BH], Bt)
            k.tt("dve", k.YC[:], k.YC[:], pr[:], ALU.add, [BH, k.BYC], [k.BYC])
        P.barrier()


def mixer_prefix(k):
    P, din = k.P, k.din
    with ExitStack() as mx:
        k.UT = k.sb("UTp", [128, NG, 2, NJ], BF16, mx)
        k.BUT = Buf()
        with ExitStack() as sa:
            hnT = k.sb("hnTp", [128, 8, TOK], BF16, sa)
            BhnT = [Buf() for _ in range(NT)]
            W = k.sb("Wp", [128, 8, 640], BF16, sa)
            BW = Buf()
            k.dma("pool", W[:], din["w_in"].rearrange("(kk p) c -> p kk c", p=128)[:, :, 640:1280], writes=[BW])
            with ExitStack() as s1:
                tp = k.ps("mp_tp", [128, 8, 128], BF16, s1)
                norm_transpose(k, "mix_g", list(range(1, 17)), hnT, BhnT, s1, tp, Buf())
                P.barrier()
            with ExitStack() as s3:
                ssm_project(k, hnT, BhnT, W, BW, s3, False)
                P.barrier()
        ssm_scan(k, False)


def cmul(k, eng, outr, outi, ar, ai, br, bi, t1, t2, reads, writes, Bt, neg_i=False):
    rd = list(reads)
    k.tt(eng, t1, ar, br, ALU.mult, rd, [Bt])
    k.tt(eng, t2, ai, bi, ALU.mult, rd + [Bt], [Bt])
    k.tt(eng, outr, t1, t2, ALU.subtract, [Bt], list(writes))
    k.tt(eng, t1, ar, bi, ALU.mult, rd + [Bt] + list(writes), [Bt])
    k.tt(eng, t2, ai, br, ALU.mult, rd + [Bt], [Bt])
    if neg_i:
        k.ts(eng, t1, t1, -1.0, None, ALU.mult, None, [Bt], [Bt])
        k.tt(eng, outi, t1, t2, ALU.subtract, [Bt], list(writes))
    else:
        k.tt(eng, outi, t1, t2, ALU.add, [Bt], list(writes))


def ssm_setup_alloc(k):
    nc = k.nc
    k.APW = k.sb("APW", [64, 2, NG, 17], F32)
    k.RHO = k.sb("RHO16", [64, NG], F32)
    k.RP = k.sb("RP", [64, 7, 2, NG], F32)
    k.E127 = k.sb("E127", [64, 2, NG], F32)
    k.APX = k.sb("APX", [64, 2, 2, NG], F32)
    k.Bsm = Buf("ssm_small")
    k.T_d = nc.dram_tensor("T_d", [128, NG, 2, 256], BF16).ap()
    k.WS_d = nc.dram_tensor("WS_d", [128, NG, 2, 128], BF16).ap()
    k.Q_d = nc.dram_tensor("Q_d", [64, 2, NG, 256], BF16).ap()
    k.E_d = nc.dram_tensor("E_d", [64, 2, NG, NJ], F32).ap()
    k.Btab = Buf("tables_dram")


def ssm_setup(k, sc):
    nc, P, din = k.nc, k.P, k.din
    APW, RHO, RP, B1 = k.APW, k.RHO, k.RP, k.Bsm
    if True:
        def T(name, shape, dt=F32):
            return k.sb("su_" + name, shape, dt, sc)
        are, aim, ldt = T("are", [64, NG]), T("aim", [64, NG]), T("ldt", [64, NG])
        k.dma("sp", are[:], din["a_reT"], writes=[B1])
        k.dma("sp", aim[:], din["a_imT"], writes=[B1])
        k.dma("sp", ldt[:], din["logdt_b"], writes=[B1])
        Br, Bi = T("Br", [64, NG, CH]), T("Bi", [64, NG, CH])
        Cr, Ci = T("Cr", [64, NG, CH]), T("Ci", [64, NG, CH])
        for t, nm in ((Br, "bT_re"), (Bi, "bT_im"), (Cr, "cT_re"), (Ci, "cT_im")):
            k.dma("act", t[:], din[nm], writes=[B1])
        dcol = T("dcol", [128, NG])
        k.dma("sp", dcol[:], din["dcol"], writes=[B1])
        tmask = T("tmask", [128, 2, 256], BF16)
        dmask = T("dmask", [128, 2, 256], BF16)
        k.dma("pool", tmask[:], din["tmask"], writes=[B1])
        k.dma("pool", dmask[:], din["dmask"], writes=[B1])
        dt, dar, dai, mag = T("dt", [64, NG]), T("dar", [64, NG]), T("dai", [64, NG]), T("mag", [64, NG])
        sn, cs = T("sn", [64, NG]), T("cs", [64, NG])
        tmp, rr = T("tmp", [64, NG]), T("rr", [64, NG])
        ki = T("ki", [64, NG], I32)
        k.act(dt[:], ldt[:], AF.Exp, [B1], [B1])
        k.tt("dve", dar[:], dt[:], are[:], ALU.mult, [B1], [B1])
        k.tt("dve", dai[:], dt[:], aim[:], ALU.mult, [B1], [B1])
        k.act(mag[:], dar[:], AF.Exp, [B1], [B1])
        k.act(RHO[:], dar[:], AF.Exp, [B1], [B1], scale=float(LC))
        sin_table(k, sn[:], dai[:], 0.0, tmp[:], ki[:], rr[:], B1, B1, B1, B1, B1)
        sin_table(k, cs[:], dai[:], float(np.pi / 2), tmp[:], ki[:], rr[:], B1, B1, B1, B1, B1)
        A_r, A_i = APW[:, 0, :, 1], APW[:, 1, :, 1]
        k.memset("dve", APW[:, 0, :, 0], 1.0, [B1])
        k.memset("dve", APW[:, 1, :, 0], 0.0, [B1])
        k.tt("dve", A_r, mag[:], cs[:], ALU.mult, [B1], [B1])
        k.tt("dve", A_i, mag[:], sn[:], ALU.mult, [B1], [B1])
        t1, t2 = T("t1", [64, NG]), T("t2", [64, NG])
        for p in range(2, 17):
            cmul(k, "dve", APW[:, 0, :, p], APW[:, 1, :, p], APW[:, 0, :, p - 1], APW[:, 1, :, p - 1], A_r, A_i,
                 t1[:], t2[:], [B1], [B1], B1)
        den, nr, fre, fim = T("den", [64, NG]), T("nr", [64, NG]), T("fre", [64, NG]), T("fim", [64, NG])
        k.tt("dve", den[:], are[:], are[:], ALU.mult, [B1], [B1])
        k.tt("dve", t1[:], aim[:], aim[:], ALU.mult, [B1], [B1])
        k.tt("dve", den[:], den[:], t1[:], ALU.add, [B1], [B1])
        k.recip(den[:], den[:], [B1], [B1])
        k.ts("dve", nr[:], A_r, -1.0, None, ALU.add, None, [B1], [B1])
        k.tt("dve", t1[:], nr[:], are[:], ALU.mult, [B1], [B1])
        k.tt("dve", t2[:], A_i, aim[:], ALU.mult, [B1], [B1])
        k.tt("dve", t1[:], t1[:], t2[:], ALU.add, [B1], [B1])
        k.tt("dve", fre[:], t1[:], den[:], ALU.mult, [B1], [B1])
        k.tt("dve", t1[:], A_i, are[:], ALU.mult, [B1], [B1])
        k.tt("dve", t2[:], nr[:], aim[:], ALU.mult, [B1], [B1])
        k.tt("dve", t1[:], t1[:], t2[:], ALU.subtract, [B1], [B1])
        k.tt("dve", fim[:], t1[:], den[:], ALU.mult, [B1], [B1])
        Bbr, Bbi = T("Bbr", [64, NG, CH]), T("Bbi", [64, NG, CH])
        u1, u2 = T("u1", [64, NG, CH]), T("u2", [64, NG, CH])
        bc = lambda ap: ap.unsqueeze(2).to_broadcast([64, NG, CH])
        cmul(k, "dve", Bbr[:], Bbi[:], bc(fre[:]), bc(fim[:]), Br[:], Bi[:], u1[:], u2[:], [B1], [B1], B1)
        Anr, Ani = T("Anr", [64, NG, LC]), T("Ani", [64, NG, LC])
        m2 = T("m2", [64, NG, LC])
        k.tt("dve", m2[:], APW[:, 0, :, 0:LC], APW[:, 0, :, 0:LC], ALU.mult, [B1], [B1])
        k.tt("dve", u1[:], APW[:, 1, :, 0:LC], APW[:, 1, :, 0:LC], ALU.mult, [B1], [B1])
        k.tt("dve", m2[:], m2[:], u1[:], ALU.add, [B1], [B1])
        k.recip(m2[:], m2[:], [B1], [B1])
        k.tt("dve", Anr[:], APW[:, 0, :, 0:LC], m2[:], ALU.mult, [B1], [B1])
        k.stt(Ani[:], APW[:, 1, :, 0:LC], -1.0, m2[:], ALU.mult, ALU.mult, [B1], [B1])
        rrho = T("rrho", [64, NG])
        k.recip(rrho[:], RHO[:], [B1], [B1])
        k.tt("dve", RP[:, 0, 0, :], APW[:, 0, :, 16], rrho[:], ALU.mult, [B1], [B1])
        k.tt("dve", RP[:, 0, 1, :], APW[:, 1, :, 16], rrho[:], ALU.mult, [B1], [B1])
        for l in range(1, 7):
            cmul(k, "dve", RP[:, l, 0, :], RP[:, l, 1, :], RP[:, l - 1, 0, :], RP[:, l - 1, 1, :],
                 RP[:, l - 1, 0, :], RP[:, l - 1, 1, :], t1[:], t2[:], [B1], [B1], B1)
        sqa = T("sqa", [64, 2, 2, NG])
        k.cp("dve", sqa[:, 0, 0, :], APW[:, 0, :, 16], [B1], [B1])
        k.cp("dve", sqa[:, 0, 1, :], APW[:, 1, :, 16], [B1], [B1])
        cur = 0
        for l in range(7):
            cmul(k, "dve", sqa[:, 1 - cur, 0, :], sqa[:, 1 - cur, 1, :], sqa[:, cur, 0, :], sqa[:, cur, 1, :],
                 sqa[:, cur, 0, :], sqa[:, cur, 1, :], t1[:], t2[:], [B1], [B1], B1)
            cur = 1 - cur
        k.cp("dve", k.APX[:, 0], sqa[:, cur], [B1], [B1])
        cmul(k, "dve", k.APX[:, 1, 0, :], k.APX[:, 1, 1, :], sqa[:, cur, 0, :], sqa[:, cur, 1, :],
             sqa[:, cur, 0, :], sqa[:, cur, 1, :], t1[:], t2[:], [B1], [B1], B1)
        with ExitStack() as se:
            GE = 8
            Eall = k.sb("su_Eall", [64, 2, GE, NJ], F32, se)
            ew1, ew2 = k.sb("su_ew1", [64, GE, NJ // 2], F32, se), k.sb("su_ew2", [64, GE, NJ // 2], F32, se)
            BEa, Bew = Buf(), Buf()
            for ge in range(NG // GE):
                es_ = slice(ge * GE, (ge + 1) * GE)
                k.memset("pool", Eall[:, 0, :, 0:1], 1.0, [BEa])
                k.memset("pool", Eall[:, 1, :, 0:1], 0.0, [BEa])
                k.cp("pool", Eall[:, 0, :, 1:2], RP[:, 0, 0, es_].unsqueeze(2), [B1, BEa], [BEa])
                k.cp("pool", Eall[:, 1, :, 1:2], RP[:, 0, 1, es_].unsqueeze(2), [B1, BEa], [BEa])
                for l in range(1, 7):
                    m = 1 << l
                    bR = lambda c_: RP[:, l, c_, es_].unsqueeze(2).to_broadcast([64, GE, m])
                    cmul(k, "pool" if l < 5 else "dve", Eall[:, 0, :, m:2 * m], Eall[:, 1, :, m:2 * m],
                         Eall[:, 0, :, 0:m], Eall[:, 1, :, 0:m], bR(0), bR(1), ew1[:, :, 0:m], ew2[:, :, 0:m],
                         [B1, BEa], [BEa], Bew)
                k.dma("sp", k.E_d[:, :, es_, :], Eall[:], reads=[BEa], writes=[k.Btab])
                k.cp("dve", k.E127[:, :, es_], Eall[:, :, :, NJ - 1], [BEa, B1], [B1])
            P.barrier()
        GB = 2
        NSET = 3
        w1_, w2_ = T("w1", [64, GB, LC, CH]), T("w2", [64, GB, LC, CH])
        w3_, w4_ = T("w3", [64, GB, LC, CH]), T("w4", [64, GB, LC, CH])
        Bw, Bw2 = Buf(), Buf()
        Pre = [T(f"Pre{i}", [64, GB, 256], BF16) for i in range(NSET)]
        Pim = [T(f"Pim{i}", [64, GB, 256], BF16) for i in range(NSET)]
        nPim = [T(f"nPim{i}", [64, GB, 256], BF16) for i in range(NSET)]
        Qri = [T(f"Qri{i}", [64, 2, GB, 256], BF16) for i in range(NSET)]
        BP = [Buf() for _ in range(NSET)]
        BQ = [Buf() for _ in range(NSET)]
        Tb = T("Tb", [128, GB, 2, 256], BF16)
        WSb = T("WSb", [128, GB, 2, 128], BF16)
        tmpT = [T(f"tmpT{i}", [128, 256]) for i in range(2)]
        BtmpT = [Buf(), Buf()]
        BTb, BWSb = Buf(), Buf()
        pT = k.ps("su_pT", [128, 512], F32, sc)
        BpT = Buf()
        pW = k.ps("su_pW", [128, 2 * GB, 128], BF16, sc)
        BpW = Buf()
        v4 = lambda ap: ap.rearrange("p g (s c) -> p g s c", c=CH)
        NB_ = NG // GB

        def emit_pq(gb):
            i = gb % NSET
            gs_ = slice(gb * GB, (gb + 1) * GB)
            bB = lambda ap: ap[:, gs_, :].unsqueeze(2).to_broadcast([64, GB, LC, CH])
            bA = lambda ap: ap.unsqueeze(3).to_broadcast([64, GB, LC, CH])
            cmul(k, "pool", v4(Pre[i][:]), v4(Pim[i][:]), bB(Bbr), bB(Bbi), bA(Anr[:, gs_, :]), bA(Ani[:, gs_, :]),
                 w1_[:], w2_[:], [B1], [BP[i]], Bw)
            k.ts("pool", nPim[i][:], Pim[i][:], -1.0, None, ALU.mult, None, [BP[i]], [BP[i]])
            cmul(k, "dve", v4(Qri[i][:, 0]), v4(Qri[i][:, 1]), bA(APW[:, 0, gs_, 0:LC]), bA(APW[:, 1, gs_, 0:LC]),
                 bB(Cr), bB(Ci), w3_[:], w4_[:], [B1], [BQ[i]], Bw2)

        def emit_tw(gb):
            i = gb % NSET
            gs_ = slice(gb * GB, (gb + 1) * GB)
            n = 0
            for gl in range(GB):
                g = gb * GB + gl
                for sh in range(2):
                    j = n % 2
                    n += 1
                    hs_ = slice(sh * 128, (sh + 1) * 128)
                    k.mm(pT[:, 0:256], Pre[i][:, gl, hs_], Qri[i][:, 0, gl, :], True, False, [BP[i], BQ[i]], [BpT])
                    k.mm(pT[:, 0:256], nPim[i][:, gl, hs_], Qri[i][:, 1, gl, :], False, True, [BP[i], BQ[i]], [BpT])
                    k.tt("dve", tmpT[j][:], pT[:, 0:256], tmask[:, sh, :], ALU.mult, [BpT, B1], [BtmpT[j]])
                    k.stt(Tb[:, gl, sh, :], dmask[:, sh, :], dcol[:, g:g + 1], tmpT[j][:], ALU.mult, ALU.add,
                          [BtmpT[j], B1], [BTb])
            for sh in range(2):
                hs_ = slice(sh * 128, (sh + 1) * 128)
                for gl in range(GB):
                    k.tr(pW[0:128, sh * GB + gl, 0:64], Pre[i][:, gl, hs_], k.identb[0:64, 0:64], [BP[i], k.Bident], [BpW])
                    k.tr(pW[0:128, sh * GB + gl, 64:128], Pim[i][:, gl, hs_], k.identb[0:64, 0:64], [BP[i], k.Bident], [BpW])
            k.cp("act", WSb[:].rearrange("p g s n -> p s g n"), pW[:].rearrange("p (s g) n -> p s g n", s=2), [BpW], [BWSb])
            k.dma("sp", k.T_d[:, gs_, :, :], Tb[:], reads=[BTb], writes=[k.Btab])
            k.dma("sp", k.WS_d[:, gs_, :, :], WSb[:], reads=[BWSb], writes=[k.Btab])
            k.dma("sp", k.Q_d[:, :, gs_, :], Qri[i][:], reads=[BQ[i]], writes=[k.Btab])

        for gb in range(NB_ + 2):
            if gb < NB_:
                emit_pq(gb)
            if gb >= 2:
                emit_tw(gb - 2)
        P.barrier()


def ssm_project(k, hnT, BhnT, W, BW, sc, own):
    P = k.P
    UT, BUT = k.UT, k.BUT
    uJ = k.sb("uJ", [128, NG, 8, CH], BF16, sc)
    BuJ = Buf()
    pu = [k.ps(f"pu{i}", [128, 512], F32, sc) for i in range(2)]
    Bpu = [Buf(), Buf()]
    ptr = [k.ps(f"ptu{i}", [128, 8, 128], BF16, sc) for i in range(2)]
    Bptr = [Buf(), Buf()]
    rd = [BhnT[t] for t in range(1, 17)] + [BW]
    n = 0
    for sh in range(2):
        for rl in range(8):
            r = sh * 8 + rl
            i = n % 2
            n += 1
            for kk in range(8):
                k.mm(pu[i][:], hnT[:, kk, 128 + r:128 + SEG:LC], W[:, kk, 128:640], kk == 0, kk == 7, rd, [Bpu[i]])
            k.cp("act", uJ[:, :, rl, :], pu[i][:].rearrange("p (g c) -> p g c", c=CH), [Bpu[i]], [BuJ])
        for gq in range(4):
            i = gq % 2
            for gl in range(8):
                g = gq * 8 + gl
                k.tr(ptr[i][:, gl, :], uJ[:, g].rearrange("p r c -> p (r c)"), k.identb[:], [BuJ, k.Bident], [Bptr[i]])
            k.cp("dve", UT[:, gq * 8:(gq + 1) * 8, sh, :], ptr[i][:], [Bptr[i]], [BUT])
    if own:
        uJs = k.sb("uJs", [16, NG, 4, CH], BF16, sc)
        BuJs = Buf()
        st0 = 17 * 128
        for r in range(4):
            i = n % 2
            n += 1
            for kk in range(8):
                k.mm(pu[i][0:16, :], hnT[:, kk, st0 + r:st0 + 64:4], W[:, kk, 128:640], kk == 0, kk == 7,
                     [BhnT[17], BW], [Bpu[i]])
            k.cp("act", uJs[0:16, :, r, :], pu[i][0:16, :].rearrange("p (g c) -> p g c", c=CH), [Bpu[i]], [BuJs])
        pf = ptr[0][:].rearrange("p a b -> p (a b)")
        for g in range(NG):
            k.tr(pf[0:64, g * 16:(g + 1) * 16], uJs[0:16, g].rearrange("p r c -> p (r c)"), k.identb[0:16, 0:16],
                 [BuJs, k.Bident], [Bptr[0]])
        k.cp("dve", k.UTs[0:64, :, :], pf[0:64, 0:512].rearrange("p (g q) -> p g q", q=16), [Bptr[0]], [k.BUTs])


def ssm_scan(k, own):
    nc, P, din = k.nc, k.P, k.din
    UT, BUT = k.UT, k.BUT
    APW, RHO, RP, B1 = k.APW, k.RHO, k.RP, k.Bsm
    YC, BYC = k.YC, k.BYC
    GB = 4
    with ExitStack() as sc:
        def T(name, shape, dt=F32):
            return k.sb("sc_" + name, shape, dt, sc)
        WSb = T("WSb", [128, GB, 2, 128], BF16)
        BWSb = Buf()
        S = T("S", [64, 2, GB, NJ])
        E = T("E", [64, 2, GB, NJ])
        SR = T("SR", [64, 2, GB, NJ])
        V = T("V", [64, 2, GB, NJ])
        dec = T("dec", [64, GB, NJ])
        w1_, w2_ = T("w1", [64, GB, NJ]), T("w2", [64, GB, NJ])
        vin = T("vin", [64, 2, NG])
        Vl = T("Vl", [64, 2, NG])
        El = T("El", [64, 2, NG])
        s3_, s4_ = T("s3", [64, NG]), T("s4", [64, NG])
        BVl, BEl = Buf(), Buf()
        s1_, s2_ = T("s1", [64, GB]), T("s2", [64, GB])
        BS, BE, BSR, BV, Bdec, Bw, Bvin, Bs = (Buf() for _ in range(8))
        pS = [k.ps(f"sc_pS{i}", [128, 4, NJ], F32, sc) for i in range(1)]
        BpS = [Buf()]
        if own:
            Tb = T("Tb", [128, GB, 2, 256], BF16)
            Qb = T("Qb", [64, 2, GB, 256], BF16)
            BTb, BQb = Buf(), Buf()
            Zr, nZi = T("Zr", [64, GB, NJ], BF16), T("nZi", [64, GB, NJ], BF16)
            BZ = Buf()
            zJ = T("zJ", [128, LC, GB, CH], BF16)
            BzJ = Buf()
            pY = [k.ps(f"sc_pY{i}", [128, 2, 256], F32, sc) for i in range(2)]
            BpY = [Buf(), Buf()]
            pz = [k.ps(f"sc_pz{i}", [128, 8, 128], BF16, sc) for i in range(2)]
            Bpz = [Buf(), Buf()]
            x0 = T("x0", [64, 2, NG, NSEQ])
            Bx0 = Buf()
            k.dma("sp", x0[:], din["x0"], writes=[Bx0])
            Ss = T("Ss", [64, 2, GB, NSEQ])
            Ss2 = T("Ss2", [64, 2, GB, NSEQ])
            Zsr, nZsi = T("Zsr", [64, GB, NSEQ], BF16), T("nZsi", [64, GB, NSEQ], BF16)
            ws1, ws2 = T("ws1", [64, GB, NSEQ]), T("ws2", [64, GB, NSEQ])
            zJs = T("zJs", [16, LC, GB, CH], BF16)
            BSs, BZs, Bws, BzJs = Buf(), Buf(), Buf(), Buf()
        yn = 0
        cmul(k, "dve", vin[:, 0], vin[:, 1], RP[:, 0, 0, :], RP[:, 0, 1, :], YC[:, 0], YC[:, 1],
             s3_[:], s4_[:], [B1, BYC, Bvin], [Bvin], Bs)
        for gb in range(NG // GB):
            gs_ = slice(gb * GB, (gb + 1) * GB)
            k.dma("sp", WSb[:], k.WS_d[:, gs_, :, :], reads=[k.Btab], writes=[BWSb])
            if own:
                k.dma("act", Tb[:], k.T_d[:, gs_, :, :], reads=[k.Btab], writes=[BTb])
                k.dma("act", Qb[:], k.Q_d[:, :, gs_, :], reads=[k.Btab], writes=[BQb])
            for gl in range(GB):
                g = gb * GB + gl
                h = gl // 4
                for sh in range(2):
                    k.mm(pS[h][:, gl % 4, :], WSb[:, gl, sh, :], UT[:, g, sh, :], sh == 0, sh == 1,
                         [BWSb, BUT], [BpS[h]])
            for h in range(GB // 4):
                k.cp("act", S[:, 0, 4 * h:4 * h + 4, :], pS[h][0:64], [BpS[h]], [BS])
                k.cp("dve", S[:, 1, 4 * h:4 * h + 4, :], pS[h][64:128], [BpS[h], BS], [BS])
            k.dma("act", E[:], k.E_d[:, :, gs_, :], reads=[k.Btab], writes=[BE])
            k.tt("pool", w1_[:], E[:, 0], S[:, 0], ALU.mult, [BE, BS, Bw], [Bw])
            k.tt("pool", w2_[:], E[:, 1], S[:, 1], ALU.mult, [BE, BS, Bw], [Bw])
            k.tt("pool", SR[:, 0], w1_[:], w2_[:], ALU.add, [Bw, BSR], [BSR])
            k.tt("dve", V[:, 0], E[:, 0], S[:, 1], ALU.mult, [BE, BS, BV], [BV])
            k.tt("dve", V[:, 1], E[:, 1], S[:, 0], ALU.mult, [BE, BS, BV], [BV])
            k.tt("dve", SR[:, 1], V[:, 0], V[:, 1], ALU.subtract, [BV, BSR], [BSR])
            k.cp("pool", dec[:], RHO[:, gs_].unsqueeze(2).to_broadcast([64, GB, NJ]), [B1, Bdec], [Bdec])
            for gl in range(GB):
                for c_ in range(2):
                    P.op("dve", lambda e, o=V[:, c_, gl, :], d0=dec[:, gl, :], d1=SR[:, c_, gl, :],
                         ini=vin[:, c_, gb * GB + gl:gb * GB + gl + 1]: e.tensor_tensor_scan(out=o, data0=d0, data1=d1, initial=ini,
                                                                         op0=ALU.mult, op1=ALU.add),
                         reads=[Bdec, BSR, Bvin, BV], writes=[BV])
            k.cp("dve", Vl[:, :, gs_], V[:, :, :, NJ - 1], [BV, BVl], [BVl])
            if not own:
                continue
            k.tt("pool", SR[:], V[:], SR[:], ALU.subtract, [BV, BSR], [BSR])
            cmul(k, "pool", Zr[:], nZi[:], E[:, 0], E[:, 1], SR[:, 0], SR[:, 1], w1_[:], w2_[:], [BE, BSR, BZ], [BZ],
                 Bw, neg_i=True)
            bAp = lambda c_, p: APW[:, c_, gs_, p].unsqueeze(2).to_broadcast([64, GB, NSEQ])
            cmul(k, "pool", Zsr[:], nZsi[:], bAp(0, 1), bAp(1, 1), x0[:, 0, gs_, :], x0[:, 1, gs_, :],
                 ws1[:], ws2[:], [B1, Bx0, BZs], [BZs], Bws, neg_i=True)
            pSs = pS[0][:].rearrange("p a b -> p (a b)")[:, 0:GB * NSEQ].rearrange("p (g q) -> p g q", q=NSEQ)
            for gl in range(GB):
                g = gb * GB + gl
                k.mm(pSs[:, gl, :], WSb[0:64, gl, 0, :], k.UTs[0:64, g, :], True, True, [BWSb, k.BUTs, BS], [BpS[0]])
            k.cp("act", Ss[:, 0], pSs[0:64], [BpS[0]], [BSs])
            k.cp("dve", Ss[:, 1], pSs[64:128], [BpS[0], BSs], [BSs])
            XSg = k.XS[:, :, gs_, :]
            cmul(k, "pool", XSg[:, 0], XSg[:, 1], bAp(0, 4), bAp(1, 4), x0[:, 0, gs_, :], x0[:, 1, gs_, :],
                 ws1[:], ws2[:], [B1, Bx0], [k.BXS], Bws)
            cmul(k, "pool", Ss2[:, 0], Ss2[:, 1], bAp(0, 3), bAp(1, 3), Ss[:, 0], Ss[:, 1],
                 ws1[:], ws2[:], [B1, BSs], [BSs], Bws)
            k.tt("pool", XSg, XSg, Ss2[:], ALU.add, [BSs, k.BXS], [k.BXS])
            for gl in range(GB):
                g = gb * GB + gl
                i = (yn // 2) % 2
                po = pY[i][:, gl % 2, :]
                k.mm(po, UT[:, g, 0, :], Tb[:, gl, 0, :], True, False, [BUT, BTb], [BpY[i]])
                k.mm(po, UT[:, g, 1, :], Tb[:, gl, 1, :], False, False, [BUT, BTb], [BpY[i]])
                k.mm(po, Zr[:, gl, :], Qb[:, 0, gl, :], False, False, [BZ, BQb], [BpY[i]])
                k.mm(po, nZi[:, gl, :], Qb[:, 1, gl, :], False, True, [BZ, BQb], [BpY[i]])
                yn += 1
                if gl % 2 == 1:
                    k.act(zJ[:, :, gl - 1:gl + 1, :].rearrange("p r g c -> p g r c"),
                          pY[i][:].rearrange("p g (r c) -> p g r c", c=CH), AF.Gelu, [BpY[i]], [BzJ])
            for rh in range(2):
                for rl in range(8):
                    r = rh * 8 + rl
                    k.tr(pz[rh][0:64, rl, :], zJ[:, r].rearrange("p g c -> p (g c)"), k.identb[:], [BzJ, k.Bident], [Bpz[rh]])
                hp = (gb % 2) * 64
                dst = k.zT[hp:hp + 64, gb // 2, 128:128 + SEG].rearrange("p (j r) -> p r j", r=LC)[:, rh * 8:(rh + 1) * 8, :]
                k.cp("act" if rh == 0 else "dve", dst, pz[rh][0:64], [Bpz[rh]], [k.BzT])
            for gp in range(GB // 2):
                i = gp % 2
                for q_ in range(2):
                    gl = gp * 2 + q_
                    g = gb * GB + gl
                    po = pY[i][0:16, q_, :]
                    k.mm(po, k.UTs[0:64, g, :], Tb[0:64, gl, 0, :], True, False, [k.BUTs, BTb, BzJ], [BpY[i]])
                    k.mm(po, Zsr[:, gl, :], Qb[:, 0, gl, :], False, False, [BZs, BQb], [BpY[i]])
                    k.mm(po, nZsi[:, gl, :], Qb[:, 1, gl, :], False, True, [BZs, BQb], [BpY[i]])
                k.act(zJs[0:16, :, gp * 2:gp * 2 + 2, :].rearrange("p r g c -> p g r c"),
                      pY[i][0:16].rearrange("p g (r c) -> p g r c", c=CH), AF.Gelu, [BpY[i]], [BzJs])
            pzs = pz[0][:].rearrange("p a b -> p (a b)")
            for r in range(4):
                k.tr(pzs[0:64, r * NSEQ:(r + 1) * NSEQ], zJs[0:16, r].rearrange("p g c -> p (g c)"), k.identb[0:16, 0:16],
                     [BzJs, k.Bident], [Bpz[0]])
            dst = k.zT[hp:hp + 64, gb // 2, 17 * 128:17 * 128 + 64].rearrange("p (q r) -> p r q", r=4)
            k.cp("act", dst, pzs[0:64, 0:4 * NSEQ].rearrange("p (r q) -> p r q", q=NSEQ), [Bpz[0]], [k.BzT])
        cmul(k, "dve", YC[:, 0], YC[:, 1], k.E127[:, 0], k.E127[:, 1], Vl[:, 0], Vl[:, 1], s3_[:], s4_[:],
             [B1, BVl, Bvin, BYC], [BYC], Bs)
        P.barrier()


def outputs_kv(k, KTf, BKTf, Vf, BVf):
    nc, P, din, dout = k.nc, k.P, k.din, k.dout
    with ExitStack() as sc:
        pk = k.ps("okv_pk", [128, 256], F32, sc)
        Bpk = Buf()
        ko = k.sb("okv_ko", [128, 256], F32, sc)
        Bko = Buf()
        k.tr(pk[:, 0:128], KTf[:, 0:128], k.identf[:], [BKTf, k.Bidentf], [Bpk])
        k.tr(pk[0:64, 128:256], KTf[:, 128:192], k.identf[:], [BKTf, k.Bidentf], [Bpk])
        k.cp("act", ko[:], pk[:], [Bpk], [Bko])
        k.dma("sp", dout["kp"], ko[:, 0:128], reads=[Bko])
        k.dma("sp", dout["vp"], Vf[:, 0, :], reads=[BVf])
        scr_k = nc.dram_tensor("scr_k", [64, 128], F32).ap()
        scr_v = nc.dram_tensor("scr_v", [64, 128], F32).ap()
        Bsk, Bsv = Buf(), Buf()
        k.dma("sp", scr_k, ko[0:64, 128:256], reads=[Bko], writes=[Bsk])
        k.dma("sp", scr_v, Vf[0:64, 1, :], reads=[BVf], writes=[Bsv])
        k.dma("sp", dout["ks"][:, 124:128, :], scr_k.rearrange("(q t) f -> q t f", t=4), reads=[Bsk])
        k.dma("sp", dout["vs"][:, 124:128, :], scr_v.rearrange("(q t) f -> q t f", t=4), reads=[Bsv])
        k.dma("act", dout["ks"][:, 0:124, :], din["cache_k"][:, 4:128, :])
        k.dma("act", dout["vs"][:, 0:124, :], din["cache_v"][:, 4:128, :])
        P.barrier()


def outputs_state(k):
    nc, P, dout = k.nc, k.P, k.dout
    with ExitStack() as sc:
        xe = k.sb("os_xe", [64, 2, NG], F32, sc)
        t1, t2 = k.sb("os_t1", [64, NG], F32, sc), k.sb("os_t2", [64, NG], F32, sc)
        Bxe, Bt = Buf(), Buf()
        cmul(k, "dve", xe[:, 0], xe[:, 1], k.APW[:, 0, :, 15], k.APW[:, 1, :, 15], k.YC[:, 0], k.YC[:, 1],
             t1[:], t2[:], [k.Bsm, k.BYC], [Bxe], Bt)
        pp = k.ps("os_pp", [32, 2, 64], F32, sc)
        Bpp = Buf()
        for c_ in range(2):
            k.tr(pp[:, c_, :], xe[:, c_, :], k.identf[0:64, 0:64], [Bxe, k.Bidentf], [Bpp])
        so = k.sb("os_so", [32, 2, 64], F32, sc)
        Bso = Buf()
        k.cp("act", so[:], pp[:], [Bpp], [Bso])
        k.dma("sp", dout["rp"], so[:, 0, :], reads=[Bso])
        k.dma("sp", dout["ip"], so[:, 1, :], reads=[Bso])
        ps_ = [k.ps(f"os_ps{i}", [16, 8, 64], F32, sc) for i in range(2)]
        Bps = [Buf(), Buf()]
        xs_o = k.sb("os_xs", [16, 2, NG, 64], F32, sc)
        Bxo = Buf()
        n = 0
        for c_ in range(2):
            for gq in range(4):
                i = n % 2
                n += 1
                for gl in range(8):
                    g = gq * 8 + gl
                    k.tr(ps_[i][:, gl, :], k.XS[:, c_, g, :], k.identf[0:64, 0:64], [k.BXS, k.Bidentf], [Bps[i]])
                k.cp("act", xs_o[:, c_, gq * 8:(gq + 1) * 8, :], ps_[i][:], [Bps[i]], [Bxo])
        k.dma("sp", dout["rs"], xs_o[:, 0], reads=[Bxo])
        k.dma("sp", dout["is"], xs_o[:, 1], reads=[Bxo])
        P.barrier()


def mixer_tail(k):
    nc, P, din = k.nc, k.P, k.din
    QA, BQA, zT, BzT = k.QA, k.BQA, k.zT, k.BzT
    blocks = [(c0, min(512, TOK - c0)) for c0 in range(128, TOK, 512)]
    P.stage = "glu"
    with ExitStack() as sc:
        ssmT = k.ssmT
        BssmT = Buf()
        with ExitStack() as s1:
            Wg = k.sb("Wglu", [128, 4, 512], BF16, s1)
            bg = k.sb("bglu", [128, 4], F32, s1)
            BWg = Buf()
            k.dma("pool", Wg[:], din["w_glu"].rearrange("(kk p) c -> p kk c", p=128), writes=[BWg])
            k.dma("sp", bg[:], din["b_glu"], writes=[BWg])
            pg = [k.ps(f"mt_pg{i}", [128, 512], F32, s1) for i in range(2)]
            Bpg = [Buf(), Buf()]
            sg = [k.sb(f"mt_sg{i}", [128, 512], BF16, s1) for i in range(2)]
            Bsg = [Buf(), Buf()]
            n = 0
            for ct in range(4):
                for (c0, nb) in blocks:
                    i = n % 2
                    n += 1
                    for kt in range(4):
                        k.mm(pg[i][:, 0:nb], Wg[:, kt, ct * 128:(ct + 1) * 128], zT[:, kt, c0:c0 + nb], kt == 0, kt == 3,
                             [BWg, BzT], [Bpg[i]])
                    k.act(sg[i][:, 0:nb], pg[i][:, 0:nb], AF.Sigmoid, [Bpg[i], BWg], [Bsg[i]], bias=bg[:, ct:ct + 1])
                    k.tt("pool", ssmT[:, ct, c0:c0 + nb], zT[:, ct, c0:c0 + nb], sg[i][:, 0:nb], ALU.mult,
                         [BzT, Bsg[i]], [BssmT])
            P.barrier()
        win = din["w_in"].rearrange("(kk p) c -> p kk c", p=128)
        wao = din["w_ao"].rearrange("(kk p) c -> p kk c", p=128)
        wso = din["w_ssm_out"].rearrange("(kk p) c -> p kk c", p=128)
        for (ta, tb) in ((1, 10), (10, NT)):
            tl_h = list(range(ta, tb))
            nh = len(tl_h) * 128
            cbase = ta * 128
            with ExitStack() as sh_:
                hnT = k.sb("hnT2", [128, 8, 1152], BF16, sh_)
                BhnT = {t: Buf() for t in tl_h}
                mT = k.mTh
                BmT = {t: Buf() for t in tl_h}
                P.stage = "m5norm"
                with ExitStack() as s2:
                    tp = k.ps("mt_tp", [128, 8, 128], BF16, s2)
                    norm_transpose(k, "mix_g", tl_h, hnT, BhnT, s2, tp, Buf(), t0=ta)
                    P.barrier()
                P.stage = "m5"
                with ExitStack() as s3:
                    NBm = 256
                    Wm = [k.sb(f"Wm{i}", [128, 3072], BF16, s3) for i in range(2)]
                    Wga = [Wm[i][:, 0:1024].rearrange("p (a c) -> p a c", a=8) for i in range(2)]
                    Wgs = [Wm[i][:, 1024:2048].rearrange("p (a c) -> p a c", a=8) for i in range(2)]
                    Wao = [Wm[i][:, 2048:2560].rearrange("p (a c) -> p a c", a=4) for i in range(2)]
                    Wso = [Wm[i][:, 2560:3072].rearrange("p (a c) -> p a c", a=4) for i in range(2)]
                    BWm = [Buf(), Buf()]
                    pp = [[k.ps(f"mt_p{j}{i}", [128, 512], F32, s3) for i in range(2)] for j in range(4)]
                    Bpp = [[Buf(), Buf()] for _ in range(4)]
                    sga = [k.sb(f"mt_sga{i}", [128, NBm], BF16, s3) for i in range(2)]
                    sgs = [k.sb(f"mt_sgs{i}", [128, NBm], BF16, s3) for i in range(2)]
                    m1 = [k.sb(f"mt_m1{i}", [128, NBm], F32, s3) for i in range(2)]
                    m2 = [k.sb(f"mt_m2{i}", [128, NBm], F32, s3) for i in range(2)]
                    Bsga, Bsgs, Bm1, Bm2 = ([Buf(), Buf()] for _ in range(4))
                    n = 0
                    for dt_ in range(8):
                        w = dt_ % 2
                        cs_ = slice(dt_ * 128, (dt_ + 1) * 128)
                        k.dma("pool", Wm[w][:], din["wm5"][dt_], writes=[BWm[w]])
                        for l0 in range(0, nh, NBm):
                            nb = min(NBm, nh - l0)
                            c0 = cbase + l0
                            i = n % 2
                            n += 1
                            tl = list(range(c0 // 128, (c0 + nb) // 128))
                            rdh = [BhnT[t] for t in tl] + [BWm[w]]
                            for kk in range(8):
                                k.mm(pp[0][i][:, 0:nb], Wga[w][:, kk, :], hnT[:, kk, l0:l0 + nb], kk == 0, kk == 7, rdh, [Bpp[0][i]])
                            for kk in range(8):
                                k.mm(pp[1][i][:, 0:nb], Wgs[w][:, kk, :], hnT[:, kk, l0:l0 + nb], kk == 0, kk == 7, rdh, [Bpp[1][i]])
                            rq = [BQA[t][h] for t in tl for h in range(2)] + [BWm[w]]
                            for kk in range(4):
                                k.mm(pp[2][i][:, 0:nb], Wao[w][:, kk, :], QA[:, kk, c0:c0 + nb], kk == 0, kk == 3, rq, [Bpp[2][i]])
                            for kk in range(4):
                                k.mm(pp[3][i][:, 0:nb], Wso[w][:, kk, :], ssmT[:, kk, c0:c0 + nb], kk == 0, kk == 3,
                                     [BssmT, BWm[w]], [Bpp[3][i]])
                            k.act(sga[i][:, 0:nb], pp[0][i][:, 0:nb], AF.Sigmoid, [Bpp[0][i]], [Bsga[i]])
                            k.act(sgs[i][:, 0:nb], pp[1][i][:, 0:nb], AF.Sigmoid, [Bpp[1][i]], [Bsgs[i]])
                            k.tt("dve", m1[i][:, 0:nb], pp[2][i][:, 0:nb], sga[i][:, 0:nb], ALU.mult, [Bpp[2][i], Bsga[i]], [Bm1[i]])
                            k.tt("dve", m2[i][:, 0:nb], pp[3][i][:, 0:nb], sgs[i][:, 0:nb], ALU.mult, [Bpp[3][i], Bsgs[i]], [Bm2[i]])
                            k.tt("pool", mT[:, dt_, l0:l0 + nb], m1[i][:, 0:nb], m2[i][:, 0:nb], ALU.add, [Bm1[i], Bm2[i]],
                                 [BmT[t] for t in tl])
                    P.barrier()
                P.stage = "m6"
                with ExitStack() as s4:
                    Wo = k.sb("Wout", [128, 8, D], BF16, s4)
                    BWo = Buf()
                    k.dma("pool", Wo[:], din["w_out"].rearrange("(kk p) c -> p kk c", p=128), writes=[BWo])
                    po = [k.ps(f"mt_po{i}", [128, 512], F32, s4) for i in range(4)]
                    Bpo = [Buf() for _ in range(4)]
                    n = 0
                    for t in tl_h:
                        lt = (t - ta) * 128
                        for hh in range(2):
                            i = n % 4
                            n += 1
                            for kk in range(8):
                                k.mm(po[i][:], mT[:, kk, lt:lt + 128], Wo[:, kk, hh * 512:(hh + 1) * 512], kk == 0, kk == 7,
                                     [BmT[t], BWo], [Bpo[i]])
                            xs_ = k.X[:, t, hh * 512:(hh + 1) * 512]
                            k.tt("dve", xs_, po[i][:], xs_, ALU.add, [Bpo[i], k.BX[t][hh]], [k.BX[t][hh]])
                    P.barrier()


def attention(k, KT, BKT, V, BV, consts, smask, Bsm, sinkb, Bsink, ones, Bones):
    P, din = k.P, k.din
    QA, BQA = k.QA, k.BQA
    mdiag, Bmd = consts["mdiag"]
    mprev, Bmp = consts["mprev"]
    mprev1, Bmp1 = consts["mprev1"]
    with ExitStack() as s0:
        Kc = k.sb("Kc", [128, NSEQ, 128], BF16, s0)
        KcT = k.sb("KcT", [128, NSEQ, 128], BF16, s0)
        Vc = k.sb("Vc", [128, NSEQ, 128], BF16, s0)
        BKc, BKcT, BVc = Buf(), Buf(), Buf()
        k.dma("pool", Kc[:], din["cache_k"].rearrange("q s f -> s q f"), writes=[BKc])
        k.dma("pool", Vc[:], din["cache_v"].rearrange("q s f -> s q f"), writes=[BVc])
        with ExitStack() as s1:
            ptr = [k.ps(f"ptrc{i}", [128, 8, 128], BF16, s1) for i in range(2)]
            Bptr = [Buf(), Buf()]
            for hh in range(2):
                for j in range(8):
                    k.tr(ptr[hh][:, j, :], Kc[:, hh * 8 + j, :], k.identb[:], [BKc, k.Bident], [Bptr[hh]])
                k.cp("act", KcT[:, hh * 8:(hh + 1) * 8, :], ptr[hh][:], [Bptr[hh]], [BKcT])
            P.barrier()
        attention_main(k, KT, BKT, V, BV, consts, smask, Bsm, sinkb, Bsink, ones, Bones, KcT, BKcT, Vc, BVc)


def attention_main(k, KT, BKT, V, BV, consts, smask, Bsm, sinkb, Bsink, ones, Bones, KcT, BKcT, Vc, BVc):
    P, din = k.P, k.din
    QA, BQA = k.QA, k.BQA
    mdiag, Bmd = consts["mdiag"]
    mprev, Bmp = consts["mprev"]
    mprev1, Bmp1 = consts["mprev1"]
    with ExitStack() as sc:
        psc = [[k.ps(f"psc{i}{j}", [128, 512], F32, sc) for j in range(2)] for i in range(2)]
        Bpsc = [[Buf(), Buf()] for _ in range(2)]
        pnum = [k.ps(f"pnum{i}", [128, 512], F32, sc) for i in range(2)]
        pden = [k.ps(f"pden{i}", [128, 512], F32, sc) for i in range(2)]
        Bpnum, Bpden = [Buf(), Buf()], [Buf(), Buf()]
        Pb = [[k.sb(f"Pb{i}{j}", [128, 512], BF16, sc) for j in range(2)] for i in range(2)]
        Pm = [[k.sb(f"Pm{i}{j}", [128, 512], BF16, sc) for j in range(2)] for i in range(2)]
        BPb = [[Buf(), Buf()] for _ in range(2)]
        BPm = [[Buf(), Buf()] for _ in range(2)]
        rec = [k.sb(f"rec{i}", [128, 512], F32, sc) for i in range(2)]
        Brec = [Buf(), Buf()]
        it = 0
        for qt in range(1, 17):
            cq = qt * 128
            for kh in range(2):
                i = it % 2
                it += 1
                r0, r1 = kh * 64, (kh + 1) * 64
                qv = QA[r0:r1, :, cq:cq + 128]
                for kb, kt_ in enumerate((qt - 1, qt)):
                    k.mm(psc[i][kb][:].rearrange("p (a t) -> p a t", a=4), KT[r0:r1, kt_ * 128:(kt_ + 1) * 128], qv,
                         True, True, [BKT[kt_], BQA[qt][kh]], [Bpsc[i][kb]])
                    k.act(Pb[i][kb][:], psc[i][kb][:], AF.Exp, [Bpsc[i][kb]], [BPb[i][kb]])
                    if kb == 1:
                        m, Bm = mdiag, Bmd
                    elif qt == 1:
                        m, Bm = mprev1, Bmp1
                    else:
                        m, Bm = mprev, Bmp
                    k.tt("pool", Pm[i][kb][:].rearrange("p (a t) -> p a t", a=4),
                         Pb[i][kb][:].rearrange("p (a t) -> p a t", a=4),
                         m[:, :].unsqueeze(1).to_broadcast([128, 4, 128]), ALU.mult,
                         [BPb[i][kb], Bm], [BPm[i][kb]])
                for kb, kt_ in enumerate((qt - 1, qt)):
                    k.mm(pnum[i][:], V[:, kt_, :], Pm[i][kb][:], kb == 0, kb == 1, [BV[kt_], BPm[i][kb]], [Bpnum[i]])
                for kb in range(2):
                    k.mm(pden[i][:], ones[:], Pm[i][kb][:], kb == 0, kb == 1, [Bones, BPm[i][kb]], [Bpden[i]])
                rv = rec[i][r0:r1, :].rearrange("p (a t) -> p a t", a=4)
                k.tt("dve", rv, pden[i][r0:r1, :].rearrange("p (a t) -> p a t", a=4),
                     sinkb[r0:r1, kh * 4:(kh + 1) * 4].unsqueeze(2).to_broadcast([64, 4, 128]), ALU.add,
                     [Bpden[i], Bsink], [Brec[i]])
                k.recip(rec[i][r0:r1, :], rec[i][r0:r1, :], [Brec[i]], [Brec[i]])
                k.tt("dve", qv, pnum[i][r0:r1, :].rearrange("p (a t) -> p a t", a=4), rv, ALU.mult,
                     [Bpnum[i], Brec[i]], [BQA[qt][kh]])
        st0 = 17 * 128
        for kh in range(2):
            i = it % 2
            it += 1
            r0, r1 = kh * 64, (kh + 1) * 64
            qv = QA[r0:r1, :, st0:st0 + 64]
            v4 = lambda ap: ap.rearrange("p (a t) -> p a t", a=4)
            k.mm(v4(psc[i][1][0:64, 0:256]), KT[r0:r1, st0:st0 + 64], qv, True, True,
                 [BKT[17], BQA[17][kh]], [Bpsc[i][1]])
            for sq_ in range(NSEQ):
                k.mm(v4(psc[i][0][:, 0:256])[:, :, sq_ * 4:(sq_ + 1) * 4], KcT[r0:r1, sq_, :],
                     QA[r0:r1, :, st0 + sq_ * 4:st0 + sq_ * 4 + 4], True, True,
                     [BKcT, BQA[17][kh]], [Bpsc[i][0]])
            k.act(Pb[i][0][:, 0:256], psc[i][0][:, 0:256], AF.Exp, [Bpsc[i][0]], [BPb[i][0]])
            k.act(Pb[i][1][0:64, 0:256], psc[i][1][0:64, 0:256], AF.Exp, [Bpsc[i][1]], [BPb[i][1]])
            k.tt("pool", Pm[i][0][:, 0:256].rearrange("p (a q t) -> p a q t", a=4, t=4),
                 Pb[i][0][:, 0:256].rearrange("p (a q t) -> p a q t", a=4, t=4),
                 mprev[:, 0:4].unsqueeze(1).unsqueeze(1).to_broadcast([128, 4, NSEQ, 4]), ALU.mult,
                 [BPb[i][0], Bmp], [BPm[i][0]])
            k.tt("pool", v4(Pm[i][1][0:64, 0:256]), v4(Pb[i][1][0:64, 0:256]),
                 smask[:, :].unsqueeze(1).to_broadcast([64, 4, 64]), ALU.mult, [BPb[i][1], Bsm], [BPm[i][1]])
            for (pt, Bpt, lcur, lprev) in ((pnum[i], Bpnum[i], V[0:64, 17, :], None), (pden[i], Bpden[i], ones[0:64, :], ones)):
                k.mm(pt[:, 0:256], lcur, Pm[i][1][0:64, 0:256], True, False,
                     [BV[17], Bones, BPm[i][1]], [Bpt])
                for sq_ in range(NSEQ):
                    lp = Vc[:, sq_, :] if lprev is None else ones[:]
                    k.mm(v4(pt[:, 0:256])[:, :, sq_ * 4:(sq_ + 1) * 4], lp,
                         v4(Pm[i][0][:, 0:256])[:, :, sq_ * 4:(sq_ + 1) * 4], False, sq_ == NSEQ - 1,
                         [BVc, Bones, BPm[i][0]], [Bpt])
            rv = v4(rec[i][r0:r1, 0:256])
            k.tt("dve", rv, v4(pden[i][r0:r1, 0:256]),
                 sinkb[r0:r1, kh * 4:(kh + 1) * 4].unsqueeze(2).to_broadcast([64, 4, 64]), ALU.add,
                 [Bpden[i], Bsink], [Brec[i]])
            k.recip(rec[i][r0:r1, 0:256], rec[i][r0:r1, 0:256], [Brec[i]], [Brec[i]])
            k.tt("dve", qv, v4(pnum[i][r0:r1, 0:256]), rv, ALU.mult, [Bpnum[i], Brec[i]], [BQA[17][kh]])
        P.barrier()


_CACHE = {}


def _gvec(g):
    return np.ascontiguousarray(np.asarray(g, np.float32).reshape(8, 128).T)


def make_in_maps(inputs):
    f32 = np.float32
    xp = np.asarray(inputs["x_prompt"], f32)
    xs = np.asarray(inputs["x_sample"], f32)
    w_in = np.ascontiguousarray(np.asarray(inputs["w_in"][0], f32))
    qcols = []
    for j in range(4):
        qcols += list(range(j * 64, (j + 1) * 64)) + list(range((j + 4) * 64, (j + 5) * 64))
    qcols += list(range(512, 640))
    rotm = np.zeros((128, 128), f32)
    for hb in range(2):
        for d in range(32):
            rotm[hb * 64 + d + 32, hb * 64 + d] = -1.0
            rotm[hb * 64 + d, hb * 64 + 32 + d] = 1.0
    onesbd = np.zeros((128, 128), f32)
    onesbd[0:64, 0:64] = 1.0
    onesbd[64:128, 64:128] = 1.0
    inv = np.power(f32(10000.0), (-2.0 * np.arange(32, dtype=f32) / f32(64.0)).astype(f32)).astype(f32)
    invf = np.concatenate([inv, inv, inv, inv]).reshape(128, 1).astype(f32)
    si, qi = np.meshgrid(np.arange(128), np.arange(128), indexing="ij")
    mdiag = (si <= qi).astype(f32)
    mprev = (si > qi).astype(f32)
    a64 = np.arange(64)
    smask = ((a64[:, None] // 4 == a64[None, :] // 4) & (a64[:, None] % 4 <= a64[None, :] % 4)).astype(f32)
    sl = np.arange(128) // 16
    cp_ = np.arange(128) % 16
    rr_ = np.arange(256) // 16
    cc_ = np.arange(256) % 16
    tmask = np.zeros((128, 2, 256), f32)
    dmask = np.zeros((128, 2, 256), f32)
    for sh in range(2):
        sg_ = sl + 8 * sh
        tmask[:, sh, :] = (rr_[None, :] >= sg_[:, None]).astype(f32)
        dmask[:, sh, :] = ((rr_[None, :] == sg_[:, None]) & (cc_[None, :] == cp_[:, None])).astype(f32)
    ao_rows = []
    for j in range(4):
        for kh in range(2):
            ao_rows += list(range((kh * 4 + j) * 64, (kh * 4 + j + 1) * 64))
    sre = np.asarray(inputs["state_ssm_re"], f32)[0]
    sim = np.asarray(inputs["state_ssm_im"], f32)[0]
    w_ao_p = np.asarray(inputs["w_attn_out"][0], f32)[ao_rows]
    w_so = np.asarray(inputs["w_ssm_out"][0], f32)
    wm5 = np.zeros((8, 128, 3072), f32)
    for dt_ in range(8):
        cs = slice(dt_ * 128, (dt_ + 1) * 128)
        wm5[dt_, :, 0:1024] = w_in[:, 1280 + dt_ * 128:1280 + (dt_ + 1) * 128].reshape(8, 128, 128).transpose(1, 0, 2).reshape(128, 1024)
        wm5[dt_, :, 1024:2048] = w_in[:, 2304 + dt_ * 128:2304 + (dt_ + 1) * 128].reshape(8, 128, 128).transpose(1, 0, 2).reshape(128, 1024)
        wm5[dt_, :, 2048:2560] = w_ao_p[:, cs].reshape(4, 128, 128).transpose(1, 0, 2).reshape(128, 512)
        wm5[dt_, :, 2560:3072] = w_so[:, cs].reshape(4, 128, 128).transpose(1, 0, 2).reshape(128, 512)
    common = {
        "wm5": wm5,
        "ffn1_g": _gvec(inputs["ffn1_norm"][0]), "mix_g": _gvec(inputs["mix_norm"][0]),
        "ffn2_g": _gvec(inputs["ffn2_norm"][0]),
        "identb": np.eye(128, dtype=f32),
        "w_qk": np.ascontiguousarray(w_in[:, qcols]), "w_in": w_in,
        "gq2": np.tile(np.asarray(inputs["q_norm"][0], f32), 2).reshape(128, 1),
        "gk2": np.tile(np.asarray(inputs["k_norm"][0], f32), 2).reshape(128, 1),
        "invf": invf, "rotm": rotm, "onesbd": onesbd, "mdiag": mdiag, "mprev": mprev, "smask": smask,
        "sinks": np.asarray(inputs["attn_sinks"], f32).reshape(1, 8),
        "identf": np.eye(128, dtype=f32),
        "a_reT": np.ascontiguousarray(np.asarray(inputs["ssm_a_re"][0], f32).T),
        "a_imT": np.ascontiguousarray(np.asarray(inputs["ssm_a_im"][0], f32).T),
        "logdt_b": np.ascontiguousarray(np.broadcast_to(np.asarray(inputs["ssm_log_dt"][0], f32)[None, :], (64, NG))),
        "bT_re": np.ascontiguousarray(np.asarray(inputs["ssm_b_re"][0], f32).transpose(1, 0, 2)),
        "bT_im": np.ascontiguousarray(np.asarray(inputs["ssm_b_im"][0], f32).transpose(1, 0, 2)),
        "cT_re": np.ascontiguousarray(np.asarray(inputs["ssm_c_re"][0], f32).transpose(2, 0, 1)),
        "cT_im": np.ascontiguousarray(np.asarray(inputs["ssm_c_im"][0], f32).transpose(2, 0, 1)),
        "dcol": np.ascontiguousarray(np.tile(np.asarray(inputs["ssm_d"][0], f32).T, (8, 1))),
        "tmask": tmask, "dmask": dmask,
        "w_glu": np.ascontiguousarray(np.asarray(inputs["w_glu"][0], f32)),
        "b_glu": np.ascontiguousarray(np.asarray(inputs["b_glu"][0], f32).reshape(4, 128).T),
        "w_ao": np.ascontiguousarray(np.asarray(inputs["w_attn_out"][0], f32)[ao_rows]),
        "w_ssm_out": np.ascontiguousarray(np.asarray(inputs["w_ssm_out"][0], f32)),
        "w_out": np.ascontiguousarray(np.asarray(inputs["w_out"][0], f32)),
    }
    for nm in ("ffn1_w1", "ffn1_w3", "ffn1_w2", "ffn2_w1", "ffn2_w3", "ffn2_w2"):
        common[nm] = np.ascontiguousarray(np.asarray(inputs[nm][0], f32))
    ck = np.asarray(inputs["cache_k"], f32)[0].reshape(128, 128, 128)
    cv = np.asarray(inputs["cache_v"], f32)[0].reshape(128, 128, 128)
    maps = []
    for c in range(8):
        b, seg = c // 4, c % 4
        x = np.zeros((TOK, D), f32)
        if seg > 0:
            x[0:128] = xp[b, seg * SEG - 128: seg * SEG]
        x[128:128 + SEG] = xp[b, seg * SEG:(seg + 1) * SEG]
        x[17 * 128:17 * 128 + 64] = xs[16 * c:16 * c + 16].reshape(64, D)
        pos = np.zeros((1, TOK), f32)
        pos[0, 0:128 + SEG] = np.arange(seg * SEG - 128, (seg + 1) * SEG, dtype=f32)
        pos[0, 17 * 128:17 * 128 + 64] = PAST_LEN + (np.arange(64) % 4).astype(f32)
        m = dict(common)
        m["x"] = x
        m["pos"] = pos
        m["mprev1"] = mprev if seg > 0 else np.zeros_like(mprev)
        selb = np.zeros((64, 8), f32)
        selb[:, c] = 1.0
        wsel = np.zeros((64, 3, 8), f32)
        for d in range(3):
            if seg - 1 - d >= 0:
                wsel[:, d, b * 4 + seg - 1 - d] = 1.0
        if USE_CC:
            m["selb"] = selb
            m["wsel"] = wsel.reshape(64, 24)
        else:
            xpre = np.zeros((3, SEG, D), f32)
            for i in range(3):
                sg_i = seg - 3 + i
                if sg_i >= 0:
                    xpre[i] = xp[b, sg_i * SEG:(sg_i + 1) * SEG]
            m["xpre"] = xpre
        m["x0"] = np.ascontiguousarray(np.stack([sre[16 * c:16 * c + 16], sim[16 * c:16 * c + 16]], 0)
                                       .transpose(3, 0, 2, 1))
        m["cache_k"] = np.ascontiguousarray(ck[16 * c:16 * c + 16])
        m["cache_v"] = np.ascontiguousarray(cv[16 * c:16 * c + 16])
        maps.append(m)
    return maps


def kernel(**inputs):
    if "nc" not in _CACHE:
        _CACHE["nc"] = build_program()
    nc = _CACHE["nc"]
    maps = make_in_maps(inputs)
    res = run_bass_kernel_spmd(nc, maps, core_ids=list(range(8)))
    r = res.results
    f32 = np.float32
    yp = np.zeros((2, 8192, D), f32)
    ys = np.zeros((128, 4, D), f32)
    kp = np.zeros((1, 2, 128, 2, 64), f32)
    vp = np.zeros((1, 2, 128, 2, 64), f32)
    rp = np.zeros((1, 2, NG, NS), f32)
    ip = np.zeros((1, 2, NG, NS), f32)
    ks = np.zeros((1, 128, 128, 2, 64), f32)
    vs = np.zeros((1, 128, 128, 2, 64), f32)
    rs = np.zeros((1, 128, NG, NS), f32)
    is_ = np.zeros((1, 128, NG, NS), f32)
    for c in range(8):
        b, seg = c // 4, c % 4
        y = r[c]["y"]
        yp[b, seg * SEG:(seg + 1) * SEG] = y[0:SEG]
        ys[16 * c:16 * c + 16] = y[SEG:SEG + 64].reshape(16, 4, D)
        ks[0, 16 * c:16 * c + 16] = r[c]["ks"].reshape(16, 128, 2, 64)
        vs[0, 16 * c:16 * c + 16] = r[c]["vs"].reshape(16, 128, 2, 64)
        rs[0, 16 * c:16 * c + 16] = r[c]["rs"]
        is_[0, 16 * c:16 * c + 16] = r[c]["is"]
        if seg == 3:
            kp[0, b] = r[c]["kp"].reshape(128, 2, 64)
            vp[0, b] = r[c]["vp"].reshape(128, 2, 64)
            rp[0, b] = r[c]["rp"]
            ip[0, b] = r[c]["ip"]
    return (yp, ys, kp, vp, rp, ip, ks, vs, rs, is_)
```

```python
import numpy as np
from contextlib import ExitStack
import concourse.bass as bass
import concourse.mybir as mybir
from concourse.bass_utils import run_bass_kernel_spmd

F32 = mybir.dt.float32
BF16 = mybir.dt.bfloat16
I32 = mybir.dt.int32
AF = mybir.ActivationFunctionType
ALU = mybir.AluOpType
AX = mybir.AxisListType

D = 1024
DFF = 2816
NF = DFF // 128
NT = 18
TOK = NT * 128
NPT = 16
SEG = 2048
HD = 64
PAST_LEN = 16384
EPS = 1e-6
NG = 32
NS = 64
CH = 16
LC = 16
NJ = SEG // LC
NSEQ = 16
USE_CC = False


class Buf:
    __slots__ = ("name", "w", "r", "ep")

    def __init__(self, name=""):
        self.name = name
        self.w = None
        self.r = []
        self.ep = 0


class Op:
    __slots__ = ("eng", "fn", "waits", "sig", "tok", "is_dma", "dsem", "dval", "idx", "stage", "inc")


class Prog:
    ENGS = ["pe", "act", "dve", "pool", "sp"]
    NDMASEM = 8
    ROLL = 20000

    def __init__(self, nc):
        self.nc = nc
        self.ops = {e: [] for e in self.ENGS}
        self.epoch = 0
        self.dma_expect = {}
        self.dma_rr = {e: 0 for e in self.ENGS}
        self.stage = ""
        self.name2stage = {}
        self.capture = None
        self.captured = []

    def _need(self, op, tok):
        if tok is None:
            return
        if tok[0] == "e":
            if op.eng == "pe" and tok[1] == "pe":
                return
            self.ops[tok[1]][tok[2]].sig = True
        op.waits.append(tok)

    def _touch(self, b):
        if b.ep != self.epoch:
            b.w, b.r, b.ep = None, [], self.epoch

    def _deps(self, op, reads, writes):
        for b in reads:
            self._touch(b)
            self._need(op, b.w)
        for b in writes:
            self._touch(b)
            self._need(op, b.w)
            for t in b.r:
                self._need(op, t)

    def _commit(self, tok, reads, writes):
        for b in reads:
            b.r.append(tok)
        for b in writes:
            b.w = tok
            b.r = []

    def replay(self, n):
        cap, self.capture = self.capture, None
        st = self.stage
        while n > 0 and self.captured:
            kind, stage, args, kw = self.captured.pop(0)
            self.stage = stage
            if kind == "op":
                self.op(*args, **kw)
            elif kind == "dma":
                self.dma(*args, **kw)
            else:
                self.barrier()
            n -= 1
        self.stage = st
        self.capture = cap

    def op(self, eng, fn, reads=(), writes=()):
        if self.capture:
            self.captured.append(("op", self.stage, (eng, fn), dict(reads=list(reads), writes=list(writes))))
            return None
        o = Op()
        o.eng, o.fn, o.waits, o.sig, o.is_dma = eng, fn, [], False, False
        o.idx = len(self.ops[eng])
        o.stage = self.stage
        o.tok = ("e", eng, o.idx)
        self._deps(o, reads, writes)
        self.ops[eng].append(o)
        self._commit(o.tok, reads, writes)
        return o

    def dma(self, q, fn, reads=(), writes=(), inc=16, semkey=None):
        if self.capture:
            self.captured.append(("dma", self.stage, (q, fn), dict(reads=list(reads), writes=list(writes), inc=inc,
                                                                    semkey=semkey)))
            return None
        o = Op()
        o.eng, o.fn, o.waits, o.sig, o.is_dma = q, fn, [], False, True
        o.idx = len(self.ops[q])
        o.stage = self.stage
        o.inc = inc
        if semkey is None:
            k = self.dma_rr[q] % self.NDMASEM
            self.dma_rr[q] += 1
        else:
            q, k = semkey
        prev = self.dma_expect.get((q, k), 0)
        if prev:
            o.waits.append(("d", q, k, prev))
        val = prev + inc
        self.dma_expect[(q, k)] = val
        o.dsem, o.dval = (q, k), val
        o.tok = ("d", q, k, val)
        self._deps(o, reads, writes)
        self.ops[o.eng].append(o)
        self._commit(o.tok, reads, writes)
        return o

    def barrier(self):
        if self.capture:
            self.captured.append(("bar", self.stage, (), {}))
            return
        toks = []
        for e in self.ENGS:
            for o in reversed(self.ops[e]):
                if not o.is_dma:
                    o.sig = True
                    toks.append(o.tok)
                    break
        for (q, k), v in self.dma_expect.items():
            toks.append(("d", q, k, v))
        for e in self.ENGS:
            o = Op()
            o.eng, o.fn, o.waits, o.sig, o.is_dma = e, None, list(toks), False, False
            o.idx = len(self.ops[e])
            o.tok = ("e", e, o.idx)
            self.ops[e].append(o)
        self.epoch += 1

    def emit(self, st):
        nc = self.nc
        sigval = {}
        nsig = {}
        for e in self.ENGS:
            n = 0
            for o in self.ops[e]:
                if o.sig and not o.is_dma and o.fn is not None:
                    n += 1
                sigval[(e, o.idx)] = n
            nsig[e] = n
        esem = {}
        for e in self.ENGS:
            ngen = nsig[e] // self.ROLL + 1
            esem[e] = [st.enter_context(nc.semaphore(f"s_{e}_{g}")) for g in range(ngen)]
        dsem = {}
        for (q, k) in self.dma_expect:
            dsem[(q, k)] = st.enter_context(nc.semaphore(f"d_{q}_{k}"))
        emap = {"pe": "tensor", "act": "scalar", "dve": "vector", "pool": "gpsimd", "sp": "sync"}
        block = st.enter_context(nc.Block())
        ROLL = self.ROLL

        def resolve(tok):
            if tok[0] == "e":
                v = sigval[(tok[1], tok[2])]
                g = (v - 1) // ROLL if v > 0 else 0
                return (("e", tok[1], g), esem[tok[1]][g], v - g * ROLL)
            return (("d", tok[1], tok[2]), dsem[(tok[1], tok[2])], tok[3])

        def run(e, eng):
            known = {}
            for o in self.ops[e]:
                need = {}
                for t in o.waits:
                    key, sem, v = resolve(t)
                    if v <= 0 or known.get(key, 0) >= v:
                        continue
                    if need.get(key, (None, 0))[1] < v:
                        need[key] = (sem, v)
                for key, (sem, v) in need.items():
                    eng.wait_ge(sem, v)
                    known[key] = v
                if o.fn is None:
                    continue
                ins = o.fn(eng)
                try:
                    self.name2stage[ins.ins.name] = o.stage
                except Exception:
                    pass
                if o.is_dma:
                    ins.then_inc(dsem[o.dsem], o.inc)
                elif o.sig:
                    v = sigval[(e, o.idx)]
                    g = (v - 1) // ROLL
                    ins.then_inc(esem[e][g], 1)

        for e in self.ENGS:
            getattr(block, emap[e])(lambda eng, e=e: run(e, eng))


class K:
    def mm(self, out, lhsT, rhs, start, stop, reads, writes):
        return self.P.op("pe", lambda e: e.matmul(out, lhsT=lhsT, rhs=rhs, start=start, stop=stop),
                         reads=reads, writes=writes)

    def tr(self, out, in_, ident, reads, writes):
        return self.P.op("pe", lambda e: e.transpose(out, in_, ident), reads=reads, writes=writes)

    def act(self, out, in_, func, reads, writes, **kw):
        return self.P.op("act", lambda e: e.activation(out=out, in_=in_, func=func, **kw),
                         reads=reads, writes=writes)

    def tt(self, eng, out, in0, in1, op, reads, writes):
        return self.P.op(eng, lambda e: e.tensor_tensor(out=out, in0=in0, in1=in1, op=op),
                         reads=reads, writes=writes)

    def stt(self, out, in0, scalar, in1, op0, op1, reads, writes):
        return self.P.op("dve", lambda e: e.scalar_tensor_tensor(out=out, in0=in0, scalar=scalar, in1=in1,
                                                                   op0=op0, op1=op1),
                         reads=reads, writes=writes)

    def ts(self, eng, out, in0, s1, s2, op0, op1, reads, writes):
        if s2 is None:
            return self.P.op(eng, lambda e: e.tensor_scalar(out=out, in0=in0, scalar1=s1, scalar2=None, op0=op0),
                             reads=reads, writes=writes)
        return self.P.op(eng, lambda e: e.tensor_scalar(out=out, in0=in0, scalar1=s1, scalar2=s2, op0=op0,
                                                        op1=op1), reads=reads, writes=writes)

    def cp(self, eng, out, in_, reads, writes):
        if eng == "act":
            return self.P.op("act", lambda e: e.copy(out=out, in_=in_), reads=reads, writes=writes)
        return self.P.op(eng, lambda e: e.tensor_copy(out=out, in_=in_), reads=reads, writes=writes)

    def recip(self, out, in_, reads, writes):
        return self.P.op("dve", lambda e: e.reciprocal(out=out, in_=in_), reads=reads, writes=writes)

    def memset(self, eng, ap, val, writes):
        return self.P.op(eng, lambda e: e.memset(ap, val), writes=writes)

    def dma(self, q, out, in_, reads=(), writes=()):
        return self.P.dma(q, lambda e: e.dma_start(out=out, in_=in_), reads=reads, writes=writes)


def build_program(debug=(), stop=None, skip_ffn=False):
    nc = bass.Bass("TRN2", target_bir_lowering=False)
    P = Prog(nc)
    _CACHE["P"] = P
    k = K()
    k.nc, k.P = nc, P
    k.debug = set(debug)
    k.stop = stop
    din = {}
    dout = {}

    def inp(name, shape, dt=F32):
        din[name] = nc.dram_tensor(name, list(shape), dt, kind="ExternalInput").ap()
        return din[name]

    def outp(name, shape, dt=F32):
        dout[name] = nc.dram_tensor(name, list(shape), dt, kind="ExternalOutput").ap()
        return dout[name]

    k.inp, k.outp, k.din, k.dout = inp, outp, din, dout

    inp("x", [TOK, D])
    inp("ffn1_g", [128, 8]); inp("mix_g", [128, 8]); inp("ffn2_g", [128, 8])
    inp("ffn1_w1", [D, DFF]); inp("ffn1_w3", [D, DFF]); inp("ffn1_w2", [DFF, D])
    inp("ffn2_w1", [D, DFF]); inp("ffn2_w3", [D, DFF]); inp("ffn2_w2", [DFF, D])
    inp("identb", [128, 128])
    inp("w_qk", [D, 640]); inp("w_in", [D, 3328])
    inp("gq2", [128, 1]); inp("gk2", [128, 1]); inp("invf", [128, 1]); inp("pos", [1, TOK])
    inp("rotm", [128, 128]); inp("onesbd", [128, 128])
    inp("mdiag", [128, 128]); inp("mprev", [128, 128]); inp("mprev1", [128, 128]); inp("smask", [64, 64])
    inp("sinks", [1, 8])
    inp("cache_k", [NSEQ, 128, 128]); inp("cache_v", [NSEQ, 128, 128])
    if USE_CC:
        inp("selb", [64, 8]); inp("wsel", [64, 24])
    else:
        inp("xpre", [3, SEG, D])
    inp("identf", [128, 128])
    inp("a_reT", [64, NG]); inp("a_imT", [64, NG]); inp("logdt_b", [64, NG])
    inp("bT_re", [64, NG, CH]); inp("bT_im", [64, NG, CH]); inp("cT_re", [64, NG, CH]); inp("cT_im", [64, NG, CH])
    inp("dcol", [128, NG]); inp("tmask", [128, 2, 256]); inp("dmask", [128, 2, 256])
    inp("x0", [64, 2, NG, NSEQ])
    inp("w_glu", [512, 512]); inp("b_glu", [128, 4]); inp("w_ao", [512, D]); inp("w_ssm_out", [512, D])
    inp("w_out", [D, D]); inp("wm5", [8, 128, 3072])
    outp("y", [17 * 128, D])
    outp("kp", [128, 128]); outp("vp", [128, 128]); outp("rp", [NG, NS]); outp("ip", [NG, NS])
    outp("ks", [NSEQ, 128, 128]); outp("vs", [NSEQ, 128, 128])
    outp("rs", [NSEQ, NG, NS]); outp("is", [NSEQ, NG, NS])

    with ExitStack() as st:
        k.st = st

        uid = [0]

        def sb(name, shape, dt, scope=None):
            uid[0] += 1
            return (scope or st).enter_context(nc.sbuf_tensor(f"{name}_{uid[0]}", list(shape), dt))

        def ps(name, shape, dt, scope=None):
            uid[0] += 1
            return (scope or st).enter_context(nc.psum_tensor(f"{name}_{uid[0]}", list(shape), dt))

        k.sb, k.ps = sb, ps

        k.X = sb("X", [128, NT, D], F32)
        k.BX = [[Buf(f"X{t}_{h}") for h in range(2)] for t in range(NT)]
        k.identb = sb("identb_s", [128, 128], BF16)
        k.Bident = Buf("ident")
        k.dma("pool", k.identb[:], din["identb"], writes=[k.Bident])
        k.gvec = {}
        for nm in ("ffn1_g", "mix_g", "ffn2_g"):
            t = sb(nm + "_s", [128, 8], F32)
            b = Buf(nm)
            k.dma("sp", t[:], din[nm], writes=[b])
            k.gvec[nm] = (t, b)

        k.identf = sb("identf_s", [128, 128], F32)
        k.Bidentf = Buf("identf")
        k.dma("sp", k.identf[:], din["identf"], writes=[k.Bidentf])
        k.YC = sb("YC", [64, 2, NG], F32)
        k.BYC = Buf("YC")
        k.memset("dve", k.YC[:], 0.0, [k.BYC])
        P.stage = "setup"
        ssm_setup_alloc(k)
        setup_scope = ExitStack()
        ssm_setup(k, setup_scope)
        setup_scope.close()
        setup_scope = None
        n_cap = 0
        import os as _os
        nsb = 1 if (USE_CC or stop is not None) else 4
        for sbk in range(4 - nsb, 4):
            own = sbk == 3
            if own:
                xin = din["x"].rearrange("(t p) d -> p t d", p=128)
                tiles = list(range(NT))
                for t in tiles:
                    k.dma("sp" if t % 2 == 0 else "act", k.X[:, t, :], xin[:, t, :], writes=k.BX[t])
            else:
                xin = din["xpre"][sbk].rearrange("(t p) d -> p t d", p=128)
                tiles = list(range(1, 17))
                for t in tiles:
                    k.dma("sp" if t % 2 == 0 else "act", k.X[:, t, :], xin[:, t - 1, :], writes=k.BX[t])
            P.stage = f"sb{sbk}.ffn1"
            if not skip_ffn:
                if P.captured:
                    nblk = ((NF + 1) // 2) * ((len(tiles) + 1) // 2)
                    ffn(k, "ffn1", tiles, G=2, replay=n_cap // ((nblk - 12) * 6) + 1)
                else:
                    ffn(k, "ffn1", tiles)
            else:
                P.replay(1 << 30)
            if setup_scope is not None:
                setup_scope.close()
                setup_scope = None
            if not own:
                P.stage = f"sb{sbk}.pmix"
                mixer_prefix(k)
                continue
            if "x1" in k.debug:
                o = outp("dbg_x1", [TOK, D])
                ov = o.rearrange("(t p) d -> p t d", p=128)
                for t in range(NT):
                    k.dma("sp", ov[:, t, :], k.X[:, t, :], reads=k.BX[t])
            mixer(k)
            P.stage = "ffn2"
            if stop is None and not skip_ffn:
                ffn(k, "ffn2", list(range(1, NT)))
            P.stage = "store"
        yv = dout["y"].rearrange("(t p) d -> p t d", p=128)
        for t in range(1, NT):
            q = "sp" if t % 2 == 0 else "act"
            k.dma(q, yv[:, t - 1, :], k.X[:, t, :], reads=k.BX[t])
        P.barrier()
        P.emit(st)
    return nc


def norm_transpose(k, gname, tiles, xnT, BxnT, scope, tp, Btp, t0=0):
    P = k.P
    g, Bg = k.gvec[gname]
    ss = k.sb("nt_ss_" + gname, [128, NT], F32, scope)
    rs = k.sb("nt_rs_" + gname, [128, NT], F32, scope)
    junk = [k.sb(f"nt_junk{i}_" + gname, [128, D], BF16, scope) for i in range(2)]
    xs = [k.sb(f"nt_xs{i}_" + gname, [128, D], BF16, scope) for i in range(2)]
    Bjunk = [Buf() for _ in range(2)]
    Bxs = [Buf() for _ in range(2)]
    Bss = [Buf() for _ in range(NT)]
    for n, t in enumerate(tiles):
        s = n % 2
        k.act(junk[s][:], k.X[:, t, :], AF.Square, k.BX[t], [Bjunk[s], Bss[t]], accum_out=ss[:, t:t + 1])
        k.act(rs[:, t:t + 1], ss[:, t:t + 1], AF.Sqrt, [Bss[t]], [Bss[t]], scale=1.0 / D, bias=EPS)
        k.recip(rs[:, t:t + 1], rs[:, t:t + 1], [Bss[t]], [Bss[t]])
        k.act(xs[s][:], k.X[:, t, :], AF.Copy, k.BX[t] + [Bss[t]], [Bxs[s]], scale=rs[:, t:t + 1])
        for kk in range(8):
            k.tr(tp[:, kk, :], xs[s][:, kk * 128:(kk + 1) * 128], k.identb[:], [Bxs[s], k.Bident], [Btp])
        k.tt("dve", xnT[:, :, (t - t0) * 128:(t - t0 + 1) * 128], tp[:],
             g[:, :].unsqueeze(2).to_broadcast([128, 8, 128]), ALU.mult, [Btp, Bg], [BxnT[t]])


def ffn(k, name, tiles, G=4, replay=0):
    nc, P = k.nc, k.P
    w1 = k.din[name + "_w1"].rearrange("(kk p) f -> p kk f", p=128)
    w3 = k.din[name + "_w3"].rearrange("(kk p) f -> p kk f", p=128)
    w2 = k.din[name + "_w2"].rearrange("(f p) d -> p f d", p=128)
    with ExitStack() as sc:
        xnT = k.sb(name + "_xnT", [128, 8, TOK], BF16, sc)
        BxnT = [Buf() for _ in range(NT)]
        with ExitStack() as stp:
            tp = k.ps(name + "_tp", [128, 8, 128], BF16, stp)
            Btp = Buf()
            norm_transpose(k, name + "_g", tiles, xnT, BxnT, sc, tp, Btp)
            P.barrier()
        W1 = k.sb(name + "_W1", [128, 2, 8, G * 128], BF16, sc)
        W3 = k.sb(name + "_W3", [128, 2, 8, G * 128], BF16, sc)
        W2 = k.sb(name + "_W2", [128, 2, G, D], BF16, sc)
        BW = [Buf() for _ in range(2)]
        AB = [k.ps(name + f"_AB{i}", [128, 2, 256], F32, sc) for i in range(2)]
        BAB = [Buf() for _ in range(2)]
        OUT = [[k.ps(name + f"_O{t}{h}", [128, 512], F32, sc) for h in range(2)] for t in range(2)]
        BOUT = [[Buf() for h in range(2)] for t in range(2)]
        SA = [k.sb(name + f"_sa{i}", [128, 256], F32, sc) for i in range(2)]
        BSA = [Buf() for _ in range(2)]
        H = k.sb(name + "_H", [128, 2, G, 256], BF16, sc)
        BH = [[Buf() for _ in range(G)] for _ in range(2)]
        groups = []
        f0 = 0
        while f0 < NF:
            groups.append((f0, min(G, NF - f0)))
            f0 += G
        blocks = [tiles[i:i + 2] for i in range(0, len(tiles), 2)]
        abn = 0
        hsn = 0
        for gi, (f0, gn) in enumerate(groups):
            slot = gi % 2
            c0, c1 = f0 * 128, (f0 + gn) * 128
            k.dma("pool", W1[:, slot, :, 0:c1 - c0], w1[:, :, c0:c1], writes=[BW[slot]])
            k.dma("pool", W3[:, slot, :, 0:c1 - c0], w3[:, :, c0:c1], writes=[BW[slot]])
            k.dma("pool", W2[:, slot, 0:gn, :], w2[:, f0:f0 + gn, :], writes=[BW[slot]])
            for blk in blocks:
                nb = 128 * len(blk)
                col0 = blk[0] * 128
                assert all(blk[i] == blk[0] + i for i in range(len(blk)))
                hs = hsn % 2
                hsn += 1
                rd_x = [BxnT[t] for t in blk]

                def emit_ab(fi):
                    nonlocal abn
                    a = abn % 2
                    abn += 1
                    for wi, W in enumerate((W1, W3)):
                        for kk in range(8):
                            k.mm(AB[a][:, wi, 0:nb], W[:, slot, kk, fi * 128:(fi + 1) * 128],
                                 xnT[:, kk, col0:col0 + nb], kk == 0, kk == 7,
                                 [BW[slot]] + rd_x, [BAB[a]])
                    k.act(SA[a][:, 0:nb], AB[a][:, 0, 0:nb], AF.Silu, [BAB[a]], [BSA[a]])
                    k.tt("dve", H[:, hs, fi, 0:nb], AB[a][:, 1, 0:nb], SA[a][:, 0:nb], ALU.mult,
                         [BAB[a], BSA[a]], [BH[hs][fi]])

                def emit_w2(fi):
                    for tl in range(len(blk)):
                        for hh in range(2):
                            k.mm(OUT[tl][hh][:], H[:, hs, fi, tl * 128:(tl + 1) * 128],
                                 W2[:, slot, fi, hh * 512:(hh + 1) * 512], fi == 0, fi == gn - 1,
                                 [BW[slot], BH[hs][fi]], [BOUT[tl][hh]])

                emit_ab(0)
                if replay:
                    P.replay(replay)
                for fi in range(1, gn):
                    emit_ab(fi)
                    if replay:
                        P.replay(replay)
                    emit_w2(fi - 1)
                emit_w2(gn - 1)
                for tl, t in enumerate(blk):
                    for hh in range(2):
                        xs_ = k.X[:, t, hh * 512:(hh + 1) * 512]
                        k.stt(xs_, OUT[tl][hh][:], 0.5, xs_, ALU.mult, ALU.add,
                              [BOUT[tl][hh], k.BX[t][hh]], [k.BX[t][hh]])
                        if replay:
                            P.replay(replay)
        if replay:
            P.replay(1 << 30)
        P.barrier()


TWO_PI = float(2 * np.pi)
CW1 = 6.28125
CW2 = float(2 * np.pi - 6.28125)


def sin_table(k, out, ang, shift, tmp, ki, rr, Bout, Bang, Btmp, Bki, Brr):
    k.ts("dve", tmp, ang, shift, 1.0 / TWO_PI, ALU.add, ALU.mult, [Bang], [Btmp])
    k.cp("dve", ki, tmp, [Btmp], [Bki])
    k.cp("dve", tmp, ki, [Bki], [Btmp])
    k.stt(rr, tmp, -CW1, ang, ALU.mult, ALU.add, [Btmp, Bang], [Brr])
    k.stt(rr, tmp, -CW2, rr, ALU.mult, ALU.add, [Btmp, Brr], [Brr])
    k.ts("dve", rr, rr, shift, 3.1415925, ALU.add, ALU.min, [Brr], [Brr])
    k.ts("dve", rr, rr, -3.1415925, None, ALU.max, None, [Brr], [Brr])
    k.act(out, rr, AF.Sin, [Brr], [Bout])


def setup_rope(k, scope):
    P = k.P
    k.cosT = k.sb("cosT", [128, TOK], BF16, scope)
    k.sinT = k.sb("sinT", [128, TOK], BF16, scope)
    k.Brope = Buf()
    with ExitStack() as sc:
        invf = k.sb("invf_s", [128, 1], F32, sc)
        Binv = Buf()
        k.dma("sp", invf[:], k.din["invf"], writes=[Binv])
        CHK = 768
        posb = k.sb("posb", [128, CHK], F32, sc)
        ang = k.sb("ang", [128, CHK], F32, sc)
        tmp = k.sb("rtmp", [128, CHK], F32, sc)
        ki = k.sb("rki", [128, CHK], I32, sc)
        rr = k.sb("rrr", [128, CHK], F32, sc)
        Bpos, Bang, Btmp, Bki, Brr = Buf(), Buf(), Buf(), Buf(), Buf()
        import os
        dbg = int(os.environ.get("ROPE_DBG", "0"))
        for c0 in range(0, TOK, CHK):
            if dbg == 1 and c0 > 0:
                break
            k.dma("sp", posb[:], k.din["pos"][0:1, c0:c0 + CHK].to_broadcast([128, CHK]), writes=[Bpos])
            k.ts("dve", ang[:], posb[:], invf[:, 0:1], None, ALU.mult, None, [Bpos, Binv], [Bang])
            sin_table(k, k.sinT[:, c0:c0 + CHK], ang[:], 0.0, tmp[:], ki[:], rr[:], k.Brope, Bang, Btmp, Bki, Brr)
            if dbg == 2:
                continue
            sin_table(k, k.cosT[:, c0:c0 + CHK], ang[:], float(np.pi / 2), tmp[:], ki[:], rr[:], k.Brope, Bang, Btmp,
                      Bki, Brr)
        P.barrier()


def mixer(k):
    nc, P, din = k.nc, k.P, k.din
    with ExitStack() as mx:
        k.XS = k.sb("XS", [64, 2, NG, NSEQ], F32, mx)
        k.BXS = Buf("XS")
        k.selb = k.sb("selb", [64, 8], F32, mx)
        k.wsel = k.sb("wsel", [64, 24], F32, mx)
        k.gsel = k.sb("gsel", [64, 8, 64], F32, mx)
        k.Gx = k.sb("Gx", [64, 8, 64], F32, mx)
        NQ = 4 * TOK
        AR = k.sb("arena", [128, NQ + 2 * NQ + 9728], BF16, mx)
        k.AR = AR
        QA = AR[:, 0:NQ].rearrange("p (a t) -> p a t", a=4)
        BQA = [[Buf() for _ in range(2)] for _ in range(NT)]
        k.QA, k.BQA = QA, BQA
        k.hnT_ar = AR[:, NQ:3 * NQ].rearrange("p (a t) -> p a t", a=8)
        k.zT = AR[:, NQ:2 * NQ].rearrange("p (a t) -> p a t", a=4)
        k.ssmT = AR[:, 2 * NQ:3 * NQ].rearrange("p (a t) -> p a t", a=4)
        k.UT = AR[:, 3 * NQ:3 * NQ + 8192].rearrange("p (g s j) -> p g s j", g=NG, s=2)
        k.UTs = AR[0:64, 3 * NQ + 8192:3 * NQ + 8704].rearrange("p (g q) -> p g q", q=NSEQ)
        k.mTh = AR[:, 3 * NQ:3 * NQ + 9216].rearrange("p (a t) -> p a t", a=8)
        k.BUT, k.BUTs, k.BzT = Buf(), Buf(), Buf()
        with ExitStack() as sa:
            KT = k.sb("KT", [128, TOK], BF16, sa)
            BKT = [Buf() for _ in range(NT)]
            KTf = k.sb("KTf", [128, 256], F32, sa)
            BKTf = Buf()
            V = k.sb("V", [128, NT, 128], BF16, sa)
            BV = [Buf() for _ in range(NT)]
            Vf = k.sb("Vf", [128, 2, 128], F32, sa)
            BVf = Buf()
            consts = {}
            for nm, shp, dt in (("rotm", [128, 128], BF16), ("onesbd", [128, 128], BF16),
                                ("mdiag", [128, 128], BF16), ("mprev", [128, 128], BF16),
                                ("mprev1", [128, 128], BF16), ("gq2", [128, 1], F32), ("gk2", [128, 1], F32)):
                t = k.sb(nm + "_s", shp, dt, sa)
                bb = Buf()
                k.dma("pool", t[:], din[nm], writes=[bb])
                consts[nm] = (t, bb)
            smask = k.sb("smask_s", [64, 64], BF16, sa)
            Bsm = Buf()
            k.dma("pool", smask[:], din["smask"], writes=[Bsm])
            sinkb = k.sb("sinkb", [128, 8], F32, sa)
            Bsink = Buf()
            k.dma("sp", sinkb[:], din["sinks"][0:1, :].to_broadcast([128, 8]), writes=[Bsink])
            k.act(sinkb[:], sinkb[:], AF.Exp, [Bsink], [Bsink])
            ones = k.sb("ones_s", [128, 128], BF16, sa)
            Bones = Buf()
            k.memset("pool", ones[:], 1.0, [Bones])

            with ExitStack() as shn:
                hnT = k.hnT_ar
                BhnT = [Buf() for _ in range(NT)]
                setup_rope(k, shn)
                W = k.sb("Wmix", [128, 8, 640], BF16, shn)
                BW = Buf()
                P.stage = "m2a"
                with ExitStack() as s1:
                    tp = k.ps("mx_tp", [128, 8, 128], BF16, s1)
                    norm_transpose(k, "mix_g", list(range(NT)), hnT, BhnT, s1, tp, Buf())
                    P.barrier()
                if k.stop == "m2a":
                    return
                P.stage = "m2b"
                k.dma("pool", W[:], din["w_qk"].rearrange("(kk p) c -> p kk c", p=128), writes=[BW])
                with ExitStack() as s2:
                    NB = 512
                    pq = [k.ps(f"pq{i}", [128, NB], F32, s2) for i in range(2)]
                    pss = [k.ps(f"pss{i}", [128, NB], F32, s2) for i in range(2)]
                    prot = [k.ps(f"prot{i}", [128, NB], F32, s2) for i in range(2)]
                    Bpq, Bpss, Bprot = [Buf(), Buf()], [Buf(), Buf()], [Buf(), Buf()]
                    sq = [k.sb(f"sq{i}", [128, NB], BF16, s2) for i in range(2)]
                    rstd = [k.sb(f"rstd{i}", [128, NB], F32, s2) for i in range(2)]
                    xnb = [k.sb(f"xnb{i}", [128, NB], BF16, s2) for i in range(2)]
                    t1 = [k.sb(f"t1_{i}", [128, NB], F32, s2) for i in range(2)]
                    t2_0 = k.sb("t2_0", [128, NB], F32, s2)
                    t2 = [t2_0, t2_0]
                    Bsq, Brstd, Bxnb, Bt1 = ([Buf(), Buf()] for _ in range(4))
                    Bt2_0 = Buf()
                    Bt2 = [Bt2_0, Bt2_0]
                    rotm, Brot = consts["rotm"]
                    onesbd, Bobd = consts["onesbd"]
                    it = 0
                    for ct in range(5):
                        isk = ct == 4
                        g2, Bg2 = consts["gk2" if isk else "gq2"]
                        for c0 in range(0, TOK, NB):
                            nb = min(NB, TOK - c0)
                            tl = list(range(c0 // 128, (c0 + nb) // 128))
                            i = it % 2
                            it += 1
                            for kk in range(8):
                                k.mm(pq[i][:, 0:nb], W[:, kk, ct * 128:(ct + 1) * 128], hnT[:, kk, c0:c0 + nb],
                                     kk == 0, kk == 7, [BW] + [BhnT[t] for t in tl], [Bpq[i]])
                            k.act(sq[i][:, 0:nb], pq[i][:, 0:nb], AF.Square, [Bpq[i]], [Bsq[i]])
                            k.mm(pss[i][:, 0:nb], onesbd[:], sq[i][:, 0:nb], True, True, [Bobd, Bsq[i]], [Bpss[i]])
                            if isk:
                                k.act(rstd[i][:, 0:nb], pss[i][:, 0:nb], AF.Sqrt, [Bpss[i]], [Brstd[i]],
                                      scale=1.0 / HD, bias=EPS)
                            else:
                                k.act(rstd[i][:, 0:nb], pss[i][:, 0:nb], AF.Sqrt, [Bpss[i]], [Brstd[i]],
                                      scale=1.0, bias=HD * EPS)
                            k.recip(rstd[i][:, 0:nb], rstd[i][:, 0:nb], [Brstd[i]], [Brstd[i]])
                            k.stt(xnb[i][:, 0:nb], pq[i][:, 0:nb], g2[:, 0:1], rstd[i][:, 0:nb], ALU.mult, ALU.mult,
                                  [Bpq[i], Bg2, Brstd[i]], [Bxnb[i]])
                            k.mm(prot[i][:, 0:nb], rotm[:], xnb[i][:, 0:nb], True, True, [Brot, Bxnb[i]], [Bprot[i]])
                            k.tt("pool", t1[i][:, 0:nb], xnb[i][:, 0:nb], k.cosT[:, c0:c0 + nb], ALU.mult,
                                 [Bxnb[i], k.Brope], [Bt1[i]])
                            k.tt("dve", t2[i][:, 0:nb], prot[i][:, 0:nb], k.sinT[:, c0:c0 + nb], ALU.mult,
                                 [Bprot[i], k.Brope], [Bt2[i]])
                            if isk:
                                k.tt("pool", KT[:, c0:c0 + nb], t1[i][:, 0:nb], t2[i][:, 0:nb], ALU.add,
                                     [Bt1[i], Bt2[i]], [BKT[t] for t in tl])
                                if c0 + nb == TOK:
                                    k.tt("pool", KTf[:, :], t1[i][:, nb - 256:nb], t2[i][:, nb - 256:nb], ALU.add,
                                         [Bt1[i], Bt2[i]], [BKTf])
                            else:
                                k.tt("pool", QA[:, ct, c0:c0 + nb], t1[i][:, 0:nb], t2[i][:, 0:nb], ALU.add,
                                     [Bt1[i], Bt2[i]], [BQA[t][h] for t in tl for h in range(2)])
                    P.barrier()
                if k.stop == "m2b":
                    return
                P.stage = "m2c"
                k.dma("pool", W[:], din["w_in"].rearrange("(kk p) c -> p kk c", p=128)[:, :, 640:1280], writes=[BW])
                with ExitStack() as s3:
                    pv = [k.ps(f"pv{i}", [128, 512], F32, s3) for i in range(2)]
                    Bpv = [Buf(), Buf()]
                    for t in range(NT):
                        i = t % 2
                        for kk in range(8):
                            k.mm(pv[i][:, 0:128], hnT[:, kk, t * 128:(t + 1) * 128], W[:, kk, 0:128], kk == 0, kk == 7,
                                 [BW, BhnT[t]], [Bpv[i]])
                        k.cp("act", V[:, t, :], pv[i][:, 0:128], [Bpv[i]], [BV[t]])
                        if t >= 16:
                            k.cp("dve", Vf[:, t - 16, :], pv[i][:, 0:128], [Bpv[i], BV[t]], [BVf])
                    ssm_project(k, hnT, BhnT, W, BW, s3, True)
                    P.barrier()
                if "qk" in k.debug:
                    o = k.outp("dbg_qa", [128, 4 * TOK], BF16)
                    k.dma("sp", o, k.AR[:, 0:4 * TOK], reads=[b for r in BQA for b in r])
                    o = k.outp("dbg_kt", [128, TOK], BF16)
                    k.dma("sp", o, KT[:], reads=BKT)
                    o = k.outp("dbg_v", [128, NT * 128], BF16)
                    k.dma("sp", o, V[:].rearrange("p a t -> p (a t)"), reads=BV)
                    P.barrier()
                if k.stop == "m2c":
                    return
            if USE_CC:
                P.stage = "m4a"
                ssm_scan(k, False)
                exchange_start(k)
            P.stage = "m3"
            attention(k, KT, BKT, V, BV, consts, smask, Bsm, sinkb, Bsink, ones, Bones)
            outputs_kv(k, KTf, BKTf, Vf, BVf)
            P.barrier()
        if "attn" in k.debug:
            o = k.outp("dbg_attn", [128, 4 * TOK], BF16)
            k.dma("sp", o, k.AR[:, 0:4 * TOK], reads=[b for r in BQA for b in r])
            P.barrier()
        if k.stop == "att":
            return
        P.stage = "m4"
        if USE_CC:
            exchange_finish(k)
        ssm_scan(k, True)
        outputs_state(k)
        if "zt" in k.debug:
            o = k.outp("dbg_zt", [128, 4 * TOK], BF16)
            k.dma("sp", o, k.AR[:, 4 * TOK:8 * TOK], reads=[k.BzT])
            P.barrier()
        if k.stop == "ssm":
            return
        mixer_tail(k)


def exchange_start(k):
    nc, P, din = k.nc, k.P, k.din
    k.Bxc = Buf("xchg")
    k.dma("sp", k.selb[:], din["selb"], writes=[k.Bxc])
    k.dma("sp", k.wsel[:], din["wsel"], writes=[k.Bxc])
    ycf = k.YC[:].rearrange("p c g -> p (c g)")
    for s_ in range(8):
        k.ts("dve", k.gsel[:, s_, :], ycf, k.selb[:, s_:s_ + 1], None, ALU.mult, None, [k.BYC, k.Bxc], [k.Bxc])
    bin_d = nc.dram_tensor("xc_in", [512, 64], F32).ap()
    gat_d = nc.dram_tensor("xc_out", [512, 64], F32).ap()
    Bbin, Bgat = Buf(), Buf()
    k.dma("sp", bin_d.rearrange("(s n) f -> n s f", n=64), k.gsel[:], reads=[k.Bxc], writes=[Bbin])
    P.barrier()
    P.dma("pool", lambda e: e.collective_compute("AllReduce", ALU.add, replica_groups=[list(range(8))],
                                                   ins=[bin_d.opt()], outs=[gat_d.opt()]),
          reads=[Bbin], writes=[Bgat], inc=1, semkey=("cc", 0))
    P.barrier()
    k.dma("sp", k.Gx[:], gat_d.rearrange("(s n) f -> n s f", n=64), reads=[Bgat], writes=[k.Bxc])


def exchange_finish(k):
    P = k.P
    with ExitStack() as sc:
        Hs = k.sb("xc_H", [64, 3, 64], F32, sc)
        t1, t2 = k.sb("xc_t1", [64, NG], F32, sc), k.sb("xc_t2", [64, NG], F32, sc)
        pr = k.sb("xc_pr", [64, 2, NG], F32, sc)
        BH, Bt = Buf(), Buf()
        k.memset("dve", Hs[:], 0.0, [BH])
        for d in range(3):
            for s_ in range(8):
                k.stt(Hs[:, d, :], k.Gx[:, s_, :], k.wsel[:, d * 8 + s_:d * 8 + s_ + 1], Hs[:, d, :], ALU.mult, ALU.add,
                      [k.Bxc, BH], [BH])
        hv = lambda d, c: Hs[:, d, c * NG:(c + 1) * NG]
        k.cp("dve", k.YC[:].rearrange("p c g -> p (c g)"), Hs[:, 0, :], [BH, k.BYC], [k.BYC])
        for d in (1, 2):
            cmul(k, "dve", pr[:, 0], pr[:, 1], k.APX[:, d - 1, 0, :], k.APX[:, d - 1, 1, :], hv(d, 0), hv(d, 1),
                 t1[:], t2[:], [k.Bsm, BH], [BH], Bt)
            k.tt("dve", k.YC[:], k.YC[:], pr[:], ALU.add, [BH, k.BYC], [k.BYC])
        P.barrier()


def mixer_prefix(k):
    P, din = k.P, k.din
    with ExitStack() as mx:
        k.UT = k.sb("UTp", [128, NG, 2, NJ], BF16, mx)
        k.BUT = Buf()
        with ExitStack() as sa:
            hnT = k.sb("hnTp", [128, 8, TOK], BF16, sa)
            BhnT = [Buf() for _ in range(NT)]
            W = k.sb("Wp", [128, 8, 640], BF16, sa)
            BW = Buf()
            k.dma("pool", W[:], din["w_in"].rearrange("(kk p) c -> p kk c", p=128)[:, :, 640:1280], writes=[BW])
            with ExitStack() as s1:
                tp = k.ps("mp_tp", [128, 8, 128], BF16, s1)
                norm_transpose(k, "mix_g", list(range(1, 17)), hnT, BhnT, s1, tp, Buf())
                P.barrier()
            with ExitStack() as s3:
                ssm_project(k, hnT, BhnT, W, BW, s3, False)
                P.barrier()
        ssm_scan(k, False)


def cmul(k, eng, outr, outi, ar, ai, br, bi, t1, t2, reads, writes, Bt, neg_i=False):
    rd = list(reads)
    k.tt(eng, t1, ar, br, ALU.mult, rd, [Bt])
    k.tt(eng, t2, ai, bi, ALU.mult, rd + [Bt], [Bt])
    k.tt(eng, outr, t1, t2, ALU.subtract, [Bt], list(writes))
    k.tt(eng, t1, ar, bi, ALU.mult, rd + [Bt] + list(writes), [Bt])
    k.tt(eng, t2, ai, br, ALU.mult, rd + [Bt], [Bt])
    if neg_i:
        k.ts(eng, t1, t1, -1.0, None, ALU.mult, None, [Bt], [Bt])
        k.tt(eng, outi, t1, t2, ALU.subtract, [Bt], list(writes))
    else:
        k.tt(eng, outi, t1, t2, ALU.add, [Bt], list(writes))


def ssm_setup_alloc(k):
    nc = k.nc
    k.APW = k.sb("APW", [64, 2, NG, 17], F32)
    k.RHO = k.sb("RHO16", [64, NG], F32)
    k.RP = k.sb("RP", [64, 7, 2, NG], F32)
    k.E127 = k.sb("E127", [64, 2, NG], F32)
    k.APX = k.sb("APX", [64, 2, 2, NG], F32)
    k.Bsm = Buf("ssm_small")
    k.T_d = nc.dram_tensor("T_d", [128, NG, 2, 256], BF16).ap()
    k.WS_d = nc.dram_tensor("WS_d", [128, NG, 2, 128], BF16).ap()
    k.Q_d = nc.dram_tensor("Q_d", [64, 2, NG, 256], BF16).ap()
    k.E_d = nc.dram_tensor("E_d", [64, 2, NG, NJ], F32).ap()
    k.Btab = Buf("tables_dram")


def ssm_setup(k, sc):
    nc, P, din = k.nc, k.P, k.din
    APW, RHO, RP, B1 = k.APW, k.RHO, k.RP, k.Bsm
    if True:
        def T(name, shape, dt=F32):
            return k.sb("su_" + name, shape, dt, sc)
        are, aim, ldt = T("are", [64, NG]), T("aim", [64, NG]), T("ldt", [64, NG])
        k.dma("sp", are[:], din["a_reT"], writes=[B1])
        k.dma("sp", aim[:], din["a_imT"], writes=[B1])
        k.dma("sp", ldt[:], din["logdt_b"], writes=[B1])
        Br, Bi = T("Br", [64, NG, CH]), T("Bi", [64, NG, CH])
        Cr, Ci = T("Cr", [64, NG, CH]), T("Ci", [64, NG, CH])
        for t, nm in ((Br, "bT_re"), (Bi, "bT_im"), (Cr, "cT_re"), (Ci, "cT_im")):
            k.dma("act", t[:], din[nm], writes=[B1])
        dcol = T("dcol", [128, NG])
        k.dma("sp", dcol[:], din["dcol"], writes=[B1])
        tmask = T("tmask", [128, 2, 256], BF16)
        dmask = T("dmask", [128, 2, 256], BF16)
        k.dma("pool", tmask[:], din["tmask"], writes=[B1])
        k.dma("pool", dmask[:], din["dmask"], writes=[B1])
        dt, dar, dai, mag = T("dt", [64, NG]), T("dar", [64, NG]), T("dai", [64, NG]), T("mag", [64, NG])
        sn, cs = T("sn", [64, NG]), T("cs", [64, NG])
        tmp, rr = T("tmp", [64, NG]), T("rr", [64, NG])
        ki = T("ki", [64, NG], I32)
        k.act(dt[:], ldt[:], AF.Exp, [B1], [B1])
        k.tt("dve", dar[:], dt[:], are[:], ALU.mult, [B1], [B1])
        k.tt("dve", dai[:], dt[:], aim[:], ALU.mult, [B1], [B1])
        k.act(mag[:], dar[:], AF.Exp, [B1], [B1])
        k.act(RHO[:], dar[:], AF.Exp, [B1], [B1], scale=float(LC))
        sin_table(k, sn[:], dai[:], 0.0, tmp[:], ki[:], rr[:], B1, B1, B1, B1, B1)
        sin_table(k, cs[:], dai[:], float(np.pi / 2), tmp[:], ki[:], rr[:], B1, B1, B1, B1, B1)
        A_r, A_i = APW[:, 0, :, 1], APW[:, 1, :, 1]
        k.memset("dve", APW[:, 0, :, 0], 1.0, [B1])
        k.memset("dve", APW[:, 1, :, 0], 0.0, [B1])
        k.tt("dve", A_r, mag[:], cs[:], ALU.mult, [B1], [B1])
        k.tt("dve", A_i, mag[:], sn[:], ALU.mult, [B1], [B1])
        t1, t2 = T("t1", [64, NG]), T("t2", [64, NG])
        for p in range(2, 17):
            cmul(k, "dve", APW[:, 0, :, p], APW[:, 1, :, p], APW[:, 0, :, p - 1], APW[:, 1, :, p - 1], A_r, A_i,
                 t1[:], t2[:], [B1], [B1], B1)
        den, nr, fre, fim = T("den", [64, NG]), T("nr", [64, NG]), T("fre", [64, NG]), T("fim", [64, NG])
        k.tt("dve", den[:], are[:], are[:], ALU.mult, [B1], [B1])
        k.tt("dve", t1[:], aim[:], aim[:], ALU.mult, [B1], [B1])
        k.tt("dve", den[:], den[:], t1[:], ALU.add, [B1], [B1])
        k.recip(den[:], den[:], [B1], [B1])
        k.ts("dve", nr[:], A_r, -1.0, None, ALU.add, None, [B1], [B1])
        k.tt("dve", t1[:], nr[:], are[:], ALU.mult, [B1], [B1])
        k.tt("dve", t2[:], A_i, aim[:], ALU.mult, [B1], [B1])
        k.tt("dve", t1[:], t1[:], t2[:], ALU.add, [B1], [B1])
        k.tt("dve", fre[:], t1[:], den[:], ALU.mult, [B1], [B1])
        k.tt("dve", t1[:], A_i, are[:], ALU.mult, [B1], [B1])
        k.tt("dve", t2[:], nr[:], aim[:], ALU.mult, [B1], [B1])
        k.tt("dve", t1[:], t1[:], t2[:], ALU.subtract, [B1], [B1])
        k.tt("dve", fim[:], t1[:], den[:], ALU.mult, [B1], [B1])
        Bbr, Bbi = T("Bbr", [64, NG, CH]), T("Bbi", [64, NG, CH])
        u1, u2 = T("u1", [64, NG, CH]), T("u2", [64, NG, CH])
        bc = lambda ap: ap.unsqueeze(2).to_broadcast([64, NG, CH])
        cmul(k, "dve", Bbr[:], Bbi[:], bc(fre[:]), bc(fim[:]), Br[:], Bi[:], u1[:], u2[:], [B1], [B1], B1)
        Anr, Ani = T("Anr", [64, NG, LC]), T("Ani", [64, NG, LC])
        m2 = T("m2", [64, NG, LC])
        k.tt("dve", m2[:], APW[:, 0, :, 0:LC], APW[:, 0, :, 0:LC], ALU.mult, [B1], [B1])
        k.tt("dve", u1[:], APW[:, 1, :, 0:LC], APW[:, 1, :, 0:LC], ALU.mult, [B1], [B1])
        k.tt("dve", m2[:], m2[:], u1[:], ALU.add, [B1], [B1])
        k.recip(m2[:], m2[:], [B1], [B1])
        k.tt("dve", Anr[:], APW[:, 0, :, 0:LC], m2[:], ALU.mult, [B1], [B1])
        k.stt(Ani[:], APW[:, 1, :, 0:LC], -1.0, m2[:], ALU.mult, ALU.mult, [B1], [B1])
        rrho = T("rrho", [64, NG])
        k.recip(rrho[:], RHO[:], [B1], [B1])
        k.tt("dve", RP[:, 0, 0, :], APW[:, 0, :, 16], rrho[:], ALU.mult, [B1], [B1])
        k.tt("dve", RP[:, 0, 1, :], APW[:, 1, :, 16], rrho[:], ALU.mult, [B1], [B1])
        for l in range(1, 7):
            cmul(k, "dve", RP[:, l, 0, :], RP[:, l, 1, :], RP[:, l - 1, 0, :], RP[:, l - 1, 1, :],
                 RP[:, l - 1, 0, :], RP[:, l - 1, 1, :], t1[:], t2[:], [B1], [B1], B1)
        sqa = T("sqa", [64, 2, 2, NG])
        k.cp("dve", sqa[:, 0, 0, :], APW[:, 0, :, 16], [B1], [B1])
        k.cp("dve", sqa[:, 0, 1, :], APW[:, 1, :, 16], [B1], [B1])
        cur = 0
        for l in range(7):
            cmul(k, "dve", sqa[:, 1 - cur, 0, :], sqa[:, 1 - cur, 1, :], sqa[:, cur, 0, :], sqa[:, cur, 1, :],
                 sqa[:, cur, 0, :], sqa[:, cur, 1, :], t1[:], t2[:], [B1], [B1], B1)
            cur = 1 - cur
        k.cp("dve", k.APX[:, 0], sqa[:, cur], [B1], [B1])
        cmul(k, "dve", k.APX[:, 1, 0, :], k.APX[:, 1, 1, :], sqa[:, cur, 0, :], sqa[:, cur, 1, :],
             sqa[:, cur, 0, :], sqa[:, cur, 1, :], t1[:], t2[:], [B1], [B1], B1)
        with ExitStack() as se:
            GE = 8
            Eall = k.sb("su_Eall", [64, 2, GE, NJ], F32, se)
            ew1, ew2 = k.sb("su_ew1", [64, GE, NJ // 2], F32, se), k.sb("su_ew2", [64, GE, NJ // 2], F32, se)
            BEa, Bew = Buf(), Buf()
            for ge in range(NG // GE):
                es_ = slice(ge * GE, (ge + 1) * GE)
                k.memset("pool", Eall[:, 0, :, 0:1], 1.0, [BEa])
                k.memset("pool", Eall[:, 1, :, 0:1], 0.0, [BEa])
                k.cp("pool", Eall[:, 0, :, 1:2], RP[:, 0, 0, es_].unsqueeze(2), [B1, BEa], [BEa])
                k.cp("pool", Eall[:, 1, :, 1:2], RP[:, 0, 1, es_].unsqueeze(2), [B1, BEa], [BEa])
                for l in range(1, 7):
                    m = 1 << l
                    bR = lambda c_: RP[:, l, c_, es_].unsqueeze(2).to_broadcast([64, GE, m])
                    cmul(k, "pool" if l < 5 else "dve", Eall[:, 0, :, m:2 * m], Eall[:, 1, :, m:2 * m],
                         Eall[:, 0, :, 0:m], Eall[:, 1, :, 0:m], bR(0), bR(1), ew1[:, :, 0:m], ew2[:, :, 0:m],
                         [B1, BEa], [BEa], Bew)
                k.dma("sp", k.E_d[:, :, es_, :], Eall[:], reads=[BEa], writes=[k.Btab])
                k.cp("dve", k.E127[:, :, es_], Eall[:, :, :, NJ - 1], [BEa, B1], [B1])
            P.barrier()
        GB = 2
        NSET = 3
        w1_, w2_ = T("w1", [64, GB, LC, CH]), T("w2", [64, GB, LC, CH])
        w3_, w4_ = T("w3", [64, GB, LC, CH]), T("w4", [64, GB, LC, CH])
        Bw, Bw2 = Buf(), Buf()
        Pre = [T(f"Pre{i}", [64, GB, 256], BF16) for i in range(NSET)]
        Pim = [T(f"Pim{i}", [64, GB, 256], BF16) for i in range(NSET)]
        nPim = [T(f"nPim{i}", [64, GB, 256], BF16) for i in range(NSET)]
        Qri = [T(f"Qri{i}", [64, 2, GB, 256], BF16) for i in range(NSET)]
        BP = [Buf() for _ in range(NSET)]
        BQ = [Buf() for _ in range(NSET)]
        Tb = T("Tb", [128, GB, 2, 256], BF16)
        WSb = T("WSb", [128, GB, 2, 128], BF16)
        tmpT = [T(f"tmpT{i}", [128, 256]) for i in range(2)]
        BtmpT = [Buf(), Buf()]
        BTb, BWSb = Buf(), Buf()
        pT = k.ps("su_pT", [128, 512], F32, sc)
        BpT = Buf()
        pW = k.ps("su_pW", [128, 2 * GB, 128], BF16, sc)
        BpW = Buf()
        v4 = lambda ap: ap.rearrange("p g (s c) -> p g s c", c=CH)
        NB_ = NG // GB

        def emit_pq(gb):
            i = gb % NSET
            gs_ = slice(gb * GB, (gb + 1) * GB)
            bB = lambda ap: ap[:, gs_, :].unsqueeze(2).to_broadcast([64, GB, LC, CH])
            bA = lambda ap: ap.unsqueeze(3).to_broadcast([64, GB, LC, CH])
            cmul(k, "pool", v4(Pre[i][:]), v4(Pim[i][:]), bB(Bbr), bB(Bbi), bA(Anr[:, gs_, :]), bA(Ani[:, gs_, :]),
                 w1_[:], w2_[:], [B1], [BP[i]], Bw)
            k.ts("pool", nPim[i][:], Pim[i][:], -1.0, None, ALU.mult, None, [BP[i]], [BP[i]])
            cmul(k, "dve", v4(Qri[i][:, 0]), v4(Qri[i][:, 1]), bA(APW[:, 0, gs_, 0:LC]), bA(APW[:, 1, gs_, 0:LC]),
                 bB(Cr), bB(Ci), w3_[:], w4_[:], [B1], [BQ[i]], Bw2)

        def emit_tw(gb):
            i = gb % NSET
            gs_ = slice(gb * GB, (gb + 1) * GB)
            n = 0
            for gl in range(GB):
                g = gb * GB + gl
                for sh in range(2):
                    j = n % 2
                    n += 1
                    hs_ = slice(sh * 128, (sh + 1) * 128)
                    k.mm(pT[:, 0:256], Pre[i][:, gl, hs_], Qri[i][:, 0, gl, :], True, False, [BP[i], BQ[i]], [BpT])
                    k.mm(pT[:, 0:256], nPim[i][:, gl, hs_], Qri[i][:, 1, gl, :], False, True, [BP[i], BQ[i]], [BpT])
                    k.tt("dve", tmpT[j][:], pT[:, 0:256], tmask[:, sh, :], ALU.mult, [BpT, B1], [BtmpT[j]])
                    k.stt(Tb[:, gl, sh, :], dmask[:, sh, :], dcol[:, g:g + 1], tmpT[j][:], ALU.mult, ALU.add,
                          [BtmpT[j], B1], [BTb])
            for sh in range(2):
                hs_ = slice(sh * 128, (sh + 1) * 128)
                for gl in range(GB):
                    k.tr(pW[0:128, sh * GB + gl, 0:64], Pre[i][:, gl, hs_], k.identb[0:64, 0:64], [BP[i], k.Bident], [BpW])
                    k.tr(pW[0:128, sh * GB + gl, 64:128], Pim[i][:, gl, hs_], k.identb[0:64, 0:64], [BP[i], k.Bident], [BpW])
            k.cp("act", WSb[:].rearrange("p g s n -> p s g n"), pW[:].rearrange("p (s g) n -> p s g n", s=2), [BpW], [BWSb])
            k.dma("sp", k.T_d[:, gs_, :, :], Tb[:], reads=[BTb], writes=[k.Btab])
            k.dma("sp", k.WS_d[:, gs_, :, :], WSb[:], reads=[BWSb], writes=[k.Btab])
            k.dma("sp", k.Q_d[:, :, gs_, :], Qri[i][:], reads=[BQ[i]], writes=[k.Btab])

        for gb in range(NB_ + 2):
            if gb < NB_:
                emit_pq(gb)
            if gb >= 2:
                emit_tw(gb - 2)
        P.barrier()


def ssm_project(k, hnT, BhnT, W, BW, sc, own):
    P = k.P
    UT, BUT = k.UT, k.BUT
    uJ = k.sb("uJ", [128, NG, 8, CH], BF16, sc)
    BuJ = Buf()
    pu = [k.ps(f"pu{i}", [128, 512], F32, sc) for i in range(2)]
    Bpu = [Buf(), Buf()]
    ptr = [k.ps(f"ptu{i}", [128, 8, 128], BF16, sc) for i in range(2)]
    Bptr = [Buf(), Buf()]
    rd = [BhnT[t] for t in range(1, 17)] + [BW]
    n = 0
    for sh in range(2):
        for rl in range(8):
            r = sh * 8 + rl
            i = n % 2
            n += 1
            for kk in range(8):
                k.mm(pu[i][:], hnT[:, kk, 128 + r:128 + SEG:LC], W[:, kk, 128:640], kk == 0, kk == 7, rd, [Bpu[i]])
            k.cp("act", uJ[:, :, rl, :], pu[i][:].rearrange("p (g c) -> p g c", c=CH), [Bpu[i]], [BuJ])
        for gq in range(4):
            i = gq % 2
            for gl in range(8):
                g = gq * 8 + gl
                k.tr(ptr[i][:, gl, :], uJ[:, g].rearrange("p r c -> p (r c)"), k.identb[:], [BuJ, k.Bident], [Bptr[i]])
            k.cp("dve", UT[:, gq * 8:(gq + 1) * 8, sh, :], ptr[i][:], [Bptr[i]], [BUT])
    if own:
        uJs = k.sb("uJs", [16, NG, 4, CH], BF16, sc)
        BuJs = Buf()
        st0 = 17 * 128
        for r in range(4):
            i = n % 2
            n += 1
            for kk in range(8):
                k.mm(pu[i][0:16, :], hnT[:, kk, st0 + r:st0 + 64:4], W[:, kk, 128:640], kk == 0, kk == 7,
                     [BhnT[17], BW], [Bpu[i]])
            k.cp("act", uJs[0:16, :, r, :], pu[i][0:16, :].rearrange("p (g c) -> p g c", c=CH), [Bpu[i]], [BuJs])
        pf = ptr[0][:].rearrange("p a b -> p (a b)")
        for g in range(NG):
            k.tr(pf[0:64, g * 16:(g + 1) * 16], uJs[0:16, g].rearrange("p r c -> p (r c)"), k.identb[0:16, 0:16],
                 [BuJs, k.Bident], [Bptr[0]])
        k.cp("dve", k.UTs[0:64, :, :], pf[0:64, 0:512].rearrange("p (g q) -> p g q", q=16), [Bptr[0]], [k.BUTs])


def ssm_scan(k, own):
    nc, P, din = k.nc, k.P, k.din
    UT, BUT = k.UT, k.BUT
    APW, RHO, RP, B1 = k.APW, k.RHO, k.RP, k.Bsm
    YC, BYC = k.YC, k.BYC
    GB = 4
    with ExitStack() as sc:
        def T(name, shape, dt=F32):
            return k.sb("sc_" + name, shape, dt, sc)
        WSb = T("WSb", [128, GB, 2, 128], BF16)
        BWSb = Buf()
        S = T("S", [64, 2, GB, NJ])
        E = T("E", [64, 2, GB, NJ])
        SR = T("SR", [64, 2, GB, NJ])
        V = T("V", [64, 2, GB, NJ])
        dec = T("dec", [64, GB, NJ])
        w1_, w2_ = T("w1", [64, GB, NJ]), T("w2", [64, GB, NJ])
        vin = T("vin", [64, 2, NG])
        Vl = T("Vl", [64, 2, NG])
        s3_, s4_ = T("s3", [64, NG]), T("s4", [64, NG])
        BVl, BEl = Buf(), Buf()
        s1_, s2_ = T("s1", [64, GB]), T("s2", [64, GB])
        BS, BE, BSR, BV, Bdec, Bw, Bvin, Bs = (Buf() for _ in range(8))
        pS = [k.ps(f"sc_pS{i}", [128, 4, NJ], F32, sc) for i in range(1)]
        BpS = [Buf()]
        if own:
            Tb = T("Tb", [128, GB, 2, 256], BF16)
            Qb = T("Qb", [64, 2, GB, 256], BF16)
            BTb, BQb = Buf(), Buf()
            Zr, nZi = T("Zr", [64, GB, NJ], BF16), T("nZi", [64, GB, NJ], BF16)
            BZ = Buf()
            zJ = T("zJ", [128, LC, GB, CH], BF16)
            BzJ = Buf()
            pY = [k.ps(f"sc_pY{i}", [128, 2, 256], F32, sc) for i in range(2)]
            BpY = [Buf(), Buf()]
            pz = [k.ps(f"sc_pz{i}", [128, 8, 128], BF16, sc) for i in range(2)]
            Bpz = [Buf(), Buf()]
            x0 = T("x0", [64, 2, NG, NSEQ])
            Bx0 = Buf()
            k.dma("sp", x0[:], din["x0"], writes=[Bx0])
            Ss = T("Ss", [64, 2, NG, NSEQ])
            Ss2 = x0
            if USE_CC:
                gxb = k.Gx[:].rearrange("p a b -> p (a b)").bitcast(BF16)
                Zsr = gxb[:, 0:512].rearrange("p (g q) -> p g q", q=NSEQ)
                nZsi = gxb[:, 512:1024].rearrange("p (g q) -> p g q", q=NSEQ)
            else:
                a16 = k.AR[0:64, 8 * TOK + 2 * 3584:8 * TOK + 2 * 4608]
                Zsr = a16[:, 0:512].rearrange("p (g q) -> p g q", q=NSEQ)
                nZsi = a16[:, 512:1024].rearrange("p (g q) -> p g q", q=NSEQ)
            _wv = lambda t: t[:].rearrange("p a b -> p (a b)").rearrange("p (g q) -> p g q", q=NSEQ)
            ws1v, ws2v = _wv(w1_), _wv(w2_)
            if USE_CC:
                zJs = k.gsel[:].rearrange("p a b -> p (a b)").bitcast(BF16)[0:16, :].rearrange(
                    "p (r g c) -> p r g c", r=LC, g=GB)
            else:
                zJs = k.AR[0:16, 8 * TOK + 2 * 3584 + 1024:8 * TOK + 2 * 3584 + 2048].rearrange(
                    "p (r g c) -> p r g c", r=LC, g=GB)
            BSs, BZs, Bws, BzJs = Buf(), Buf(), Buf(), Buf()
        yn = 0
        if own:
            bAa = lambda c_, p: APW[:, c_, :, p].unsqueeze(2).to_broadcast([64, NG, NSEQ])
            cmul(k, "dve", Zsr, nZsi, bAa(0, 1), bAa(1, 1), x0[:, 0], x0[:, 1],
                 ws1v, ws2v, [B1, Bx0, BZs], [BZs], Bw, neg_i=True)
            cmul(k, "dve", k.XS[:, 0], k.XS[:, 1], bAa(0, 4), bAa(1, 4), x0[:, 0], x0[:, 1],
                 ws1v, ws2v, [B1, Bx0], [k.BXS], Bw)
        cmul(k, "dve", vin[:, 0], vin[:, 1], RP[:, 0, 0, :], RP[:, 0, 1, :], YC[:, 0], YC[:, 1],
             s3_[:], s4_[:], [B1, BYC, Bvin], [Bvin], Bs)
        if getattr(k, "AR", None) is not None:
            ar32 = k.AR[0:64, 8 * TOK:12 * TOK].bitcast(F32)
        else:
            ar32_t = T("ar32", [64, 4608])
            ar32 = ar32_t[:]
        v4g = lambda ap: ap.rearrange("p (c g j) -> p c g j", c=2, g=GB)
        v3g = lambda ap: ap.rearrange("p (g j) -> p g j", g=GB)
        Es = [E[:], v4g(ar32[:, 0:1024])]
        SRs = [SR[:], v4g(ar32[:, 1024:2048])]
        decs = [dec[:], v3g(ar32[:, 2048:2560])]
        r3, r4 = v3g(ar32[:, 2560:3072]), v3g(ar32[:, 3072:3584])
        c1, c2 = w1_[:], w2_[:]
        BEs, BSRs, Bdecs = [Buf(), Buf()], [Buf(), Buf()], [Buf(), Buf()]
        Br34, Bc12 = Buf(), Buf()
        NBt = NG // GB

        def phase1(gb):
            pp = gb % 2
            E_, SR_, dec_ = Es[pp], SRs[pp], decs[pp]
            gs_ = slice(gb * GB, (gb + 1) * GB)
            k.dma("sp", WSb[:], k.WS_d[:, gs_, :, :], reads=[k.Btab], writes=[BWSb])
            k.dma("act", E_, k.E_d[:, :, gs_, :], reads=[k.Btab], writes=[BEs[pp]])
            for gl in range(GB):
                g = gb * GB + gl
                for sh in range(2):
                    k.mm(pS[0][:, gl, :], WSb[:, gl, sh, :], UT[:, g, sh, :], sh == 0, sh == 1, [BWSb, BUT], [BpS[0]])
            k.cp("act", S[:, 0], pS[0][0:64], [BpS[0]], [BS])
            k.cp("dve", S[:, 1], pS[0][64:128], [BpS[0], BS], [BS])
            k.tt("pool", w1_[:], E_[:, 0], S[:, 0], ALU.mult, [BEs[pp], BS, Bw], [Bw])
            k.tt("pool", w2_[:], E_[:, 1], S[:, 1], ALU.mult, [BEs[pp], BS, Bw], [Bw])
            k.tt("pool", SR_[:, 0], w1_[:], w2_[:], ALU.add, [Bw, BSRs[pp]], [BSRs[pp]])
            k.tt("dve", r3, E_[:, 0], S[:, 1], ALU.mult, [BEs[pp], BS, Br34], [Br34])
            k.tt("dve", r4, E_[:, 1], S[:, 0], ALU.mult, [BEs[pp], BS, Br34], [Br34])
            k.tt("dve", SR_[:, 1], r3, r4, ALU.subtract, [Br34, BSRs[pp]], [BSRs[pp]])
            k.cp("pool", dec_, RHO[:, gs_].unsqueeze(2).to_broadcast([64, GB, NJ]), [B1, Bdecs[pp]], [Bdecs[pp]])
            if own:
                pSs = pS[0][:].rearrange("p a b -> p (a b)")[:, 0:GB * NSEQ].rearrange("p (g q) -> p g q", q=NSEQ)
                for gl in range(GB):
                    g = gb * GB + gl
                    k.mm(pSs[:, gl, :], WSb[0:64, gl, 0, :], k.UTs[0:64, g, :], True, True, [BWSb, k.BUTs, BS], [BpS[0]])
                k.cp("act", Ss[:, 0, gs_, :], pSs[0:64], [BpS[0]], [BSs])
                k.cp("dve", Ss[:, 1, gs_, :], pSs[64:128], [BpS[0], BSs], [BSs])

        def phase2(gb):
            nonlocal yn
            pp = gb % 2
            E_, SR_, dec_ = Es[pp], SRs[pp], decs[pp]
            gs_ = slice(gb * GB, (gb + 1) * GB)
            if own:
                k.dma("act", Tb[:], k.T_d[:, gs_, :, :], reads=[k.Btab], writes=[BTb])
                k.dma("act", Qb[:], k.Q_d[:, :, gs_, :], reads=[k.Btab], writes=[BQb])
            for gl in range(GB):
                for c_ in range(2):
                    P.op("dve", lambda e, o=V[:, c_, gl, :], d0=dec_[:, gl, :], d1=SR_[:, c_, gl, :],
                         ini=vin[:, c_, gb * GB + gl:gb * GB + gl + 1]: e.tensor_tensor_scan(
                             out=o, data0=d0, data1=d1, initial=ini, op0=ALU.mult, op1=ALU.add),
                         reads=[Bdecs[pp], BSRs[pp], Bvin, BV], writes=[BV])
            k.cp("dve", Vl[:, :, gs_], V[:, :, :, NJ - 1], [BV, BVl], [BVl])
            if not own:
                return
            k.tt("pool", SR_, V[:], SR_, ALU.subtract, [BV, BSRs[pp]], [BSRs[pp]])
            cmul(k, "pool", Zr[:], nZi[:], E_[:, 0], E_[:, 1], SR_[:, 0], SR_[:, 1], c1, c2,
                 [BEs[pp], BSRs[pp], BZ], [BZ], Bw, neg_i=True)
            for gl in range(GB):
                g = gb * GB + gl
                i = (yn // 2) % 2
                po = pY[i][:, gl % 2, :]
                k.mm(po, UT[:, g, 0, :], Tb[:, gl, 0, :], True, False, [BUT, BTb], [BpY[i]])
                k.mm(po, UT[:, g, 1, :], Tb[:, gl, 1, :], False, False, [BUT, BTb], [BpY[i]])
                k.mm(po, Zr[:, gl, :], Qb[:, 0, gl, :], False, False, [BZ, BQb], [BpY[i]])
                k.mm(po, nZi[:, gl, :], Qb[:, 1, gl, :], False, True, [BZ, BQb], [BpY[i]])
                yn += 1
                if gl % 2 == 1:
                    k.act(zJ[:, :, gl - 1:gl + 1, :].rearrange("p r g c -> p g r c"),
                          pY[i][:].rearrange("p g (r c) -> p g r c", c=CH), AF.Gelu, [BpY[i]], [BzJ])
            hp = (gb % 2) * 64
            for rh in range(2):
                for rl in range(8):
                    r = rh * 8 + rl
                    k.tr(pz[rh][0:64, rl, :], zJ[:, r].rearrange("p g c -> p (g c)"), k.identb[:], [BzJ, k.Bident], [Bpz[rh]])
                dst = k.zT[hp:hp + 64, gb // 2, 128:128 + SEG].rearrange("p (j r) -> p r j", r=LC)[:, rh * 8:(rh + 1) * 8, :]
                k.cp("act" if rh == 0 else "dve", dst, pz[rh][0:64], [Bpz[rh]], [k.BzT])
            for gp in range(GB // 2):
                i = gp % 2
                for q_ in range(2):
                    gl = gp * 2 + q_
                    g = gb * GB + gl
                    po = pY[i][0:16, q_, :]
                    k.mm(po, k.UTs[0:64, g, :], Tb[0:64, gl, 0, :], True, False, [k.BUTs, BTb, BzJ], [BpY[i]])
                    k.mm(po, Zsr[:, g, :], Qb[:, 0, gl, :], False, False, [BZs, BQb], [BpY[i]])
                    k.mm(po, nZsi[:, g, :], Qb[:, 1, gl, :], False, True, [BZs, BQb], [BpY[i]])
                k.act(zJs[0:16, :, gp * 2:gp * 2 + 2, :].rearrange("p r g c -> p g r c"),
                      pY[i][0:16].rearrange("p g (r c) -> p g r c", c=CH), AF.Gelu, [BpY[i]], [BzJs])
            pzs = pz[0][:].rearrange("p a b -> p (a b)")
            for r in range(4):
                k.tr(pzs[0:64, r * NSEQ:(r + 1) * NSEQ], zJs[0:16, r].rearrange("p g c -> p (g c)"), k.identb[0:16, 0:16],
                     [BzJs, k.Bident], [Bpz[0]])
            dst = k.zT[hp:hp + 64, gb // 2, 17 * 128:17 * 128 + 64].rearrange("p (q r) -> p r q", r=4)
            k.cp("act", dst, pzs[0:64, 0:4 * NSEQ].rearrange("p (r q) -> p r q", q=NSEQ), [Bpz[0]], [k.BzT])

        phase1(0)
        for gb in range(NBt):
            if gb + 1 < NBt:
                phase1(gb + 1)
            phase2(gb)
        if own:
            bAa = lambda c_, p: APW[:, c_, :, p].unsqueeze(2).to_broadcast([64, NG, NSEQ])
            cmul(k, "dve", Ss2[:, 0], Ss2[:, 1], bAa(0, 3), bAa(1, 3), Ss[:, 0], Ss[:, 1],
                 ws1v, ws2v, [B1, BSs, Bx0], [Bx0], Bw)
            k.tt("dve", k.XS[:], k.XS[:], Ss2[:], ALU.add, [Bx0, k.BXS], [k.BXS])
        cmul(k, "dve", YC[:, 0], YC[:, 1], k.E127[:, 0], k.E127[:, 1], Vl[:, 0], Vl[:, 1], s3_[:], s4_[:],
             [B1, BVl, Bvin, BYC], [BYC], Bs)
        P.barrier()


def outputs_kv(k, KTf, BKTf, Vf, BVf):
    nc, P, din, dout = k.nc, k.P, k.din, k.dout
    with ExitStack() as sc:
        pk = k.ps("okv_pk", [128, 256], F32, sc)
        Bpk = Buf()
        ko = k.sb("okv_ko", [128, 256], F32, sc)
        Bko = Buf()
        k.tr(pk[:, 0:128], KTf[:, 0:128], k.identf[:], [BKTf, k.Bidentf], [Bpk])
        k.tr(pk[0:64, 128:256], KTf[:, 128:192], k.identf[:], [BKTf, k.Bidentf], [Bpk])
        k.cp("act", ko[:], pk[:], [Bpk], [Bko])
        k.dma("sp", dout["kp"], ko[:, 0:128], reads=[Bko])
        k.dma("sp", dout["vp"], Vf[:, 0, :], reads=[BVf])
        scr_k = nc.dram_tensor("scr_k", [64, 128], F32).ap()
        scr_v = nc.dram_tensor("scr_v", [64, 128], F32).ap()
        Bsk, Bsv = Buf(), Buf()
        k.dma("sp", scr_k, ko[0:64, 128:256], reads=[Bko], writes=[Bsk])
        k.dma("sp", scr_v, Vf[0:64, 1, :], reads=[BVf], writes=[Bsv])
        k.dma("sp", dout["ks"][:, 124:128, :], scr_k.rearrange("(q t) f -> q t f", t=4), reads=[Bsk])
        k.dma("sp", dout["vs"][:, 124:128, :], scr_v.rearrange("(q t) f -> q t f", t=4), reads=[Bsv])
        k.dma("act", dout["ks"][:, 0:124, :], din["cache_k"][:, 4:128, :])
        k.dma("act", dout["vs"][:, 0:124, :], din["cache_v"][:, 4:128, :])
        P.barrier()


def outputs_state(k):
    nc, P, dout = k.nc, k.P, k.dout
    with ExitStack() as sc:
        xe = k.sb("os_xe", [64, 2, NG], F32, sc)
        t1, t2 = k.sb("os_t1", [64, NG], F32, sc), k.sb("os_t2", [64, NG], F32, sc)
        Bxe, Bt = Buf(), Buf()
        cmul(k, "dve", xe[:, 0], xe[:, 1], k.APW[:, 0, :, 15], k.APW[:, 1, :, 15], k.YC[:, 0], k.YC[:, 1],
             t1[:], t2[:], [k.Bsm, k.BYC], [Bxe], Bt)
        pp = k.ps("os_pp", [32, 2, 64], F32, sc)
        Bpp = Buf()
        for c_ in range(2):
            k.tr(pp[:, c_, :], xe[:, c_, :], k.identf[0:64, 0:64], [Bxe, k.Bidentf], [Bpp])
        so = k.sb("os_so", [32, 2, 64], F32, sc)
        Bso = Buf()
        k.cp("act", so[:], pp[:], [Bpp], [Bso])
        k.dma("sp", dout["rp"], so[:, 0, :], reads=[Bso])
        k.dma("sp", dout["ip"], so[:, 1, :], reads=[Bso])
        ps_ = [k.ps(f"os_ps{i}", [16, 8, 64], F32, sc) for i in range(2)]
        Bps = [Buf(), Buf()]
        xs_o = k.sb("os_xs", [16, 2, NG, 64], F32, sc)
        Bxo = Buf()
        n = 0
        for c_ in range(2):
            for gq in range(4):
                i = n % 2
                n += 1
                for gl in range(8):
                    g = gq * 8 + gl
                    k.tr(ps_[i][:, gl, :], k.XS[:, c_, g, :], k.identf[0:64, 0:64], [k.BXS, k.Bidentf], [Bps[i]])
                k.cp("act", xs_o[:, c_, gq * 8:(gq + 1) * 8, :], ps_[i][:], [Bps[i]], [Bxo])
        k.dma("sp", dout["rs"], xs_o[:, 0], reads=[Bxo])
        k.dma("sp", dout["is"], xs_o[:, 1], reads=[Bxo])
        P.barrier()


def mixer_tail(k):
    nc, P, din = k.nc, k.P, k.din
    QA, BQA, zT, BzT = k.QA, k.BQA, k.zT, k.BzT
    blocks = [(c0, min(512, TOK - c0)) for c0 in range(128, TOK, 512)]
    P.stage = "glu"
    with ExitStack() as sc:
        ssmT = k.ssmT
        BssmT = Buf()
        with ExitStack() as s1:
            Wg = k.sb("Wglu", [128, 4, 512], BF16, s1)
            bg = k.sb("bglu", [128, 4], F32, s1)
            BWg = Buf()
            k.dma("pool", Wg[:], din["w_glu"].rearrange("(kk p) c -> p kk c", p=128), writes=[BWg])
            k.dma("sp", bg[:], din["b_glu"], writes=[BWg])
            pg = [k.ps(f"mt_pg{i}", [128, 512], F32, s1) for i in range(2)]
            Bpg = [Buf(), Buf()]
            sg = [k.sb(f"mt_sg{i}", [128, 512], BF16, s1) for i in range(2)]
            Bsg = [Buf(), Buf()]
            n = 0
            for ct in range(4):
                for (c0, nb) in blocks:
                    i = n % 2
                    n += 1
                    for kt in range(4):
                        k.mm(pg[i][:, 0:nb], Wg[:, kt, ct * 128:(ct + 1) * 128], zT[:, kt, c0:c0 + nb], kt == 0, kt == 3,
                             [BWg, BzT], [Bpg[i]])
                    k.act(sg[i][:, 0:nb], pg[i][:, 0:nb], AF.Sigmoid, [Bpg[i], BWg], [Bsg[i]], bias=bg[:, ct:ct + 1])
                    k.tt("pool", ssmT[:, ct, c0:c0 + nb], zT[:, ct, c0:c0 + nb], sg[i][:, 0:nb], ALU.mult,
                         [BzT, Bsg[i]], [BssmT])
            P.barrier()
        win = din["w_in"].rearrange("(kk p) c -> p kk c", p=128)
        wao = din["w_ao"].rearrange("(kk p) c -> p kk c", p=128)
        wso = din["w_ssm_out"].rearrange("(kk p) c -> p kk c", p=128)
        for (ta, tb) in ((1, 10), (10, NT)):
            tl_h = list(range(ta, tb))
            nh = len(tl_h) * 128
            cbase = ta * 128
            with ExitStack() as sh_:
                hnT = k.sb("hnT2", [128, 8, 1152], BF16, sh_)
                BhnT = {t: Buf() for t in tl_h}
                mT = k.mTh
                BmT = {t: Buf() for t in tl_h}
                P.stage = "m5norm"
                with ExitStack() as s2:
                    tp = k.ps("mt_tp", [128, 8, 128], BF16, s2)
                    norm_transpose(k, "mix_g", tl_h, hnT, BhnT, s2, tp, Buf(), t0=ta)
                    P.barrier()
                P.stage = "m5"
                with ExitStack() as s3:
                    NBm = 256
                    Wm = [k.sb(f"Wm{i}", [128, 3072], BF16, s3) for i in range(2)]
                    Wga = [Wm[i][:, 0:1024].rearrange("p (a c) -> p a c", a=8) for i in range(2)]
                    Wgs = [Wm[i][:, 1024:2048].rearrange("p (a c) -> p a c", a=8) for i in range(2)]
                    Wao = [Wm[i][:, 2048:2560].rearrange("p (a c) -> p a c", a=4) for i in range(2)]
                    Wso = [Wm[i][:, 2560:3072].rearrange("p (a c) -> p a c", a=4) for i in range(2)]
                    BWm = [Buf(), Buf()]
                    pp = [[k.ps(f"mt_p{j}{i}", [128, 512], F32, s3) for i in range(2)] for j in range(4)]
                    Bpp = [[Buf(), Buf()] for _ in range(4)]
                    sga = [k.sb(f"mt_sga{i}", [128, NBm], BF16, s3) for i in range(2)]
                    sgs = [k.sb(f"mt_sgs{i}", [128, NBm], BF16, s3) for i in range(2)]
                    m1 = [k.sb(f"mt_m1{i}", [128, NBm], F32, s3) for i in range(2)]
                    m2 = [k.sb(f"mt_m2{i}", [128, NBm], F32, s3) for i in range(2)]
                    Bsga, Bsgs, Bm1, Bm2 = ([Buf(), Buf()] for _ in range(4))
                    n = 0
                    for dt_ in range(8):
                        w = dt_ % 2
                        cs_ = slice(dt_ * 128, (dt_ + 1) * 128)
                        k.dma("pool", Wm[w][:], din["wm5"][dt_], writes=[BWm[w]])
                        for l0 in range(0, nh, NBm):
                            nb = min(NBm, nh - l0)
                            c0 = cbase + l0
                            i = n % 2
                            n += 1
                            tl = list(range(c0 // 128, (c0 + nb) // 128))
                            rdh = [BhnT[t] for t in tl] + [BWm[w]]
                            for kk in range(8):
                                k.mm(pp[0][i][:, 0:nb], Wga[w][:, kk, :], hnT[:, kk, l0:l0 + nb], kk == 0, kk == 7, rdh, [Bpp[0][i]])
                            for kk in range(8):
                                k.mm(pp[1][i][:, 0:nb], Wgs[w][:, kk, :], hnT[:, kk, l0:l0 + nb], kk == 0, kk == 7, rdh, [Bpp[1][i]])
                            rq = [BQA[t][h] for t in tl for h in range(2)] + [BWm[w]]
                            for kk in range(4):
                                k.mm(pp[2][i][:, 0:nb], Wao[w][:, kk, :], QA[:, kk, c0:c0 + nb], kk == 0, kk == 3, rq, [Bpp[2][i]])
                            for kk in range(4):
                                k.mm(pp[3][i][:, 0:nb], Wso[w][:, kk, :], ssmT[:, kk, c0:c0 + nb], kk == 0, kk == 3,
                                     [BssmT, BWm[w]], [Bpp[3][i]])
                            k.act(sga[i][:, 0:nb], pp[0][i][:, 0:nb], AF.Sigmoid, [Bpp[0][i]], [Bsga[i]])
                            k.act(sgs[i][:, 0:nb], pp[1][i][:, 0:nb], AF.Sigmoid, [Bpp[1][i]], [Bsgs[i]])
                            k.tt("dve", m1[i][:, 0:nb], pp[2][i][:, 0:nb], sga[i][:, 0:nb], ALU.mult, [Bpp[2][i], Bsga[i]], [Bm1[i]])
                            k.tt("dve", m2[i][:, 0:nb], pp[3][i][:, 0:nb], sgs[i][:, 0:nb], ALU.mult, [Bpp[3][i], Bsgs[i]], [Bm2[i]])
                            k.tt("pool", mT[:, dt_, l0:l0 + nb], m1[i][:, 0:nb], m2[i][:, 0:nb], ALU.add, [Bm1[i], Bm2[i]],
                                 [BmT[t] for t in tl])
                    P.barrier()
                P.stage = "m6"
                with ExitStack() as s4:
                    Wo = k.sb("Wout", [128, 8, D], BF16, s4)
                    BWo = Buf()
                    k.dma("pool", Wo[:], din["w_out"].rearrange("(kk p) c -> p kk c", p=128), writes=[BWo])
                    po = [k.ps(f"mt_po{i}", [128, 512], F32, s4) for i in range(4)]
                    Bpo = [Buf() for _ in range(4)]
                    n = 0
                    for t in tl_h:
                        lt = (t - ta) * 128
                        for hh in range(2):
                            i = n % 4
                            n += 1
                            for kk in range(8):
                                k.mm(po[i][:], mT[:, kk, lt:lt + 128], Wo[:, kk, hh * 512:(hh + 1) * 512], kk == 0, kk == 7,
                                     [BmT[t], BWo], [Bpo[i]])
                            xs_ = k.X[:, t, hh * 512:(hh + 1) * 512]
                            k.tt("dve", xs_, po[i][:], xs_, ALU.add, [Bpo[i], k.BX[t][hh]], [k.BX[t][hh]])
                    P.barrier()


def attention(k, KT, BKT, V, BV, consts, smask, Bsm, sinkb, Bsink, ones, Bones):
    P, din = k.P, k.din
    QA, BQA = k.QA, k.BQA
    mdiag, Bmd = consts["mdiag"]
    mprev, Bmp = consts["mprev"]
    mprev1, Bmp1 = consts["mprev1"]
    with ExitStack() as s0:
        Kc = k.sb("Kc", [128, NSEQ, 128], BF16, s0)
        KcT = k.sb("KcT", [128, NSEQ, 128], BF16, s0)
        Vc = k.sb("Vc", [128, NSEQ, 128], BF16, s0)
        BKc, BKcT, BVc = Buf(), Buf(), Buf()
        k.dma("pool", Kc[:], din["cache_k"].rearrange("q s f -> s q f"), writes=[BKc])
        k.dma("pool", Vc[:], din["cache_v"].rearrange("q s f -> s q f"), writes=[BVc])
        with ExitStack() as s1:
            ptr = [k.ps(f"ptrc{i}", [128, 8, 128], BF16, s1) for i in range(2)]
            Bptr = [Buf(), Buf()]
            for hh in range(2):
                for j in range(8):
                    k.tr(ptr[hh][:, j, :], Kc[:, hh * 8 + j, :], k.identb[:], [BKc, k.Bident], [Bptr[hh]])
                k.cp("act", KcT[:, hh * 8:(hh + 1) * 8, :], ptr[hh][:], [Bptr[hh]], [BKcT])
            P.barrier()
        attention_main(k, KT, BKT, V, BV, consts, smask, Bsm, sinkb, Bsink, ones, Bones, KcT, BKcT, Vc, BVc)


def attention_main(k, KT, BKT, V, BV, consts, smask, Bsm, sinkb, Bsink, ones, Bones, KcT, BKcT, Vc, BVc):
    P, din = k.P, k.din
    QA, BQA = k.QA, k.BQA
    mdiag, Bmd = consts["mdiag"]
    mprev, Bmp = consts["mprev"]
    mprev1, Bmp1 = consts["mprev1"]
    with ExitStack() as sc:
        psc = [[k.ps(f"psc{i}{j}", [128, 512], F32, sc) for j in range(2)] for i in range(2)]
        Bpsc = [[Buf(), Buf()] for _ in range(2)]
        pnum = [k.ps(f"pnum{i}", [128, 512], F32, sc) for i in range(2)]
        pden = [k.ps(f"pden{i}", [128, 512], F32, sc) for i in range(2)]
        Bpnum, Bpden = [Buf(), Buf()], [Buf(), Buf()]
        Pb = [[k.sb(f"Pb{i}{j}", [128, 512], BF16, sc) for j in range(2)] for i in range(2)]
        Pm = [[k.sb(f"Pm{i}{j}", [128, 512], BF16, sc) for j in range(2)] for i in range(2)]
        BPb = [[Buf(), Buf()] for _ in range(2)]
        BPm = [[Buf(), Buf()] for _ in range(2)]
        rec = [k.sb(f"rec{i}", [128, 512], F32, sc) for i in range(2)]
        Brec = [Buf(), Buf()]
        it = 0
        for qt in range(1, 17):
            cq = qt * 128
            for kh in range(2):
                i = it % 2
                it += 1
                r0, r1 = kh * 64, (kh + 1) * 64
                qv = QA[r0:r1, :, cq:cq + 128]
                for kb, kt_ in enumerate((qt - 1, qt)):
                    k.mm(psc[i][kb][:].rearrange("p (a t) -> p a t", a=4), KT[r0:r1, kt_ * 128:(kt_ + 1) * 128], qv,
                         True, True, [BKT[kt_], BQA[qt][kh]], [Bpsc[i][kb]])
                    k.act(Pb[i][kb][:], psc[i][kb][:], AF.Exp, [Bpsc[i][kb]], [BPb[i][kb]])
                    if kb == 1:
                        m, Bm = mdiag, Bmd
                    elif qt == 1:
                        m, Bm = mprev1, Bmp1
                    else:
                        m, Bm = mprev, Bmp
                    k.tt("pool", Pm[i][kb][:].rearrange("p (a t) -> p a t", a=4),
                         Pb[i][kb][:].rearrange("p (a t) -> p a t", a=4),
                         m[:, :].unsqueeze(1).to_broadcast([128, 4, 128]), ALU.mult,
                         [BPb[i][kb], Bm], [BPm[i][kb]])
                for kb, kt_ in enumerate((qt - 1, qt)):
                    k.mm(pnum[i][:], V[:, kt_, :], Pm[i][kb][:], kb == 0, kb == 1, [BV[kt_], BPm[i][kb]], [Bpnum[i]])
                for kb in range(2):
                    k.mm(pden[i][:], ones[:], Pm[i][kb][:], kb == 0, kb == 1, [Bones, BPm[i][kb]], [Bpden[i]])
                rv = rec[i][r0:r1, :].rearrange("p (a t) -> p a t", a=4)
                k.tt("dve", rv, pden[i][r0:r1, :].rearrange("p (a t) -> p a t", a=4),
                     sinkb[r0:r1, kh * 4:(kh + 1) * 4].unsqueeze(2).to_broadcast([64, 4, 128]), ALU.add,
                     [Bpden[i], Bsink], [Brec[i]])
                k.recip(rec[i][r0:r1, :], rec[i][r0:r1, :], [Brec[i]], [Brec[i]])
                k.tt("dve", qv, pnum[i][r0:r1, :].rearrange("p (a t) -> p a t", a=4), rv, ALU.mult,
                     [Bpnum[i], Brec[i]], [BQA[qt][kh]])
        st0 = 17 * 128
        for kh in range(2):
            i = it % 2
            it += 1
            r0, r1 = kh * 64, (kh + 1) * 64
            qv = QA[r0:r1, :, st0:st0 + 64]
            v4 = lambda ap: ap.rearrange("p (a t) -> p a t", a=4)
            k.mm(v4(psc[i][1][0:64, 0:256]), KT[r0:r1, st0:st0 + 64], qv, True, True,
                 [BKT[17], BQA[17][kh]], [Bpsc[i][1]])
            for sq_ in range(NSEQ):
                k.mm(v4(psc[i][0][:, 0:256])[:, :, sq_ * 4:(sq_ + 1) * 4], KcT[r0:r1, sq_, :],
                     QA[r0:r1, :, st0 + sq_ * 4:st0 + sq_ * 4 + 4], True, True,
                     [BKcT, BQA[17][kh]], [Bpsc[i][0]])
            k.act(Pb[i][0][:, 0:256], psc[i][0][:, 0:256], AF.Exp, [Bpsc[i][0]], [BPb[i][0]])
            k.act(Pb[i][1][0:64, 0:256], psc[i][1][0:64, 0:256], AF.Exp, [Bpsc[i][1]], [BPb[i][1]])
            k.tt("pool", Pm[i][0][:, 0:256].rearrange("p (a q t) -> p a q t", a=4, t=4),
                 Pb[i][0][:, 0:256].rearrange("p (a q t) -> p a q t", a=4, t=4),
                 mprev[:, 0:4].unsqueeze(1).unsqueeze(1).to_broadcast([128, 4, NSEQ, 4]), ALU.mult,
                 [BPb[i][0], Bmp], [BPm[i][0]])
            k.tt("pool", v4(Pm[i][1][0:64, 0:256]), v4(Pb[i][1][0:64, 0:256]),
                 smask[:, :].unsqueeze(1).to_broadcast([64, 4, 64]), ALU.mult, [BPb[i][1], Bsm], [BPm[i][1]])
            for (pt, Bpt, lcur, lprev) in ((pnum[i], Bpnum[i], V[0:64, 17, :], None), (pden[i], Bpden[i], ones[0:64, :], ones)):
                k.mm(pt[:, 0:256], lcur, Pm[i][1][0:64, 0:256], True, False,
                     [BV[17], Bones, BPm[i][1]], [Bpt])
                for sq_ in range(NSEQ):
                    lp = Vc[:, sq_, :] if lprev is None else ones[:]
                    k.mm(v4(pt[:, 0:256])[:, :, sq_ * 4:(sq_ + 1) * 4], lp,
                         v4(Pm[i][0][:, 0:256])[:, :, sq_ * 4:(sq_ + 1) * 4], False, sq_ == NSEQ - 1,
                         [BVc, Bones, BPm[i][0]], [Bpt])
            rv = v4(rec[i][r0:r1, 0:256])
            k.tt("dve", rv, v4(pden[i][r0:r1, 0:256]),
                 sinkb[r0:r1, kh * 4:(kh + 1) * 4].unsqueeze(2).to_broadcast([64, 4, 64]), ALU.add,
                 [Bpden[i], Bsink], [Brec[i]])
            k.recip(rec[i][r0:r1, 0:256], rec[i][r0:r1, 0:256], [Brec[i]], [Brec[i]])
            k.tt("dve", qv, v4(pnum[i][r0:r1, 0:256]), rv, ALU.mult, [Bpnum[i], Brec[i]], [BQA[17][kh]])
        P.barrier()


_CACHE = {}


def _gvec(g):
    return np.ascontiguousarray(np.asarray(g, np.float32).reshape(8, 128).T)


def make_in_maps(inputs):
    f32 = np.float32
    xp = np.asarray(inputs["x_prompt"], f32)
    xs = np.asarray(inputs["x_sample"], f32)
    w_in = np.ascontiguousarray(np.asarray(inputs["w_in"][0], f32))
    qcols = []
    for j in range(4):
        qcols += list(range(j * 64, (j + 1) * 64)) + list(range((j + 4) * 64, (j + 5) * 64))
    qcols += list(range(512, 640))
    rotm = np.zeros((128, 128), f32)
    for hb in range(2):
        for d in range(32):
            rotm[hb * 64 + d + 32, hb * 64 + d] = -1.0
            rotm[hb * 64 + d, hb * 64 + 32 + d] = 1.0
    onesbd = np.zeros((128, 128), f32)
    onesbd[0:64, 0:64] = 1.0
    onesbd[64:128, 64:128] = 1.0
    inv = np.power(f32(10000.0), (-2.0 * np.arange(32, dtype=f32) / f32(64.0)).astype(f32)).astype(f32)
    invf = np.concatenate([inv, inv, inv, inv]).reshape(128, 1).astype(f32)
    si, qi = np.meshgrid(np.arange(128), np.arange(128), indexing="ij")
    mdiag = (si <= qi).astype(f32)
    mprev = (si > qi).astype(f32)
    a64 = np.arange(64)
    smask = ((a64[:, None] // 4 == a64[None, :] // 4) & (a64[:, None] % 4 <= a64[None, :] % 4)).astype(f32)
    sl = np.arange(128) // 16
    cp_ = np.arange(128) % 16
    rr_ = np.arange(256) // 16
    cc_ = np.arange(256) % 16
    tmask = np.zeros((128, 2, 256), f32)
    dmask = np.zeros((128, 2, 256), f32)
    for sh in range(2):
        sg_ = sl + 8 * sh
        tmask[:, sh, :] = (rr_[None, :] >= sg_[:, None]).astype(f32)
        dmask[:, sh, :] = ((rr_[None, :] == sg_[:, None]) & (cc_[None, :] == cp_[:, None])).astype(f32)
    ao_rows = []
    for j in range(4):
        for kh in range(2):
            ao_rows += list(range((kh * 4 + j) * 64, (kh * 4 + j + 1) * 64))
    sre = np.asarray(inputs["state_ssm_re"], f32)[0]
    sim = np.asarray(inputs["state_ssm_im"], f32)[0]
    w_ao_p = np.asarray(inputs["w_attn_out"][0], f32)[ao_rows]
    w_so = np.asarray(inputs["w_ssm_out"][0], f32)
    wm5 = np.zeros((8, 128, 3072), f32)
    for dt_ in range(8):
        cs = slice(dt_ * 128, (dt_ + 1) * 128)
        wm5[dt_, :, 0:1024] = w_in[:, 1280 + dt_ * 128:1280 + (dt_ + 1) * 128].reshape(8, 128, 128).transpose(1, 0, 2).reshape(128, 1024)
        wm5[dt_, :, 1024:2048] = w_in[:, 2304 + dt_ * 128:2304 + (dt_ + 1) * 128].reshape(8, 128, 128).transpose(1, 0, 2).reshape(128, 1024)
        wm5[dt_, :, 2048:2560] = w_ao_p[:, cs].reshape(4, 128, 128).transpose(1, 0, 2).reshape(128, 512)
        wm5[dt_, :, 2560:3072] = w_so[:, cs].reshape(4, 128, 128).transpose(1, 0, 2).reshape(128, 512)
    common = {
        "wm5": wm5,
        "ffn1_g": _gvec(inputs["ffn1_norm"][0]), "mix_g": _gvec(inputs["mix_norm"][0]),
        "ffn2_g": _gvec(inputs["ffn2_norm"][0]),
        "identb": np.eye(128, dtype=f32),
        "w_qk": np.ascontiguousarray(w_in[:, qcols]), "w_in": w_in,
        "gq2": np.tile(np.asarray(inputs["q_norm"][0], f32), 2).reshape(128, 1),
        "gk2": np.tile(np.asarray(inputs["k_norm"][0], f32), 2).reshape(128, 1),
        "invf": invf, "rotm": rotm, "onesbd": onesbd, "mdiag": mdiag, "mprev": mprev, "smask": smask,
        "sinks": np.asarray(inputs["attn_sinks"], f32).reshape(1, 8),
        "identf": np.eye(128, dtype=f32),
        "a_reT": np.ascontiguousarray(np.asarray(inputs["ssm_a_re"][0], f32).T),
        "a_imT": np.ascontiguousarray(np.asarray(inputs["ssm_a_im"][0], f32).T),
        "logdt_b": np.ascontiguousarray(np.broadcast_to(np.asarray(inputs["ssm_log_dt"][0], f32)[None, :], (64, NG))),
        "bT_re": np.ascontiguousarray(np.asarray(inputs["ssm_b_re"][0], f32).transpose(1, 0, 2)),
        "bT_im": np.ascontiguousarray(np.asarray(inputs["ssm_b_im"][0], f32).transpose(1, 0, 2)),
        "cT_re": np.ascontiguousarray(np.asarray(inputs["ssm_c_re"][0], f32).transpose(2, 0, 1)),
        "cT_im": np.ascontiguousarray(np.asarray(inputs["ssm_c_im"][0], f32).transpose(2, 0, 1)),
        "dcol": np.ascontiguousarray(np.tile(np.asarray(inputs["ssm_d"][0], f32).T, (8, 1))),
        "tmask": tmask, "dmask": dmask,
        "w_glu": np.ascontiguousarray(np.asarray(inputs["w_glu"][0], f32)),
        "b_glu": np.ascontiguousarray(np.asarray(inputs["b_glu"][0], f32).reshape(4, 128).T),
        "w_ao": np.ascontiguousarray(np.asarray(inputs["w_attn_out"][0], f32)[ao_rows]),
        "w_ssm_out": np.ascontiguousarray(np.asarray(inputs["w_ssm_out"][0], f32)),
        "w_out": np.ascontiguousarray(np.asarray(inputs["w_out"][0], f32)),
    }
    for nm in ("ffn1_w1", "ffn1_w3", "ffn1_w2", "ffn2_w1", "ffn2_w3", "ffn2_w2"):
        common[nm] = np.ascontiguousarray(np.asarray(inputs[nm][0], f32))
    ck = np.asarray(inputs["cache_k"], f32)[0].reshape(128, 128, 128)
    cv = np.asarray(inputs["cache_v"], f32)[0].reshape(128, 128, 128)
    maps = []
    for c in range(8):
        b, seg = c // 4, c % 4
        x = np.zeros((TOK, D), f32)
        if seg > 0:
            x[0:128] = xp[b, seg * SEG - 128: seg * SEG]
        x[128:128 + SEG] = xp[b, seg * SEG:(seg + 1) * SEG]
        x[17 * 128:17 * 128 + 64] = xs[16 * c:16 * c + 16].reshape(64, D)
        pos = np.zeros((1, TOK), f32)
        pos[0, 0:128 + SEG] = np.arange(seg * SEG - 128, (seg + 1) * SEG, dtype=f32)
        pos[0, 17 * 128:17 * 128 + 64] = PAST_LEN + (np.arange(64) % 4).astype(f32)
        m = dict(common)
        m["x"] = x
        m["pos"] = pos
        m["mprev1"] = mprev if seg > 0 else np.zeros_like(mprev)
        selb = np.zeros((64, 8), f32)
        selb[:, c] = 1.0
        wsel = np.zeros((64, 3, 8), f32)
        for d in range(3):
            if seg - 1 - d >= 0:
                wsel[:, d, b * 4 + seg - 1 - d] = 1.0
        if USE_CC:
            m["selb"] = selb
            m["wsel"] = wsel.reshape(64, 24)
        else:
            xpre = np.zeros((3, SEG, D), f32)
            for i in range(3):
                sg_i = seg - 3 + i
                if sg_i >= 0:
                    xpre[i] = xp[b, sg_i * SEG:(sg_i + 1) * SEG]
            m["xpre"] = xpre
        m["x0"] = np.ascontiguousarray(np.stack([sre[16 * c:16 * c + 16], sim[16 * c:16 * c + 16]], 0)
                                       .transpose(3, 0, 2, 1))
        m["cache_k"] = np.ascontiguousarray(ck[16 * c:16 * c + 16])
        m["cache_v"] = np.ascontiguousarray(cv[16 * c:16 * c + 16])
        maps.append(m)
    return maps


def kernel(**inputs):
    if "nc" not in _CACHE:
        _CACHE["nc"] = build_program()
    nc = _CACHE["nc"]
    maps = make_in_maps(inputs)
    res = run_bass_kernel_spmd(nc, maps, core_ids=list(range(8)))
    r = res.results
    f32 = np.float32
    yp = np.zeros((2, 8192, D), f32)
    ys = np.zeros((128, 4, D), f32)
    kp = np.zeros((1, 2, 128, 2, 64), f32)
    vp = np.zeros((1, 2, 128, 2, 64), f32)
    rp = np.zeros((1, 2, NG, NS), f32)
    ip = np.zeros((1, 2, NG, NS), f32)
    ks = np.zeros((1, 128, 128, 2, 64), f32)
    vs = np.zeros((1, 128, 128, 2, 64), f32)
    rs = np.zeros((1, 128, NG, NS), f32)
    is_ = np.zeros((1, 128, NG, NS), f32)
    for c in range(8):
        b, seg = c // 4, c % 4
        y = r[c]["y"]
        yp[b, seg * SEG:(seg + 1) * SEG] = y[0:SEG]
        ys[16 * c:16 * c + 16] = y[SEG:SEG + 64].reshape(16, 4, D)
        ks[0, 16 * c:16 * c + 16] = r[c]["ks"].reshape(16, 128, 2, 64)
        vs[0, 16 * c:16 * c + 16] = r[c]["vs"].reshape(16, 128, 2, 64)
        rs[0, 16 * c:16 * c + 16] = r[c]["rs"]
        is_[0, 16 * c:16 * c + 16] = r[c]["is"]
        if seg == 3:
            kp[0, b] = r[c]["kp"].reshape(128, 2, 64)
            vp[0, b] = r[c]["vp"].reshape(128, 2, 64)
            rp[0, b] = r[c]["rp"]
            ip[0, b] = r[c]["ip"]
    return (yp, ys, kp, vp, rp, ip, ks, vs, rs, is_)
```
